# Optimizing a Trainium2 kernel written in Bass

```python
import math
import jax, jax.numpy as jnp
from jax import lax
import numpy as np

D_MODEL = 1024
BATCH = 2
SEQ = 8192
DEPTH = 2

GRID_W = 64
CTX_LEN = 256
HEAD_DIM = 64
NA_HEADS = D_MODEL // 2 // HEAD_DIM
NA_WIN_ROWS = 8
NA_WIN_COLS = 16
DIFF_HEAD_DIM = 64
DIFF_V_DIM = 2 * DIFF_HEAD_DIM
DIFF_HEADS = D_MODEL // 2 // DIFF_V_DIM
NA_WIDTH = NA_HEADS * HEAD_DIM
DIFF_QK_WIDTH = DIFF_HEADS * 2 * DIFF_HEAD_DIM
DIFF_V_WIDTH = DIFF_HEADS * DIFF_V_DIM
ATTN_IN_WIDTH = 3 * NA_WIDTH + 2 * DIFF_QK_WIDTH + DIFF_V_WIDTH
ATTN_OUT_WIDTH = NA_WIDTH + DIFF_V_WIDTH
ATTN_SPLITS = [NA_WIDTH, 2 * NA_WIDTH, 3 * NA_WIDTH,
               3 * NA_WIDTH + DIFF_QK_WIDTH, 3 * NA_WIDTH + 2 * DIFF_QK_WIDTH]
CONV_WIDTH = 3
D_FF = 2816
ROPE_THETA = 10000.0
ATTN_BLOCK = 128
N_MOD = 9
N_ATTN_LAYERS = (DEPTH + 1) // 2
N_CONV_LAYERS = DEPTH // 2
EPS = 1e-6
NEG_INF = -1e30

kernel_name = "hybrid_natten_diffattn_shortconv_macaron_dit"


def rmsnorm(x, g):
    x32 = x.astype(jnp.float32)
    y = x32 * lax.rsqrt(jnp.mean(x32 * x32, axis=-1, keepdims=True) + EPS)
    return (y * g.astype(jnp.float32)).astype(x.dtype)


def modulate(x, g, shift, scale):
    return rmsnorm(x, g) * (1 + scale) + shift


def swiglu(x, w_gate, w_up, w_down):
    return (jax.nn.silu(x @ w_gate) * (x @ w_up)) @ w_down


def macaron_ffn(xs, g, shift, scale, gate, w_gate, w_up, w_down):
    return xs + 0.5 * gate * swiglu(modulate(xs, g, shift, scale), w_gate, w_up, w_down)


def axial_rope(n):
    t = jnp.arange(n)
    row = (t // GRID_W).astype(jnp.float32)
    col = (t % GRID_W).astype(jnp.float32)
    n_freq = DIFF_HEAD_DIM // 4
    inv = ROPE_THETA ** (-jnp.arange(n_freq, dtype=jnp.float32) / n_freq)
    ang = jnp.concatenate([row[:, None] * inv, col[:, None] * inv], axis=-1)
    return jnp.cos(ang)[:, None, None, :], jnp.sin(ang)[:, None, None, :]


def apply_rope(x, cos, sin):
    x32 = x.astype(jnp.float32)
    x1, x2 = x32[..., 0::2], x32[..., 1::2]
    out = jnp.stack([x1 * cos - x2 * sin, x1 * sin + x2 * cos], axis=-1)
    return out.reshape(x.shape).astype(x.dtype)


def split_attn_proj(p):
    b, t, _ = p.shape
    qa, ka, va, qb, kb, vb = jnp.split(p, ATTN_SPLITS, axis=-1)
    na = lambda a: a.reshape(b, t, NA_HEADS, HEAD_DIM)
    dqk = lambda a: a.reshape(b, t, DIFF_HEADS, 2, DIFF_HEAD_DIM)
    return (na(qa), na(ka), na(va), dqk(qb), dqk(kb), vb.reshape(b, t, DIFF_HEADS, DIFF_V_DIM))


def dense_attend(q, k, v):
    s = jnp.einsum('bqhd,bkhd->bhqk', q * q.shape[-1] ** -0.5, k)
    p = jax.nn.softmax(s.astype(jnp.float32), axis=-1).astype(v.dtype)
    return jnp.einsum('bhqk,bkhd->bqhd', p, v)


def diff_attend(q, k, v, lam):
    s = jnp.einsum('bqhmd,bkhmd->bhmqk', q * q.shape[-1] ** -0.5, k)
    p = jax.nn.softmax(s.astype(jnp.float32), axis=-1)
    pd = (p[:, :, 0] - lam * p[:, :, 1]).astype(v.dtype)
    return jnp.einsum('bhqk,bkhe->bqhe', pd, v)


def neighbourhood_attention(q, k, v, k_ctx, v_ctx, rpb):
    b, n, h, d = q.shape
    rows = n // GRID_W
    kr = min(NA_WIN_ROWS, rows)
    qg = (q * d ** -0.5).reshape(b, rows, GRID_W, h, d)
    kg = k.reshape(b, rows, GRID_W, h, d)
    vg = v.reshape(b, rows, GRID_W, h, d)
    qc = jnp.arange(GRID_W)
    col_start = jnp.clip(qc - NA_WIN_COLS // 2, 0, GRID_W - NA_WIN_COLS)
    col_mask = (qc[None, :] >= col_start[:, None]) & (qc[None, :] < col_start[:, None] + NA_WIN_COLS)
    dc_idx = jnp.clip(qc[None, :] - qc[:, None] + NA_WIN_COLS - 1, 0, 2 * NA_WIN_COLS - 2)

    def row_block(r):
        start = jnp.clip(r - kr // 2, 0, rows - kr)
        q_r = lax.dynamic_index_in_dim(qg, r, axis=1, keepdims=False)
        k_band = lax.dynamic_slice_in_dim(kg, start, kr, axis=1)
        v_band = lax.dynamic_slice_in_dim(vg, start, kr, axis=1)
        s_band = jnp.einsum('bqhd,brkhd->bhqrk', q_r, k_band).astype(jnp.float32)
        dr_idx = start + jnp.arange(kr) - r + NA_WIN_ROWS - 1
        bias = rpb[:, dr_idx[None, :, None], dc_idx[:, None, :]].astype(jnp.float32)
        s_band = jnp.where(col_mask[:, None, :], s_band + bias, NEG_INF)
        s_ctx = jnp.einsum('bqhd,blhd->bhql', q_r, k_ctx).astype(jnp.float32)
        s = jnp.concatenate([s_band.reshape(b, h, GRID_W, kr * GRID_W), s_ctx], axis=-1)
        p = jax.nn.softmax(s, axis=-1).astype(v.dtype)
        p_band = p[..., :kr * GRID_W].reshape(b, h, GRID_W, kr, GRID_W)
        p_ctx = p[..., kr * GRID_W:]
        return (jnp.einsum('bhqrk,brkhd->bqhd', p_band, v_band)
                + jnp.einsum('bhql,blhd->bqhd', p_ctx, v_ctx))

    out = lax.map(row_block, jnp.arange(rows))
    return jnp.moveaxis(out, 0, 1).reshape(b, n, h, d)


def attn_mixer(xn, xn_ctx, w_in, w_out, rpb, lam_vec, subln_g, lam_init, cos, sin, ctx_out):
    b, n, _ = xn.shape
    qa, ka, va, qb, kb, vb = split_attn_proj(xn @ w_in)
    qa_c, ka_c, va_c, qb_c, kb_c, vb_c = split_attn_proj(xn_ctx @ w_in)
    qb, kb = apply_rope(qb, cos, sin), apply_rope(kb, cos, sin)
    l32 = lam_vec.astype(jnp.float32)
    lam = jnp.exp(jnp.sum(l32[0] * l32[1])) - jnp.exp(jnp.sum(l32[2] * l32[3])) + lam_init

    a_lat = neighbourhood_attention(qa, ka, va, ka_c, va_c, rpb)
    k_all = jnp.concatenate([kb, kb_c], axis=1)
    v_all = jnp.concatenate([vb, vb_c], axis=1)
    nb = n // ATTN_BLOCK
    q_blocks = jnp.moveaxis(qb.reshape(b, nb, ATTN_BLOCK, DIFF_HEADS, 2, DIFF_HEAD_DIM), 1, 0)
    d_lat = lax.map(lambda q_blk: diff_attend(q_blk, k_all, v_all, lam), q_blocks)
    d_lat = jnp.moveaxis(d_lat, 0, 1).reshape(b, n, DIFF_HEADS, DIFF_V_DIM)

    def merge(a, dd):
        t = a.shape[1]
        dd = rmsnorm(dd, subln_g) * (1.0 - lam_init)
        return jnp.concatenate([a.reshape(b, t, NA_WIDTH), dd.reshape(b, t, DIFF_V_WIDTH)], axis=-1) @ w_out

    y_lat = merge(a_lat, d_lat)
    if ctx_out:
        y_ctx = merge(dense_attend(qa_c, ka_c, va_c), diff_attend(qb_c, kb_c, vb_c, lam))
    else:
        y_ctx = None
    return y_lat, y_ctx


def conv_mixer(xn, w_in, w_out, conv_w):
    bg, cg, h = jnp.split(xn @ w_in, 3, axis=-1)
    u = cg * h
    v = lax.conv_general_dilated(u, conv_w[:, None, :].astype(u.dtype), window_strides=(1,),
                                 padding=((CONV_WIDTH // 2, CONV_WIDTH // 2),),
                                 dimension_numbers=('NWC', 'WIO', 'NWC'),
                                 feature_group_count=u.shape[-1])
    return (bg * v) @ w_out


def setup_inputs(seed: int = 0) -> dict:
    key = jax.random.key(seed)
    ks = jax.random.split(key, 20)
    nrm = lambda k, shape, s: jax.random.normal(k, shape, jnp.float32) * s
    D = D_MODEL
    return {
        "x": nrm(ks[0], (BATCH, SEQ, D), 1.0),
        "c": nrm(ks[1], (BATCH, D), 1.0),
        "ctx": nrm(ks[2], (BATCH, CTX_LEN, D), 1.0),
        "c_ctx": nrm(ks[3], (D,), 1.0),
        "mod_w": nrm(ks[4], (DEPTH, D, N_MOD * D), D ** -0.5),
        "mod_b": nrm(ks[5], (DEPTH, N_MOD * D), 0.01),
        "norm_g": 1.0 + nrm(ks[6], (DEPTH, 3, D), 0.1),
        "ffn_w_gate": nrm(ks[7], (DEPTH, 2, D, D_FF), D ** -0.5),
        "ffn_w_up": nrm(ks[8], (DEPTH, 2, D, D_FF), D ** -0.5),
        "ffn_w_down": nrm(ks[9], (DEPTH, 2, D_FF, D), D_FF ** -0.5),
        "attn_w_in": nrm(ks[10], (N_ATTN_LAYERS, D, ATTN_IN_WIDTH), D ** -0.5),
        "attn_w_out": nrm(ks[11], (N_ATTN_LAYERS, ATTN_OUT_WIDTH, D), ATTN_OUT_WIDTH ** -0.5),
        "na_rpb": nrm(ks[12], (N_ATTN_LAYERS, NA_HEADS, 2 * NA_WIN_ROWS - 1, 2 * NA_WIN_COLS - 1), 0.2),
        "diff_lambda": nrm(ks[13], (N_ATTN_LAYERS, 4, DIFF_HEAD_DIM), 0.1),
        "diff_subln_g": 1.0 + nrm(ks[14], (N_ATTN_LAYERS, DIFF_V_DIM), 0.1),
        "conv_w_in": nrm(ks[15], (N_CONV_LAYERS, D, 3 * D), D ** -0.5),
        "conv_w_out": nrm(ks[16], (N_CONV_LAYERS, D, D), D ** -0.5),
        "conv_w": nrm(ks[17], (N_CONV_LAYERS, CONV_WIDTH, D), CONV_WIDTH ** -0.5),
        "final_g": 1.0 + nrm(ks[18], (D,), 0.1),
    }


def reference(x, c, ctx, c_ctx, mod_w, mod_b, norm_g, ffn_w_gate, ffn_w_up, ffn_w_down,
              attn_w_in, attn_w_out, na_rpb, diff_lambda, diff_subln_g,
              conv_w_in, conv_w_out, conv_w, final_g):
    n = x.shape[1]
    cos, sin = axial_rope(n)
    x_lat, x_ctx = x, ctx
    for i in range(DEPTH):
        last = i == DEPTH - 1
        is_attn = i % 2 == 0
        ctx_live = is_attn or not last
        j = i // 2
        m = jnp.split((jax.nn.silu(c) @ mod_w[i] + mod_b[i])[:, None, :], N_MOD, axis=-1)
        if ctx_live:
            mc = jnp.split(jax.nn.silu(c_ctx) @ mod_w[i] + mod_b[i], N_MOD, axis=-1)

        x_lat = macaron_ffn(x_lat, norm_g[i, 0], m[0], m[1], m[2],
                            ffn_w_gate[i, 0], ffn_w_up[i, 0], ffn_w_down[i, 0])
        if ctx_live:
            x_ctx = macaron_ffn(x_ctx, norm_g[i, 0], mc[0], mc[1], mc[2],
                                ffn_w_gate[i, 0], ffn_w_up[i, 0], ffn_w_down[i, 0])

        xn = modulate(x_lat, norm_g[i, 1], m[3], m[4])
        if is_attn:
            xn_c = modulate(x_ctx, norm_g[i, 1], mc[3], mc[4])
            lam_init = 0.8 - 0.6 * math.exp(-0.3 * i)
            y, y_c = attn_mixer(xn, xn_c, attn_w_in[j], attn_w_out[j], na_rpb[j], diff_lambda[j],
                                diff_subln_g[j], lam_init, cos, sin, ctx_out=not last)
        else:
            y = conv_mixer(xn, conv_w_in[j], conv_w_out[j], conv_w[j])
            if not last:
                y_c = conv_mixer(modulate(x_ctx, norm_g[i, 1], mc[3], mc[4]),
                                 conv_w_in[j], conv_w_out[j], conv_w[j])
        x_lat = x_lat + m[5] * y
        if not last:
            x_ctx = x_ctx + mc[5] * y_c

        x_lat = macaron_ffn(x_lat, norm_g[i, 2], m[6], m[7], m[8],
                            ffn_w_gate[i, 1], ffn_w_up[i, 1], ffn_w_down[i, 1])
        if not last:
            x_ctx = macaron_ffn(x_ctx, norm_g[i, 2], mc[6], mc[7], mc[8],
                                ffn_w_gate[i, 1], ffn_w_up[i, 1], ffn_w_down[i, 1])
    return rmsnorm(x_lat, final_g)
```

```python
import math
from contextlib import ExitStack

import numpy as np
import ml_dtypes
import concourse.bass as bass
import concourse.mybir as mybir
from concourse.bass_utils import run_bass_kernel_spmd

F32 = mybir.dt.float32
BF16 = mybir.dt.bfloat16
ALU = mybir.AluOpType
AF = mybir.ActivationFunctionType
NPBF = ml_dtypes.bfloat16

D = 1024
DFF = 2816
NKC = 8
NFC = 22
TPC = 2048
EPS = 1e-6
NEG = -30000.0
DEBUG_STOP = 0

ENGS = ["pe", "act", "dve", "pool", "sp"]
SEM_CHUNK = 30000


class Res:
    __slots__ = ("name", "w", "r")

    def __init__(self, name):
        self.name = name
        self.w = None
        self.r = []


class Op:
    __slots__ = ("eng", "fn", "deps", "signal", "sig", "dma_key", "dma_val", "dma_inc")

    def __init__(self, eng, fn):
        self.eng = eng
        self.fn = fn
        self.deps = []
        self.signal = False
        self.sig = None
        self.dma_key = None
        self.dma_val = 0
        self.dma_inc = 16


class Prog:
    def __init__(self, nc):
        self.nc = nc
        self.q = {e: [] for e in ENGS}
        self.dma_cnt = {}
        self.nres = 0
        self.pending = {e: [] for e in ENGS}
        self.dmas = []

    def res(self, name=None):
        self.nres += 1
        return Res(name or f"r{self.nres}")

    def _add(self, eng, fn, reads, writes, dma_key=None, inc=16):
        op = Op(eng, fn)
        deps = []
        for r in reads:
            if r.w is not None:
                deps.append(r.w)
        for w in writes:
            if w.w is not None:
                deps.append(w.w)
            deps.extend(w.r)
        if self.pending[eng]:
            deps.extend(self.pending[eng])
            self.pending[eng] = []
        seen = set()
        for d in deps:
            if id(d) in seen:
                continue
            seen.add(id(d))
            if d.dma_key is None:
                if eng == "pe" and d.eng == "pe":
                    continue
                d.signal = True
            op.deps.append(d)
        if dma_key is not None:
            op.dma_key = dma_key
            self.dma_cnt[dma_key] = self.dma_cnt.get(dma_key, 0) + inc
            op.dma_val = self.dma_cnt[dma_key]
            op.dma_inc = inc
            self.dmas.append(op)
        for r in reads:
            r.r.append(op)
        for w in writes:
            w.w = op
            w.r = []
        self.q[eng].append(op)
        return op

    def op(self, eng, fn, reads=(), writes=()):
        return self._add(eng, fn, reads, writes)

    def dma(self, eng, key, out, in_, reads=(), writes=()):
        return self._add(eng, lambda e: e.dma_start(out=out, in_=in_), reads, writes, dma_key=key)

    def barrier(self):
        lasts = []
        for e in ENGS:
            for op in reversed(self.q[e]):
                if op.dma_key is None:
                    lasts.append(op)
                    break
        lasts.extend(self.dmas)
        self.dmas = []
        for e in ENGS:
            self.pending[e] = list(lasts)

    def emit(self):
        nc = self.nc
        sems = {}

        def sem(name):
            if name not in sems:
                sems[name] = nc.alloc_semaphore(name)
            return sems[name]

        for e in ENGS:
            k = 0
            for op in self.q[e]:
                if op.dma_key is None and op.signal:
                    op.sig = (f"s_{e}_{k // SEM_CHUNK}", k % SEM_CHUNK + 1)
                    k += 1
                elif op.dma_key is not None:
                    op.sig = (f"d_{op.dma_key}", op.dma_val)

        def run(e, engine):
            waited = {}
            for op in self.q[e]:
                need = {}
                for d in op.deps:
                    s, v = d.sig
                    if waited.get(s, 0) >= v:
                        continue
                    if need.get(s, 0) < v:
                        need[s] = v
                for s, v in need.items():
                    engine.wait_ge(sem(s), v)
                    waited[s] = v
                ins = op.fn(engine)
                if op.dma_key is not None:
                    ins.then_inc(sem(op.sig[0]), op.dma_inc)
                elif op.signal:
                    ins.then_inc(sem(op.sig[0]), 1)

        with nc.Block() as block:
            @block.tensor
            def _(eng):
                run("pe", eng)

            @block.scalar
            def _(eng):
                run("act", eng)

            @block.vector
            def _(eng):
                run("dve", eng)

            @block.gpsimd
            def _(eng):
                run("pool", eng)

            @block.sync
            def _(eng):
                run("sp", eng)


class KB:
    def __init__(self):
        self.nc = bass.Bass("TRN2", target_bir_lowering=False)
        self.P = Prog(self.nc)
        self.ps = [self.nc.alloc_psum_tensor(f"psb{i}", [128, 512], F32) for i in range(8)]
        self.psr = [self.P.res(f"psb{i}") for i in range(8)]
        self.out_res = []
        self.uid = 0

    def inp(self, name, shape, dt=F32):
        return self.nc.dram_tensor(name, list(shape), dt, kind="ExternalInput").ap()

    def outp(self, name, shape, dt=F32):
        return self.nc.dram_tensor(name, list(shape), dt, kind="ExternalOutput").ap()

    def sb(self, es, name, shape, dt):
        self.uid += 1
        return es.enter_context(self.nc.sbuf_tensor(f"{name}_{self.uid}", list(shape), dt))

    def store(self, dram_ap, sb_ap, reads, key):
        r = self.P.res()
        self.P.dma("sp", "st_" + key, dram_ap, sb_ap, reads=reads, writes=[r])
        self.out_res.append(r)

    def finish(self):
        self.P.op("sp", lambda e: e.nop(), reads=self.out_res)
        self.P.emit()
        return self.nc

    def load_consts(self, es, cst_ap):
        P = self.P
        self.cst = self.sb(es, "cst", [128, 3, 128], BF16)
        self.r_cst = P.res("cst")
        P.dma("pool", "cst", self.cst[:], cst_ap, writes=[self.r_cst])
        self.ones = self.cst[:, 0, :]
        self.ident = self.cst[:, 1, :]
        self.swap = self.cst[:, 2, :]


class Tile:
    def __init__(self, x, xres, n, A=None, B=None, HG=None, modres=None):
        self.x = x
        self.xres = xres
        self.n = n
        self.A = A
        self.B = B
        self.HG = HG
        self.modres = modres
        self.off = 0


def emit_mod(kb, es, modw_ap, nch, cvec_ap, nv, modb_ap, name):
    P = kb.P
    nc = kb.nc
    assert nch % 4 == 0
    cv = kb.sb(es, name + "cv", [128, 8, nv], F32)
    sc = kb.sb(es, name + "sc", [128, 8, nv], BF16)
    mb = kb.sb(es, name + "mb", [128, nch], F32)
    mvec = kb.sb(es, name + "mvec", [128, nch, nv], F32)
    r_cv, r_sc, r_mb, r_mvec = P.res(), P.res(), P.res(), P.res()
    P.dma("sp", name + "cv", cv[:], cvec_ap, writes=[r_cv])
    P.dma("sp", name + "mb", mb[:], modb_ap, writes=[r_mb])
    P.op("act", lambda e: e.activation(sc[:], cv[:], AF.Silu), reads=[r_cv], writes=[r_sc])
    psm = kb.ps[7]
    r_psm = kb.psr[7]
    with ExitStack() as es2:
        mw = [kb.sb(es2, name + f"mw{s}", [128, 8, 512], BF16) for s in range(2)]
        r_mw = [P.res(), P.res()]
        src = modw_ap.rearrange("(kc p) n -> p kc n", p=128)
        for g in range(nch // 4):
            s = g % 2
            P.dma("pool", name + f"mw{s}", mw[s][:], src[:, :, g * 512:(g + 1) * 512], writes=[r_mw[s]])
            for c4 in range(4):
                oc = g * 4 + c4
                for kc in range(8):
                    P.op("pe", lambda e, s=s, c4=c4, oc=oc, kc=kc: e.matmul(
                        psm[:, oc * nv:(oc + 1) * nv], mw[s][:, kc, c4 * 128:(c4 + 1) * 128], sc[:, kc, :],
                        start=(kc == 0), stop=(kc == 7)), reads=[r_mw[s], r_sc], writes=[r_psm])
        for v in range(nv):
            P.op("dve", lambda e, v=v: e.tensor_tensor(
                mvec[:, :, v], psm[:, 0:nch * nv].rearrange("p (c v) -> p c v", v=nv)[:, :, v], mb[:], ALU.add),
                reads=[r_psm, r_mb], writes=[r_mvec])
        P.barrier()
    return mvec, r_mvec


def emit_ab(kb, es, mvec, r_mvec, v, i_shift, i_scale, g_ap, r_g, name, i_gate=None, gate_mul=1.0):
    P = kb.P
    A = kb.sb(es, name + "A", [128, 8], F32)
    r = P.res()
    P.op("dve", lambda e: e.scalar_tensor_tensor(
        A[:], mvec[:, i_scale * 8:(i_scale + 1) * 8, v], 1.0, g_ap, ALU.add, ALU.mult),
        reads=[r_mvec, r_g], writes=[r])
    Bm = mvec[:, i_shift * 8:(i_shift + 1) * 8, v]
    G = None
    if i_gate is not None:
        G = kb.sb(es, name + "G", [128, 8], F32)
        P.op("dve", lambda e: e.tensor_scalar_mul(G[:], mvec[:, i_gate * 8:(i_gate + 1) * 8, v], float(gate_mul)),
             reads=[r_mvec], writes=[r])
    return A, Bm, G, r


def emit_normmod(kb, scr, t, A, Bv, r_ab, out_fn, r_out):
    P = kb.P
    n = t.n
    ps_stat, r_ps = kb.ps[0], kb.psr[0]
    for kc in range(NKC):
        s = kc % 2
        P.op("act", lambda e, s=s, kc=kc: e.activation(scr["sq"][s][:, :n], t.x[kc], AF.Square),
             reads=[t.xres[kc]], writes=[scr["r_sq"][s]])
        P.op("pe", lambda e, s=s, kc=kc: e.matmul(ps_stat[:, :n], kb.ones, scr["sq"][s][:, :n],
                                                  start=(kc == 0), stop=(kc == NKC - 1)),
             reads=[scr["r_sq"][s], kb.r_cst], writes=[r_ps])
    rstd = scr["rstd"]
    P.op("act", lambda e: e.activation(rstd[:, :n], ps_stat[:, :n], AF.Sqrt, bias=scr["eps"][:, 0:1], scale=1.0 / D),
         reads=[r_ps, scr["r_eps"]], writes=[scr["r_rstd"]])
    P.op("dve", lambda e: e.reciprocal(rstd[:, :n], rstd[:, :n]), reads=[scr["r_rstd"]], writes=[scr["r_rstd"]])
    for kc in range(NKC):
        s = kc % 2
        P.op("dve", lambda e, s=s, kc=kc: e.tensor_tensor(scr["tmp"][s][:, :n], t.x[kc], rstd[:, :n], ALU.mult),
             reads=[t.xres[kc], scr["r_rstd"]], writes=[scr["r_tmp"][s]])
        P.op("act", lambda e, s=s, kc=kc: e.activation(out_fn(kc), scr["tmp"][s][:, :n], AF.Identity,
                                                       bias=Bv[:, kc:kc + 1], scale=A[:, kc:kc + 1]),
             reads=[scr["r_tmp"][s], r_ab], writes=[r_out])


def make_scratch(kb, es):
    P = kb.P
    scr = {
        "sq": [kb.sb(es, "sq", [128, 512], BF16) for _ in range(2)],
        "r_sq": [P.res(), P.res()],
        "rstd": kb.sb(es, "rstd", [128, 512], F32),
        "r_rstd": P.res(),
        "tmp": [kb.sb(es, "tmp", [128, 512], F32) for _ in range(2)],
        "r_tmp": [P.res(), P.res()],
        "eps": kb.sb(es, "eps", [128, 1], F32),
        "r_eps": P.res(),
    }
    P.op("dve", lambda e: e.memset(scr["eps"][:], EPS), writes=[scr["r_eps"]])
    return scr


def emit_ffn(kb, groups, wg_ap, wu_ap, wd_ap, xn_ext=None):
    P = kb.P
    with ExitStack() as es:
        maxw = max(sum(t.n for t in g) for g in groups)
        if xn_ext is None:
            xn_t = kb.sb(es, "ffn_xn", [128, NKC, maxw], BF16)
            xn = lambda kc, a, b: xn_t[:, kc, a:b]
        else:
            xn = xn_ext
        H = kb.sb(es, "ffn_H", [128, NFC, maxw], BF16)
        wgb = [kb.sb(es, "wg", [128, NKC, 256], BF16) for _ in range(2)]
        wub = [kb.sb(es, "wu", [128, NKC, 256], BF16) for _ in range(2)]
        wdb = [kb.sb(es, "wd", [128, NFC, 256], BF16) for _ in range(2)]
        sg = [kb.sb(es, "sg", [128, 512], F32) for _ in range(2)]
        r_wg, r_wu, r_wd, r_sg = ([P.res(), P.res()] for _ in range(4))
        scr = make_scratch(kb, es)
        wg_src = wg_ap.rearrange("(kc p) n -> p kc n", p=128)
        wu_src = wu_ap.rearrange("(kc p) n -> p kc n", p=128)
        wd_src = wd_ap.rearrange("(kc p) n -> p kc n", p=128)
        gu_banks = [(1, 2), (3, 4)]
        y_banks = [5, 6]
        cnt = 0
        ycnt = 0
        wcnt = 0
        dcnt = 0
        for g in groups:
            off = 0
            for t in g:
                t.off = off
                off += t.n
            r_xn = {id(t): P.res() for t in g}
            r_H = {(id(t), j): P.res() for t in g for j in range(NFC)}
            for t in g:
                emit_normmod(kb, scr, t, t.A, t.B, t.modres,
                             lambda kc, o=t.off, n=t.n: xn(kc, o, o + n), r_xn[id(t)])
            for j2 in range(NFC // 2):
                s = wcnt % 2
                wcnt += 1
                P.dma("pool", f"wg{s}", wgb[s][:], wg_src[:, :, j2 * 256:(j2 + 1) * 256], writes=[r_wg[s]])
                P.dma("pool", f"wu{s}", wub[s][:], wu_src[:, :, j2 * 256:(j2 + 1) * 256], writes=[r_wu[s]])
                for jj in range(2):
                    j = j2 * 2 + jj
                    for t in g:
                        n = t.n
                        bg_, bu_ = gu_banks[cnt % 2]
                        q = cnt % 2
                        cnt += 1
                        for kc in range(NKC):
                            P.op("pe", lambda e, s=s, jj=jj, kc=kc, o=t.off, n=n, bg_=bg_: e.matmul(
                                kb.ps[bg_][:, :n], wgb[s][:, kc, jj * 128:(jj + 1) * 128], xn(kc, o, o + n),
                                start=(kc == 0), stop=(kc == NKC - 1)),
                                reads=[r_wg[s], r_xn[id(t)]], writes=[kb.psr[bg_]])
                        for kc in range(NKC):
                            P.op("pe", lambda e, s=s, jj=jj, kc=kc, o=t.off, n=n, bu_=bu_: e.matmul(
                                kb.ps[bu_][:, :n], wub[s][:, kc, jj * 128:(jj + 1) * 128], xn(kc, o, o + n),
                                start=(kc == 0), stop=(kc == NKC - 1)),
                                reads=[r_wu[s], r_xn[id(t)]], writes=[kb.psr[bu_]])
                        P.op("act", lambda e, q=q, n=n, bg_=bg_: e.activation(sg[q][:, :n], kb.ps[bg_][:, :n], AF.Silu),
                             reads=[kb.psr[bg_]], writes=[r_sg[q]])
                        P.op("dve", lambda e, q=q, n=n, bu_=bu_, j=j, o=t.off: e.tensor_tensor(
                            H[:, j, o:o + n], sg[q][:, :n], kb.ps[bu_][:, :n], ALU.mult),
                            reads=[r_sg[q], kb.psr[bu_]], writes=[r_H[(id(t), j)]])
            for i4 in range(4):
                s = dcnt % 2
                dcnt += 1
                P.dma("pool", f"wd{s}", wdb[s][:], wd_src[:, :, i4 * 256:(i4 + 1) * 256], writes=[r_wd[s]])
                for ii in range(2):
                    i = i4 * 2 + ii
                    for t in g:
                        n = t.n
                        by = y_banks[ycnt % 2]
                        ycnt += 1
                        for kc in range(NFC):
                            P.op("pe", lambda e, s=s, ii=ii, kc=kc, o=t.off, n=n, by=by: e.matmul(
                                kb.ps[by][:, :n], wdb[s][:, kc, ii * 128:(ii + 1) * 128], H[:, kc, o:o + n],
                                start=(kc == 0), stop=(kc == NFC - 1)),
                                reads=[r_wd[s], r_H[(id(t), kc)]], writes=[kb.psr[by]])
                        P.op("dve", lambda e, i=i, t=t, n=n, by=by, hg=t.HG: e.scalar_tensor_tensor(
                            t.x[i], kb.ps[by][:, :n], hg[:, i:i + 1], t.x[i], ALU.mult, ALU.add),
                            reads=[kb.psr[by], t.xres[i], t.modres], writes=[t.xres[i]])
        P.barrier()


def make_tiles(x_t, P, ntok, width=512):
    tiles = []
    for a in range(0, ntok, width):
        n = min(width, ntok - a)
        tiles.append(Tile([x_t[:, kc, a:a + n] for kc in range(NKC)], [P.res() for _ in range(NKC)], n))
    return tiles


def build_A():
    kb = KB()
    P = kb.P
    xT = kb.inp("xT", [NKC, 128, TPC])
    cxT = kb.inp("cxT", [NKC, 128, 256])
    cvec = kb.inp("cvec", [128, 8, 2])
    modw = kb.inp("modw", [D, 5 * D])
    modb = kb.inp("modb", [128, 40])
    normg = kb.inp("normg", [128, 2, 8])
    wg = kb.inp("wg", [D, DFF])
    wu = kb.inp("wu", [D, DFF])
    wd = kb.inp("wd", [DFF, D])
    w_in = kb.inp("w_in", [D, 3 * D])
    cst = kb.inp("cst", [128, 3, 128])
    rope = kb.inp("rope", [128, 2, TPC])
    x1T = kb.outp("x1T", [NKC, 128, TPC])
    if DEBUG_STOP in (0, 3, 4, 5, 6, 7):
        qkT = kb.outp("qkT", [16, 128, TPC], BF16)
        vtm = kb.outp("vtm", [16, 128, 1024], BF16)
        kcT = kb.outp("kcT", [8, 128, 256], BF16)
        vc = kb.outp("vc", [2, 128, 1024], BF16)

    with ExitStack() as es:
        kb.load_consts(es, cst)
        x_t = kb.sb(es, "x", [128, NKC, TPC], F32)
        xc_t = kb.sb(es, "xc", [128, NKC, 256], F32)
        ng = kb.sb(es, "ng", [128, 2, 8], F32)
        r_ng = P.res()
        P.dma("sp", "ng", ng[:], normg, writes=[r_ng])
        lat = make_tiles(x_t, P, TPC)
        ctx = make_tiles(xc_t, P, 256)
        for kc in range(NKC):
            for t in lat:
                pass
            P.dma("sp", f"xin{kc}", x_t[:, kc, :], xT[kc], writes=[t.xres[kc] for t in lat])
            P.dma("sp", f"xcin{kc}", xc_t[:, kc, :], cxT[kc], writes=[ctx[0].xres[kc]])
        if DEBUG_STOP == -1:
            for kc in range(NKC):
                kb.store(x1T[kc], x_t[:, kc, :], [t.xres[kc] for t in lat], f"x{kc}")
            kb.finish()
            return kb.nc
        mvec, r_mvec = emit_mod(kb, es, modw, 40, cvec, 2, modb, "m0")
        if DEBUG_STOP == -2:
            for kc in range(NKC):
                kb.store(x1T[kc], x_t[:, kc, :], [t.xres[kc] for t in lat], f"x{kc}")
            kb.finish()
            return kb.nc
        ab = {}
        for v in range(2):
            A1, B1, G1, r1 = emit_ab(kb, es, mvec, r_mvec, v, 0, 1, ng[:, 0, :], r_ng, f"f1v{v}", i_gate=2, gate_mul=0.5)
            A2, B2, _, r2 = emit_ab(kb, es, mvec, r_mvec, v, 3, 4, ng[:, 1, :], r_ng, f"qkv{v}")
            ab[v] = (A1, B1, G1, r1, A2, B2, r2)
        for t in lat:
            t.A, t.B, t.HG, t.modres = ab[0][0], ab[0][1], ab[0][2], ab[0][3]
        for t in ctx:
            t.A, t.B, t.HG, t.modres = ab[1][0], ab[1][1], ab[1][2], ab[1][3]
        if DEBUG_STOP != 1:
            emit_ffn(kb, [lat[0:2] + ctx, lat[2:4]], wg, wu, wd)
        for kc in range(NKC):
            kb.store(x1T[kc], x_t[:, kc, :], [t.xres[kc] for t in lat], f"x{kc}")

        if DEBUG_STOP in (1, 2):
            kb.finish()
            return kb.nc
        with ExitStack() as es2:
            wi = kb.sb(es2, "wi", [128, NKC, 3 * D], BF16)
            r_wi = [P.res() for _ in range(6)]
            wi_src = w_in.rearrange("(kc p) n -> p kc n", p=128)
            for g6 in range(6):
                P.dma("pool", f"wi{g6}", wi[:, :, g6 * 512:(g6 + 1) * 512], wi_src[:, :, g6 * 512:(g6 + 1) * 512],
                      writes=[r_wi[g6]])
            rp = kb.sb(es2, "rope", [128, 2, TPC], F32)
            r_rp = P.res()
            P.dma("sp", "rope", rp[:], rope, writes=[r_rp])
            scr = make_scratch(kb, es2)
            xn2 = [kb.sb(es2, "xn2", [128, NKC, 512], BF16) for _ in range(2)]
            r_xn2 = [P.res(), P.res()]
            stg = [kb.sb(es2, "stg", [128, 512], BF16) for _ in range(4)]
            r_stg = [P.res() for _ in range(4)]
            qs = [kb.sb(es2, "qs", [128, 512], BF16) for _ in range(2)]
            r_qs = [P.res(), P.res()]
            t1 = [kb.sb(es2, "t1", [128, 512], F32) for _ in range(2)]
            r_t1 = [P.res(), P.res()]
            t2 = [kb.sb(es2, "t2", [128, 512], F32) for _ in range(2)]
            r_t2 = [P.res(), P.res()]
            banks = [1, 2, 3, 4]
            bc = 0
            sc_ = 0
            rc = 0
            for ti, t in enumerate(lat + ctx):
                is_ctx = ti >= len(lat)
                v = 1 if is_ctx else 0
                n = t.n
                tok0 = 0 if is_ctx else ti * 512
                xb = ti % 2
                emit_normmod(kb, scr, t, ab[v][4], ab[v][5], ab[v][6],
                             lambda kc, xb=xb, n=n: xn2[xb][:, kc, :n], r_xn2[xb])
                fm = [(c, c) for c in range(0, 8)] + [(12 + c, 8 + c) for c in range(8)]
                for (wc, oc) in fm:
                    kind = oc // 4
                    if is_ctx and kind in (0, 2):
                        continue
                    b = banks[bc % 4]
                    bc += 1
                    for kc in range(NKC):
                        P.op("pe", lambda e, b=b, wc=wc, kc=kc, xb=xb, n=n: e.matmul(
                            kb.ps[b][:, :n], wi[:, kc, wc * 128:(wc + 1) * 128], xn2[xb][:, kc, :n],
                            start=(kc == 0), stop=(kc == NKC - 1)),
                            reads=[r_wi[wc // 4], r_xn2[xb]], writes=[kb.psr[b]])
                    s = sc_ % 4
                    sc_ += 1
                    if kind >= 2 and not is_ctx and DEBUG_STOP not in (4, 5):
                        r = rc % 2
                        rc += 1
                        if DEBUG_STOP != 7:
                            b2 = banks[bc % 4]
                            bc += 1
                            P.op("act", lambda e, r=r, b=b, n=n: e.activation(qs[r][:, :n], kb.ps[b][:, :n], AF.Identity),
                                 reads=[kb.psr[b]], writes=[r_qs[r]])
                            P.op("pe", lambda e, r=r, b2=b2, n=n: e.matmul(kb.ps[b2][:, :n], kb.swap, qs[r][:, :n],
                                                                           start=True, stop=True),
                                 reads=[r_qs[r], kb.r_cst], writes=[kb.psr[b2]])
                        else:
                            b2 = b
                        if DEBUG_STOP != 6:
                            P.op("dve", lambda e, r=r, b=b, n=n, tok0=tok0: e.tensor_tensor(
                                t1[r][:, :n], kb.ps[b][:, :n], rp[:, 0, tok0:tok0 + n], ALU.mult),
                                reads=[kb.psr[b], r_rp] + ([r_qs[r]] if DEBUG_STOP != 7 else []), writes=[r_t1[r]])
                            P.op("dve", lambda e, r=r, b2=b2, n=n, tok0=tok0: e.tensor_tensor(
                                t2[r][:, :n], kb.ps[b2][:, :n], rp[:, 1, tok0:tok0 + n], ALU.mult),
                                reads=[kb.psr[b2], r_rp], writes=[r_t2[r]])
                            P.op("dve", lambda e, r=r, s=s, n=n: e.tensor_tensor(
                                stg[s][:, :n], t1[r][:, :n], t2[r][:, :n], ALU.add),
                                reads=[r_t1[r], r_t2[r]], writes=[r_stg[s]])
                        else:
                            P.op("act", lambda e, s=s, b2=b2, n=n: e.activation(
                                stg[s][:, :n], kb.ps[b2][:, :n], AF.Identity),
                                reads=[kb.psr[b2]], writes=[r_stg[s]])
                    else:
                        scale = 0.125 if kind == 0 else 1.0
                        P.op("act", lambda e, s=s, b=b, n=n, scale=scale: e.activation(
                            stg[s][:, :n], kb.ps[b][:, :n], AF.Identity, scale=scale),
                            reads=[kb.psr[b]], writes=[r_stg[s]])
                    if is_ctx:
                        dst = kcT[(oc - 4) if kind == 1 else (oc - 8)]
                        kb.store(dst, stg[s][:, :n], [r_stg[s]], f"stg{s}")
                    else:
                        kb.store(qkT[oc][:, tok0:tok0 + n], stg[s][:, :n], [r_stg[s]], f"stg{s}")
                for tb in range(0 if DEBUG_STOP in (3, 5, 6, 7) else n // 128):
                    for vi, wc0 in enumerate((8, 20)):
                        b = banks[bc % 4]
                        bc += 1
                        for kc in range(NKC):
                            P.op("pe", lambda e, b=b, wc0=wc0, kc=kc, xb=xb, tb=tb: e.matmul(
                                kb.ps[b][:, :], xn2[xb][:, kc, tb * 128:(tb + 1) * 128],
                                wi[:, kc, wc0 * 128:(wc0 + 4) * 128],
                                start=(kc == 0), stop=(kc == NKC - 1)),
                                reads=[r_wi[wc0 // 4], r_xn2[xb]], writes=[kb.psr[b]])
                        s = sc_ % 4
                        sc_ += 1
                        if vi == 0:
                            P.op("act", lambda e, s=s, b=b: e.activation(stg[s][:], kb.ps[b][:], AF.Identity),
                                 reads=[kb.psr[b]], writes=[r_stg[s]])
                        else:
                            P.op("dve", lambda e, s=s, b=b: e.tensor_copy(stg[s][:], kb.ps[b][:]),
                                 reads=[kb.psr[b]], writes=[r_stg[s]])
                        if is_ctx:
                            kb.store(vc[tb][:, vi * 512:(vi + 1) * 512], stg[s][:], [r_stg[s]], f"stg{s}")
                        else:
                            kb.store(vtm[ti * 4 + tb][:, vi * 512:(vi + 1) * 512], stg[s][:], [r_stg[s]], f"stg{s}")
            P.barrier()
        kb.finish()
    return kb.nc


def build_B():
    kb = KB()
    P = kb.P
    qaT = kb.inp("qaT", [4, 128, TPC], BF16)
    qbT = kb.inp("qbT", [4, 128, TPC], BF16)
    kaB = kb.inp("kaB", [4, 128, 2816], BF16)
    vaP = kb.inp("vaP", [4, 128, 22 * 256], BF16)
    kbT = kb.inp("kbT", [4, 128, 8448], BF16)
    vbT = kb.inp("vbT", [4, 128, 66 * 128], BF16)
    TT = kb.inp("TT", [128, 8, 22 * 64], BF16)
    Rv = kb.inp("Rv", [4, 2, 8 * 512], BF16)
    E2 = kb.inp("E2", [2, 128], BF16)
    onesP = kb.inp("onesP", [128, 2, 128], BF16)
    x1T = kb.inp("x1T", [NKC, 128, TPC])
    cst = kb.inp("cst", [128, 3, 128])
    lamv = kb.inp("lamv", [128, 4, 64])
    sublng = kb.inp("sublng", [128, 1])
    cvec = kb.inp("cvec", [128, 8, 1])
    modwA = kb.inp("modwA", [D, 4 * D])
    modbA = kb.inp("modbA", [128, 32])
    modwB = kb.inp("modwB", [D, 4 * D])
    modbB = kb.inp("modbB", [128, 32])
    normg = kb.inp("normg", [128, 2, 8])
    w_out = kb.inp("w_out", [D, D])
    wg1 = kb.inp("wg1", [D, DFF])
    wu1 = kb.inp("wu1", [D, DFF])
    wd1 = kb.inp("wd1", [DFF, D])
    wg2 = kb.inp("wg2", [D, DFF])
    wu2 = kb.inp("wu2", [D, DFF])
    wd2 = kb.inp("wd2", [DFF, D])
    x4T = kb.outp("x4T", [NKC, 128, TPC])

    with ExitStack() as es:
        kb.load_consts(es, cst)
        an = kb.sb(es, "an", [128, NKC, TPC], BF16)
        r_an = [[P.res() for _ in range(4)] for _ in range(NKC)]
        ng = kb.sb(es, "ng", [128, 2, 8], F32)
        r_ng = P.res()
        P.dma("sp", "ng", ng[:], normg, writes=[r_ng])
        mvA, r_mvA = emit_mod(kb, es, modwA, 32, cvec, 1, modbA, "mA")
        mvB, r_mvB = emit_mod(kb, es, modwB, 32, cvec, 1, modbB, "mB")
        lam_t = kb.sb(es, "lam", [128, 4, 64], F32)
        lsc = kb.sb(es, "lsc", [128, 8], F32)
        sg8 = kb.sb(es, "sg8", [128, 1], F32)
        r_lam, r_lsc, r_sg8 = P.res(), P.res(), P.res()
        P.dma("sp", "lam", lam_t[:], lamv, writes=[r_lam])
        P.dma("sp", "sg8", sg8[:], sublng, writes=[r_sg8])
        P.op("dve", lambda e: e.tensor_tensor(lam_t[:, 0, :], lam_t[:, 0, :], lam_t[:, 1, :], ALU.mult),
             reads=[r_lam], writes=[r_lam])
        P.op("dve", lambda e: e.tensor_tensor(lam_t[:, 2, :], lam_t[:, 2, :], lam_t[:, 3, :], ALU.mult),
             reads=[r_lam], writes=[r_lam])
        P.op("dve", lambda e: e.reduce_sum(lsc[:, 0:1], lam_t[:, 0, :], mybir.AxisListType.X), reads=[r_lam], writes=[r_lsc])
        P.op("dve", lambda e: e.reduce_sum(lsc[:, 1:2], lam_t[:, 2, :], mybir.AxisListType.X), reads=[r_lam], writes=[r_lsc])
        P.op("act", lambda e: e.activation(lsc[:, 2:4], lsc[:, 0:2], AF.Exp), reads=[r_lsc], writes=[r_lsc])
        LAM_INIT = 0.8 - 0.6 * math.exp(-0.3 * 0)
        P.op("dve", lambda e: e.tensor_tensor(lsc[:, 4:5], lsc[:, 3:4], lsc[:, 2:3], ALU.subtract), reads=[r_lsc], writes=[r_lsc])
        P.op("dve", lambda e: e.tensor_scalar_add(lsc[:, 5:6], lsc[:, 4:5], -LAM_INIT), reads=[r_lsc], writes=[r_lsc])
        P.op("dve", lambda e: e.tensor_scalar_mul(sg8[:], sg8[:], 1.0 - LAM_INIT), reads=[r_sg8], writes=[r_sg8])
        neglam = lsc[:, 5:6]

        with ExitStack() as es2:
            tt = kb.sb(es2, "tt", [128, 8, 22 * 64], BF16)
            r_tt = P.res()
            P.dma("sp", "tt", tt[:], TT, writes=[r_tt])
            e2 = kb.sb(es2, "e2", [2, 128], BF16)
            r_e2 = P.res()
            P.dma("sp", "e2", e2[:], E2, writes=[r_e2])
            op_ = kb.sb(es2, "onesP", [128, 2, 128], BF16)
            r_op = P.res()
            P.dma("sp", "onesP", op_[:], onesP, writes=[r_op])
            rv = [kb.sb(es2, "rv", [2, 8 * 512], BF16) for _ in range(2)]
            r_rv = [P.res(), P.res()]
            qsb = [kb.sb(es2, "qsb", [128, 512], BF16) for _ in range(2)]
            r_q = [P.res(), P.res()]
            kA = [kb.sb(es2, "kA", [128, 1280], BF16) for _ in range(2)]
            r_kA = [P.res(), P.res()]
            vP = [kb.sb(es2, "vP", [128, 10 * 256], BF16) for _ in range(2)]
            r_vP = [P.res(), P.res()]
            kB_ = [kb.sb(es2, "kB", [128, 8448], BF16) for _ in range(2)]
            r_kB = [P.res(), P.res()]
            vB_ = [kb.sb(es2, "vB", [128, 66 * 128], BF16) for _ in range(2)]
            r_vB = [P.res(), P.res()]
            pT = [kb.sb(es2, "pT", [128, 512], BF16) for _ in range(4)]
            r_pT = [P.res() for _ in range(4)]
            ev = [kb.sb(es2, "ev", [128, 512], F32) for _ in range(4)]
            r_ev = [P.res() for _ in range(4)]
            sqb = kb.sb(es2, "sqb", [128, 512], BF16)
            r_sqb = P.res()
            epst = kb.sb(es2, "epst", [128, 1], F32)
            r_epst = P.res()
            P.op("dve", lambda e: e.memset(epst[:], EPS), writes=[r_epst])
            sb_i = 0
            p_i = 0
            qcnt = 0
            acnt = 0
            for qt in range(4):
                rs = qt % 2
                P.dma("sp", f"rv{rs}", rv[rs][:], Rv[qt], writes=[r_rv[rs]])
                for g in range(4):
                    a = acnt % 2
                    acnt += 1
                    P.dma("sp", f"kA{a}", kA[a][:, 0:1024], kaB[g][:, qt * 512:qt * 512 + 1024], writes=[r_kA[a]])
                    P.dma("sp", f"kA{a}", kA[a][:, 1024:1280], kaB[g][:, 2560:2816], writes=[r_kA[a]])
                    P.dma("sp", f"vP{a}", vP[a][:, 0:8 * 256], vaP[g][:, qt * 4 * 256:(qt * 4 + 8) * 256], writes=[r_vP[a]])
                    P.dma("sp", f"vP{a}", vP[a][:, 8 * 256:10 * 256], vaP[g][:, 20 * 256:22 * 256], writes=[r_vP[a]])
                    qq = qcnt % 2
                    qcnt += 1
                    P.dma("sp", f"q{qq}", qsb[qq][:], qaT[g][:, qt * 512:(qt + 1) * 512], writes=[r_q[qq]])
                    nO, nL = 4, 5
                    for jt in range(10):
                        for hh in range(2):
                            head = 2 * g + hh
                            rows = slice(hh * 64, (hh + 1) * 64)
                            b = sb_i % 4
                            sb_i += 1
                            band = jt < 8
                            P.op("pe", lambda e, b=b, a=a, rows=rows, jt=jt, qq=qq, band=band: e.matmul(
                                kb.ps[b][:], kA[a][rows, jt * 128:(jt + 1) * 128], qsb[qq][rows, :],
                                start=True, stop=(not band)),
                                reads=[r_kA[a], r_q[qq]], writes=[kb.psr[b]])
                            if band:
                                s_j = 14 - 2 * jt
                                P.op("pe", lambda e, b=b, head=head, s_j=s_j: e.matmul(
                                    kb.ps[b][:], kb.ident, tt[:, head, s_j * 64:(s_j + 8) * 64], start=False, stop=False),
                                    reads=[r_tt, kb.r_cst], writes=[kb.psr[b]])
                                P.op("pe", lambda e, b=b, rs=rs, jt=jt: e.matmul(
                                    kb.ps[b][:], e2[:, :], rv[rs][:, jt * 512:(jt + 1) * 512], start=False, stop=True),
                                    reads=[r_e2, r_rv[rs]], writes=[kb.psr[b]])
                            pi = p_i % 4
                            p_i += 1
                            P.op("act", lambda e, pi=pi, b=b: e.activation(pT[pi][:], kb.ps[b][:], AF.Exp),
                                 reads=[kb.psr[b]], writes=[r_pT[pi]])
                            first = (jt == 0 and hh == 0)
                            last = (jt == 9 and hh == 1)
                            P.op("pe", lambda e, pi=pi, a=a, jt=jt, hh=hh, first=first, last=last: e.matmul(
                                kb.ps[nO][:], vP[a][:, (jt * 2 + hh) * 128:(jt * 2 + hh + 1) * 128], pT[pi][:], start=first, stop=last),
                                reads=[r_vP[a], r_pT[pi]], writes=[kb.psr[nO]])
                            P.op("pe", lambda e, pi=pi, hh=hh, first=first, last=last: e.matmul(
                                kb.ps[nL][:], op_[:, hh, :], pT[pi][:], start=first, stop=last),
                                reads=[r_op, r_pT[pi]], writes=[kb.psr[nL]])
                    P.op("dve", lambda e: e.reciprocal(ev[0][:], kb.ps[nL][:]), reads=[kb.psr[nL]], writes=[r_ev[0]])
                    P.op("dve", lambda e, g=g, qt=qt: e.tensor_tensor(
                        an[:, g, qt * 512:(qt + 1) * 512], kb.ps[nO][:], ev[0][:], ALU.mult),
                        reads=[kb.psr[nO], r_ev[0]], writes=[r_an[g][qt]])
            for h in range(4):
                a = h % 2
                P.dma("sp", f"kB{a}", kB_[a][:], kbT[h], writes=[r_kB[a]])
                P.dma("sp", f"vB{a}", vB_[a][:], vbT[h], writes=[r_vB[a]])
                for qt in range(4):
                    qq = qcnt % 2
                    qcnt += 1
                    P.dma("sp", f"q{qq}", qsb[qq][:], qbT[h][:, qt * 512:(qt + 1) * 512], writes=[r_q[qq]])
                    bO = [4, 5]
                    bL = [6, 7]
                    for kt in range(66):
                        for m in range(2):
                            rows = slice(m * 64, (m + 1) * 64)
                            b = sb_i % 4
                            sb_i += 1
                            P.op("pe", lambda e, b=b, a=a, rows=rows, kt=kt, qq=qq: e.matmul(
                                kb.ps[b][:], kB_[a][rows, kt * 128:(kt + 1) * 128], qsb[qq][rows, :],
                                start=True, stop=True),
                                reads=[r_kB[a], r_q[qq]], writes=[kb.psr[b]])
                            pi = p_i % 4
                            p_i += 1
                            P.op("act", lambda e, pi=pi, b=b: e.activation(pT[pi][:], kb.ps[b][:], AF.Exp, scale=0.125),
                                 reads=[kb.psr[b]], writes=[r_pT[pi]])
                            P.op("pe", lambda e, pi=pi, a=a, kt=kt, m=m: e.matmul(
                                kb.ps[bO[m]][:], vB_[a][:, kt * 128:(kt + 1) * 128], pT[pi][:], start=(kt == 0), stop=(kt == 65)),
                                reads=[r_vB[a], r_pT[pi]], writes=[kb.psr[bO[m]]])
                            P.op("pe", lambda e, pi=pi, kt=kt, m=m: e.matmul(
                                kb.ps[bL[m]][:], kb.ones, pT[pi][:], start=(kt == 0), stop=(kt == 65)),
                                reads=[kb.r_cst, r_pT[pi]], writes=[kb.psr[bL[m]]])
                    for m in range(2):
                        P.op("dve", lambda e, m=m: e.reciprocal(ev[m][:], kb.ps[bL[m]][:]),
                             reads=[kb.psr[bL[m]]], writes=[r_ev[m]])
                        P.op("dve", lambda e, m=m: e.tensor_tensor(ev[m][:], kb.ps[bO[m]][:], ev[m][:], ALU.mult),
                             reads=[kb.psr[bO[m]], r_ev[m]], writes=[r_ev[m]])
                    P.op("dve", lambda e: e.scalar_tensor_tensor(ev[2][:], ev[1][:], neglam, ev[0][:], ALU.mult, ALU.add),
                         reads=[r_ev[0], r_ev[1], r_lsc], writes=[r_ev[2]])
                    P.op("act", lambda e: e.activation(sqb[:], ev[2][:], AF.Square), reads=[r_ev[2]], writes=[r_sqb])
                    b = sb_i % 4
                    sb_i += 1
                    P.op("pe", lambda e, b=b: e.matmul(kb.ps[b][:], kb.ones, sqb[:], start=True, stop=True),
                         reads=[r_sqb, kb.r_cst], writes=[kb.psr[b]])
                    P.op("act", lambda e, b=b: e.activation(ev[3][:], kb.ps[b][:], AF.Sqrt, bias=epst[:, 0:1], scale=1.0 / 128),
                         reads=[kb.psr[b], r_epst], writes=[r_ev[3]])
                    P.op("dve", lambda e: e.reciprocal(ev[3][:], ev[3][:]), reads=[r_ev[3]], writes=[r_ev[3]])
                    P.op("dve", lambda e: e.tensor_tensor(ev[2][:], ev[2][:], ev[3][:], ALU.mult),
                         reads=[r_ev[2], r_ev[3]], writes=[r_ev[2]])
                    P.op("act", lambda e, h=h, qt=qt: e.activation(
                        an[:, 4 + h, qt * 512:(qt + 1) * 512], ev[2][:], AF.Identity, scale=sg8[:, 0:1]),
                        reads=[r_ev[2], r_sg8], writes=[r_an[4 + h][qt]])
            P.barrier()

        with ExitStack() as es3:
            x_t = kb.sb(es3, "x", [128, NKC, TPC], F32)
            lat = make_tiles(x_t, P, TPC)
            for kc in range(NKC):
                P.dma("sp", f"xin{kc}", x_t[:, kc, :], x1T[kc], writes=[t.xres[kc] for t in lat])
            G5 = mvA[:, 0:8, 0]
            with ExitStack() as es4:
                wo = kb.sb(es4, "wo", [128, NKC, D], BF16)
                r_wo = P.res()
                P.dma("pool", "wo", wo[:], w_out.rearrange("(kc p) n -> p kc n", p=128), writes=[r_wo])
                bc = 0
                for ti, t in enumerate(lat):
                    for oc in range(NKC):
                        b = 1 + bc % 4
                        bc += 1
                        for kc in range(NKC):
                            P.op("pe", lambda e, b=b, oc=oc, kc=kc, ti=ti: e.matmul(
                                kb.ps[b][:], wo[:, kc, oc * 128:(oc + 1) * 128], an[:, kc, ti * 512:(ti + 1) * 512],
                                start=(kc == 0), stop=(kc == NKC - 1)),
                                reads=[r_wo, r_an[kc][ti]], writes=[kb.psr[b]])
                        P.op("dve", lambda e, b=b, oc=oc, t=t: e.scalar_tensor_tensor(
                            t.x[oc], kb.ps[b][:], G5[:, oc:oc + 1], t.x[oc], ALU.mult, ALU.add),
                            reads=[kb.psr[b], t.xres[oc], r_mvA], writes=[t.xres[oc]])
                P.barrier()
            A1, B1, G1, r1 = emit_ab(kb, es3, mvA, r_mvA, 0, 1, 2, ng[:, 0, :], r_ng, "f2", i_gate=3, gate_mul=0.5)
            for t in lat:
                t.A, t.B, t.HG, t.modres = A1, B1, G1, r1
            xn_ext = lambda kc, a, b: an[:, kc, a:b]
            emit_ffn(kb, [lat[0:2], lat[2:4]], wg1, wu1, wd1, xn_ext=xn_ext)
            A2, B2, G2, r2 = emit_ab(kb, es3, mvB, r_mvB, 0, 0, 1, ng[:, 1, :], r_ng, "f3", i_gate=2, gate_mul=0.5)
            for t in lat:
                t.A, t.B, t.HG, t.modres = A2, B2, G2, r2
            emit_ffn(kb, [lat[0:2], lat[2:4]], wg2, wu2, wd2, xn_ext=xn_ext)
            for kc in range(NKC):
                kb.store(x4T[kc], x_t[:, kc, :], [t.xres[kc] for t in lat], f"x{kc}")
            kb.finish()
    return kb.nc


def build_C():
    kb = KB()
    P = kb.P
    x4T = kb.inp("x4T", [NKC, 128, TPC])
    xhT = kb.inp("xhT", [128, NKC, 2])
    hmask = kb.inp("hmask", [128, 2])
    cvec = kb.inp("cvec", [128, 8, 1])
    modw = kb.inp("modw", [D, 6 * D])
    modb = kb.inp("modb", [128, 48])
    normg = kb.inp("normg", [128, 3, 8])
    cw = kb.inp("cw", [128, 3, 8])
    cw_in = kb.inp("cw_in", [D, 3 * D])
    cw_out = kb.inp("cw_out", [D, D])
    wg = kb.inp("wg", [D, DFF])
    wu = kb.inp("wu", [D, DFF])
    wd = kb.inp("wd", [DFF, D])
    cst = kb.inp("cst", [128, 3, 128])
    outT = kb.outp("outT", [NKC, 128, TPC])

    with ExitStack() as es:
        kb.load_consts(es, cst)
        x_t = kb.sb(es, "x", [128, NKC, TPC], F32)
        xh_t = kb.sb(es, "xh", [128, NKC, 2], F32)
        hm = kb.sb(es, "hm", [128, 2], F32)
        ng = kb.sb(es, "ng", [128, 3, 8], F32)
        cwt = kb.sb(es, "cwt", [128, 3, 8], F32)
        r_ng, r_hm, r_cw = P.res(), P.res(), P.res()
        P.dma("sp", "ng", ng[:], normg, writes=[r_ng])
        P.dma("sp", "hm", hm[:], hmask, writes=[r_hm])
        P.dma("sp", "cw", cwt[:], cw, writes=[r_cw])
        lat = make_tiles(x_t, P, TPC)
        halo = make_tiles(xh_t, P, 2)
        for kc in range(NKC):
            P.dma("sp", f"xin{kc}", x_t[:, kc, :], x4T[kc], writes=[t.xres[kc] for t in lat])
        P.dma("sp", "xh", xh_t[:], xhT, writes=halo[0].xres)
        mv, r_mv = emit_mod(kb, es, modw, 48, cvec, 1, modb, "mC")
        Am, Bm, _, rm = emit_ab(kb, es, mv, r_mv, 0, 0, 1, ng[:, 0, :], r_ng, "cm")
        G5 = mv[:, 16:24, 0]
        with ExitStack() as es2:
            U = kb.sb(es2, "U", [128, NKC, TPC + 2], F32)
            r_U = [[P.res() for _ in range(6)] for _ in range(NKC)]
            scr = make_scratch(kb, es2)
            xn = [kb.sb(es2, "xnc", [128, NKC, 512], BF16)] * 2
            r_xn = [P.res()] * 2
            tmpc = [kb.sb(es2, "tmpc", [128, 512], F32) for _ in range(2)]
            r_tmpc = [P.res(), P.res()]
            wi_src = cw_in.rearrange("(kc p) n -> p kc n", p=128)
            with ExitStack() as es3:
                wch = kb.sb(es3, "wch", [128, NKC, 2 * D], BF16)
                r_wch = [P.res() for _ in range(4)]
                for g4 in range(4):
                    P.dma("pool", f"wch{g4}", wch[:, :, g4 * 512:(g4 + 1) * 512],
                          wi_src[:, :, D + g4 * 512:D + (g4 + 1) * 512], writes=[r_wch[g4]])
                bc = 0
                for ti, t in enumerate(lat + halo):
                    is_h = ti >= 4
                    n = t.n
                    xb = ti % 2
                    emit_normmod(kb, scr, t, Am, Bm, rm, lambda kc, xb=xb, n=n: xn[xb][:, kc, :n], r_xn[xb])
                    for c in range(NKC):
                        b1 = 1 + bc % 4
                        b2 = 1 + (bc + 1) % 4
                        bc += 2
                        for (bb, wc) in ((b1, c), (b2, 8 + c)):
                            for kc in range(NKC):
                                P.op("pe", lambda e, bb=bb, wc=wc, kc=kc, xb=xb, n=n: e.matmul(
                                    kb.ps[bb][:, :n], wch[:, kc, wc * 128:(wc + 1) * 128], xn[xb][:, kc, :n],
                                    start=(kc == 0), stop=(kc == NKC - 1)),
                                    reads=[r_wch[wc // 4], r_xn[xb]], writes=[kb.psr[bb]])
                        q = c % 2
                        P.op("act", lambda e, q=q, b1=b1, n=n: e.activation(tmpc[q][:, :n], kb.ps[b1][:, :n], AF.Identity),
                             reads=[kb.psr[b1]], writes=[r_tmpc[q]])
                        if not is_h:
                            P.op("dve", lambda e, q=q, b2=b2, c=c, ti=ti: e.tensor_tensor(
                                U[:, c, 1 + ti * 512:1 + (ti + 1) * 512], tmpc[q][:], kb.ps[b2][:], ALU.mult),
                                reads=[r_tmpc[q], kb.psr[b2]], writes=[r_U[c][1 + ti]])
                        else:
                            P.op("dve", lambda e, q=q, b2=b2: e.tensor_tensor(
                                tmpc[q][:, 0:2], tmpc[q][:, 0:2], kb.ps[b2][:, 0:2], ALU.mult),
                                reads=[r_tmpc[q], kb.psr[b2]], writes=[r_tmpc[q]])
                            P.op("dve", lambda e, q=q, c=c: e.tensor_tensor(
                                U[:, c, 0:1], tmpc[q][:, 0:1], hm[:, 0:1], ALU.mult),
                                reads=[r_tmpc[q], r_hm], writes=[r_U[c][0]])
                            P.op("dve", lambda e, q=q, c=c: e.tensor_tensor(
                                U[:, c, TPC + 1:TPC + 2], tmpc[q][:, 1:2], hm[:, 1:2], ALU.mult),
                                reads=[r_tmpc[q], r_hm], writes=[r_U[c][5]])
                P.barrier()
            with ExitStack() as es3:
                wbg = kb.sb(es3, "wbg", [128, NKC, D], BF16)
                wo = kb.sb(es3, "wo", [128, NKC, D], BF16)
                r_wbg, r_wo = P.res(), P.res()
                P.dma("pool", "wbg", wbg[:], wi_src[:, :, 0:D], writes=[r_wbg])
                P.dma("pool", "wo", wo[:], cw_out.rearrange("(kc p) n -> p kc n", p=128), writes=[r_wo])
                Z = [kb.sb(es3, "Z", [128, NKC, 512], BF16)] * 2
                r_Z = [[P.res() for _ in range(NKC)]] * 2
                bc = 0
                for ti, t in enumerate(lat):
                    xb = ti % 2
                    zb = ti % 2
                    emit_normmod(kb, scr, t, Am, Bm, rm, lambda kc, xb=xb: xn[xb][:, kc, :], r_xn[xb])
                    for c in range(NKC):
                        b = 1 + bc % 4
                        bc += 1
                        for kc in range(NKC):
                            P.op("pe", lambda e, b=b, c=c, kc=kc, xb=xb: e.matmul(
                                kb.ps[b][:], wbg[:, kc, c * 128:(c + 1) * 128], xn[xb][:, kc, :],
                                start=(kc == 0), stop=(kc == NKC - 1)),
                                reads=[r_wbg, r_xn[xb]], writes=[kb.psr[b]])
                        q = c % 2
                        base = 1 + ti * 512
                        ru = [r_U[c][k] for k in range(6)]
                        P.op("dve", lambda e, q=q, c=c, base=base: e.tensor_scalar_mul(
                            tmpc[q][:], U[:, c, base:base + 512], cwt[:, 1, c:c + 1]),
                            reads=ru + [r_cw], writes=[r_tmpc[q]])
                        P.op("dve", lambda e, q=q, c=c, base=base: e.scalar_tensor_tensor(
                            tmpc[q][:], U[:, c, base - 1:base + 511], cwt[:, 0, c:c + 1], tmpc[q][:], ALU.mult, ALU.add),
                            reads=ru + [r_cw, r_tmpc[q]], writes=[r_tmpc[q]])
                        P.op("dve", lambda e, q=q, c=c, base=base: e.scalar_tensor_tensor(
                            tmpc[q][:], U[:, c, base + 1:base + 513], cwt[:, 2, c:c + 1], tmpc[q][:], ALU.mult, ALU.add),
                            reads=ru + [r_cw, r_tmpc[q]], writes=[r_tmpc[q]])
                        P.op("dve", lambda e, q=q, c=c, b=b, zb=zb: e.tensor_tensor(
                            Z[zb][:, c, :], tmpc[q][:], kb.ps[b][:], ALU.mult),
                            reads=[r_tmpc[q], kb.psr[b]], writes=[r_Z[zb][c]])
                    for oc in range(NKC):
                        b = 1 + bc % 4
                        bc += 1
                        for kc in range(NKC):
                            P.op("pe", lambda e, b=b, oc=oc, kc=kc, zb=zb: e.matmul(
                                kb.ps[b][:], wo[:, kc, oc * 128:(oc + 1) * 128], Z[zb][:, kc, :],
                                start=(kc == 0), stop=(kc == NKC - 1)),
                                reads=[r_wo, r_Z[zb][kc]], writes=[kb.psr[b]])
                        P.op("dve", lambda e, b=b, oc=oc, t=t: e.scalar_tensor_tensor(
                            t.x[oc], kb.ps[b][:], G5[:, oc:oc + 1], t.x[oc], ALU.mult, ALU.add),
                            reads=[kb.psr[b], t.xres[oc], r_mv], writes=[t.xres[oc]])
                P.barrier()
        A1, B1, G1, r1 = emit_ab(kb, es, mv, r_mv, 0, 3, 4, ng[:, 1, :], r_ng, "f4", i_gate=5, gate_mul=0.5)
        for t in lat:
            t.A, t.B, t.HG, t.modres = A1, B1, G1, r1
        emit_ffn(kb, [lat[0:2], lat[2:4]], wg, wu, wd)
        with ExitStack() as es2:
            scr = make_scratch(kb, es2)
            ob = [kb.sb(es2, "ob", [128, 512], F32) for _ in range(4)]
            r_ob = [P.res() for _ in range(4)]
            oc_ = 0
            for ti, t in enumerate(lat):
                n = t.n
                ps_stat, r_ps = kb.ps[0], kb.psr[0]
                for kc in range(NKC):
                    s = kc % 2
                    P.op("act", lambda e, s=s, kc=kc, t=t: e.activation(scr["sq"][s][:], t.x[kc], AF.Square),
                         reads=[t.xres[kc]], writes=[scr["r_sq"][s]])
                    P.op("pe", lambda e, s=s, kc=kc: e.matmul(ps_stat[:], kb.ones, scr["sq"][s][:],
                                                              start=(kc == 0), stop=(kc == NKC - 1)),
                         reads=[scr["r_sq"][s], kb.r_cst], writes=[r_ps])
                rstd = scr["rstd"]
                P.op("act", lambda e: e.activation(rstd[:], ps_stat[:], AF.Sqrt, bias=scr["eps"][:, 0:1], scale=1.0 / D),
                     reads=[r_ps, scr["r_eps"]], writes=[scr["r_rstd"]])
                P.op("dve", lambda e: e.reciprocal(rstd[:], rstd[:]), reads=[scr["r_rstd"]], writes=[scr["r_rstd"]])
                for kc in range(NKC):
                    o = oc_ % 4
                    oc_ += 1
                    P.op("dve", lambda e, o=o, kc=kc, t=t: e.scalar_tensor_tensor(
                        ob[o][:], t.x[kc], ng[:, 2, kc:kc + 1], rstd[:], ALU.mult, ALU.mult),
                        reads=[t.xres[kc], scr["r_rstd"], r_ng], writes=[r_ob[o]])
                    kb.store(outT[kc][:, ti * 512:(ti + 1) * 512], ob[o][:], [r_ob[o]], f"ob{o}")
            kb.finish()
    return kb.nc


_CACHE = {}


def _get(name, fn):
    if name not in _CACHE:
        _CACHE[name] = fn()
    return _CACHE[name]


def _pk(v, nch):
    return np.ascontiguousarray(np.asarray(v, np.float32).reshape(nch, 128).T)


def _consts():
    c = np.zeros((128, 3, 128), np.float32)
    c[:, 0, :] = 1.0
    c[:, 1, :] = np.eye(128, dtype=np.float32)
    idx = np.arange(128)
    c[idx, 2, idx ^ 1] = 1.0
    return c


def _rope_tables(t0):
    t = np.arange(t0, t0 + TPC)
    row = (t // 64).astype(np.float64)
    col = (t % 64).astype(np.float64)
    inv = 10000.0 ** (-np.arange(16, dtype=np.float64) / 16)
    ang = np.concatenate([row[:, None] * inv, col[:, None] * inv], -1)
    cos = np.cos(ang.astype(np.float32).astype(np.float64))
    sin = np.sin(ang.astype(np.float32).astype(np.float64))
    ang32 = np.concatenate([row[:, None].astype(np.float32) * inv.astype(np.float32),
                            col[:, None].astype(np.float32) * inv.astype(np.float32)], -1)
    cos = np.cos(ang32)
    sin = np.sin(ang32)
    tab = np.zeros((128, 2, TPC), np.float32)
    for p in range(128):
        d = p % 64
        i = d // 2
        tab[p, 0] = cos[:, i]
        tab[p, 1] = (-sin[:, i]) if d % 2 == 0 else sin[:, i]
    return tab


def _bias_table(rpb):
    rpb = np.asarray(rpb, np.float32)
    qc = np.arange(64)
    kc = np.arange(64)
    col_start = np.clip(qc - 8, 0, 48)
    cmask = (kc[None, :] >= col_start[:, None]) & (kc[None, :] < col_start[:, None] + 16)
    dc = np.clip(kc[None, :] - qc[:, None] + 15, 0, 30)
    TT = np.zeros((128, 8, 22, 64), np.float32)
    for half in range(2):
        for jj in range(22):
            dr = 10 - jj + half
            if -7 <= dr <= 7:
                val = rpb[:, dr + 7, :][:, dc]
            else:
                val = np.zeros((8, 64, 64), np.float32)
            val = np.where(cmask[None], val, NEG)
            TT[half * 64:(half + 1) * 64, :, jj, :] = np.transpose(val, (2, 0, 1))
    return TT.reshape(128, 8, 22 * 64).astype(NPBF)


def _row_valid(r0):
    Rv = np.zeros((4, 2, 8, 8, 64), np.float32)
    for qt in range(4):
        R0 = r0 + 8 * qt
        for j in range(8):
            for a in range(2):
                kr = R0 - 4 + 2 * j + a
                for i in range(8):
                    qr = R0 + i
                    st = min(max(qr - 4, 0), 120)
                    ok = (st <= kr < st + 8)
                    Rv[qt, a, j, i, :] = 0.0 if ok else NEG
    return Rv.reshape(4, 2, 8 * 512).astype(NPBF)


def _run(nc, in_maps):
    res = run_bass_kernel_spmd(nc, in_maps, core_ids=list(range(8)))
    return res.results


def kernel(x, c, ctx, c_ctx, mod_w, mod_b, norm_g, ffn_w_gate, ffn_w_up, ffn_w_down,
           attn_w_in, attn_w_out, na_rpb, diff_lambda, diff_subln_g,
           conv_w_in, conv_w_out, conv_w, final_g):
    f32 = lambda a: np.ascontiguousarray(np.asarray(a, np.float32))
    x, c, ctx, c_ctx = f32(x), f32(c), f32(ctx), f32(c_ctx)
    mod_w, mod_b, norm_g = f32(mod_w), f32(mod_b), f32(norm_g)
    cst = _consts()
    NCORE = 8

    ncA = build_A()
    mapsA = []
    wA = dict(modw=f32(mod_w[0][:, :5 * D]), modb=_pk(mod_b[0][:5 * D], 40),
              normg=f32(np.stack([_pk(norm_g[0, 0], 8), _pk(norm_g[0, 1], 8)], 1)),
              wg=f32(ffn_w_gate[0, 0]), wu=f32(ffn_w_up[0, 0]), wd=f32(ffn_w_down[0, 0]),
              w_in=f32(attn_w_in[0]), cst=cst)
    for core in range(NCORE):
        b, t0 = core // 4, (core % 4) * TPC
        xs = x[b, t0:t0 + TPC]
        xT = f32(xs.T.reshape(NKC, 128, TPC))
        cxT = f32(ctx[b].T.reshape(NKC, 128, 256))
        cv = np.stack([c[b].reshape(8, 128).T, c_ctx.reshape(8, 128).T], -1)
        mapsA.append(dict(xT=xT, cxT=cxT, cvec=f32(cv), rope=_rope_tables(t0), **wA))
    resA = _run(ncA, mapsA)

    ncB = build_B()
    mapsB = []
    E2 = np.zeros((2, 128), np.float32)
    E2[0, :64] = 1.0
    E2[1, 64:] = 1.0
    onesP = np.zeros((128, 2, 128), np.float32)
    onesP[:, 0, :64] = 1.0
    onesP[:, 1, 64:] = 1.0
    TTh = _bias_table(na_rpb[0])
    lamv = f32(np.broadcast_to(np.asarray(diff_lambda[0], np.float32)[None], (128, 4, 64)))
    modwB = np.zeros((D, 4 * D), np.float32)
    modwB[:, :3 * D] = mod_w[1][:, :3 * D]
    modbB = np.zeros((128, 32), np.float32)
    modbB[:, :24] = _pk(mod_b[1][:3 * D], 24)
    wB = dict(TT=TTh, E2=E2.astype(NPBF), onesP=onesP.astype(NPBF), cst=cst, lamv=lamv,
              sublng=f32(np.asarray(diff_subln_g[0], np.float32).reshape(128, 1)),
              modwA=f32(mod_w[0][:, 5 * D:]), modbA=_pk(mod_b[0][5 * D:], 32),
              modwB=modwB, modbB=modbB,
              normg=f32(np.stack([_pk(norm_g[0, 2], 8), _pk(norm_g[1, 0], 8)], 1)),
              w_out=f32(attn_w_out[0]),
              wg1=f32(ffn_w_gate[0, 1]), wu1=f32(ffn_w_up[0, 1]), wd1=f32(ffn_w_down[0, 1]),
              wg2=f32(ffn_w_gate[1, 0]), wu2=f32(ffn_w_up[1, 0]), wd2=f32(ffn_w_down[1, 0]))
    per_batch = {}
    for b in range(2):
        cores = [4 * b + i for i in range(4)]
        qk = [np.asarray(resA[cc]["qkT"]) for cc in cores]
        vt = [np.asarray(resA[cc]["vtm"]).reshape(TPC, 1024) for cc in cores]
        kc_ = np.asarray(resA[cores[0]]["kcT"])
        vc_ = np.asarray(resA[cores[0]]["vc"]).reshape(256, 1024)
        ka_full = np.concatenate([q[4:8] for q in qk], axis=2)
        z = np.zeros((4, 128, 256), ka_full.dtype)
        ka_pad = np.concatenate([z, ka_full, z], axis=2)
        va_full = np.concatenate([v[:, 0:512] for v in vt], axis=0)
        zv = np.zeros((256, 512), va_full.dtype)
        va_pad = np.concatenate([zv, va_full, zv], axis=0)
        kb_full = np.concatenate([q[12:16] for q in qk] , axis=2)
        kbT = np.ascontiguousarray(np.concatenate([kb_full, kc_[4:8]], axis=2))
        vb_full = np.concatenate([v[:, 512:1024] for v in vt] + [vc_[:, 512:1024]], axis=0)
        vbT = np.ascontiguousarray(vb_full.reshape(66, 128, 4, 128).transpose(2, 1, 0, 3).reshape(4, 128, 66 * 128))
        per_batch[b] = (ka_pad, va_pad, kc_, vc_, kbT, vbT)
    for core in range(NCORE):
        b, t0 = core // 4, (core % 4) * TPC
        ka_pad, va_pad, kc_, vc_, kbT, vbT = per_batch[b]
        qk = np.asarray(resA[core]["qkT"])
        kaB = np.ascontiguousarray(np.concatenate([ka_pad[:, :, t0:t0 + 2560], kc_[0:4]], axis=2))
        vband = np.concatenate([va_pad[t0:t0 + 2560], vc_[:, 0:512]], axis=0).reshape(2816, 8, 64)
        vP = np.zeros((2816, 8, 128), vband.dtype)
        vP[:, 0::2, 0:64] = vband[:, 0::2]
        vP[:, 1::2, 64:128] = vband[:, 1::2]
        vaP = np.ascontiguousarray(vP.reshape(22, 128, 4, 256).transpose(2, 1, 0, 3).reshape(4, 128, 22 * 256))
        cv = c[b].reshape(8, 128).T[:, :, None]
        mapsB.append(dict(qaT=np.ascontiguousarray(qk[0:4]), qbT=np.ascontiguousarray(qk[8:12]),
                          kaB=kaB, vaP=vaP, kbT=kbT, vbT=vbT, Rv=_row_valid((core % 4) * 32),
                          x1T=np.asarray(resA[core]["x1T"]), cvec=f32(cv), **wB))
    resB = _run(ncB, mapsB)

    ncC = build_C()
    mapsC = []
    wC = dict(modw=f32(mod_w[1][:, 3 * D:]), modb=_pk(mod_b[1][3 * D:], 48),
              normg=f32(np.stack([_pk(norm_g[1, 1], 8), _pk(norm_g[1, 2], 8), _pk(final_g, 8)], 1)),
              cw=f32(np.stack([_pk(np.asarray(conv_w[0], np.float32)[k], 8) for k in range(3)], 1)),
              cw_in=f32(conv_w_in[0]), cw_out=f32(conv_w_out[0]),
              wg=f32(ffn_w_gate[1, 1]), wu=f32(ffn_w_up[1, 1]), wd=f32(ffn_w_down[1, 1]), cst=cst)
    x4 = [np.asarray(resB[cc]["x4T"]) for cc in range(NCORE)]
    for core in range(NCORE):
        b = core // 4
        xh = np.zeros((128, NKC, 2), np.float32)
        hmask = np.zeros((128, 2), np.float32)
        if core % 4 != 0:
            xh[:, :, 0] = x4[core - 1][:, :, TPC - 1].T
            hmask[:, 0] = 1.0
        if core % 4 != 3:
            xh[:, :, 1] = x4[core + 1][:, :, 0].T
            hmask[:, 1] = 1.0
        cv = c[b].reshape(8, 128).T[:, :, None]
        mapsC.append(dict(x4T=x4[core], xhT=xh, hmask=hmask, cvec=f32(cv), **wC))
    resC = _run(ncC, mapsC)

    out = np.zeros((2, 8192, D), np.float32)
    for core in range(NCORE):
        b, t0 = core // 4, (core % 4) * TPC
        oT = np.asarray(resC[core]["outT"]).reshape(D, TPC)
        out[b, t0:t0 + TPC] = oT.T
    return out
```

```python
import math
from contextlib import ExitStack

import numpy as np
import ml_dtypes
import concourse.bass as bass
import concourse.mybir as mybir
from concourse.bass_utils import run_bass_kernel_spmd

F32 = mybir.dt.float32
BF16 = mybir.dt.bfloat16
ALU = mybir.AluOpType
AF = mybir.ActivationFunctionType
NPBF = ml_dtypes.bfloat16

D = 1024
DFF = 2816
NKC = 8
NFC = 22
TPC = 2048
EPS = 1e-6
NEG = -30000.0
DEBUG_STOP = 0

ENGS = ["pe", "act", "dve", "pool", "sp"]
SEM_CHUNK = 30000


class Res:
    __slots__ = ("name", "w", "r")

    def __init__(self, name):
        self.name = name
        self.w = None
        self.r = []


class Op:
    __slots__ = ("eng", "fn", "deps", "signal", "sig", "dma_key", "dma_val", "dma_inc")

    def __init__(self, eng, fn):
        self.eng = eng
        self.fn = fn
        self.deps = []
        self.signal = False
        self.sig = None
        self.dma_key = None
        self.dma_val = 0
        self.dma_inc = 16


class Prog:
    def __init__(self, nc):
        self.nc = nc
        self.q = {e: [] for e in ENGS}
        self.dma_cnt = {}
        self.nres = 0
        self.pending = {e: [] for e in ENGS}
        self.dmas = []

    def res(self, name=None):
        self.nres += 1
        return Res(name or f"r{self.nres}")

    def _add(self, eng, fn, reads, writes, dma_key=None, inc=16):
        op = Op(eng, fn)
        deps = []
        for r in reads:
            if r.w is not None:
                deps.append(r.w)
        for w in writes:
            if w.w is not None:
                deps.append(w.w)
            deps.extend(w.r)
        if self.pending[eng]:
            deps.extend(self.pending[eng])
            self.pending[eng] = []
        seen = set()
        for d in deps:
            if id(d) in seen:
                continue
            seen.add(id(d))
            if d.dma_key is None:
                if eng == "pe" and d.eng == "pe":
                    continue
                d.signal = True
            op.deps.append(d)
        if dma_key is not None:
            op.dma_key = dma_key
            self.dma_cnt[dma_key] = self.dma_cnt.get(dma_key, 0) + inc
            op.dma_val = self.dma_cnt[dma_key]
            op.dma_inc = inc
            self.dmas.append(op)
        for r in reads:
            r.r.append(op)
        for w in writes:
            w.w = op
            w.r = []
        self.q[eng].append(op)
        return op

    def op(self, eng, fn, reads=(), writes=()):
        return self._add(eng, fn, reads, writes)

    def dma(self, eng, key, out, in_, reads=(), writes=()):
        return self._add(eng, lambda e: e.dma_start(out=out, in_=in_), reads, writes, dma_key=key)

    def barrier(self):
        lasts = []
        for e in ENGS:
            for op in reversed(self.q[e]):
                if op.dma_key is None:
                    lasts.append(op)
                    break
        lasts.extend(self.dmas)
        self.dmas = []
        for e in ENGS:
            self.pending[e] = list(lasts)

    def emit(self):
        nc = self.nc
        sems = {}

        def sem(name):
            if name not in sems:
                sems[name] = nc.alloc_semaphore(name)
            return sems[name]

        for e in ENGS:
            k = 0
            for op in self.q[e]:
                if op.dma_key is None and op.signal:
                    op.sig = (f"s_{e}_{k // SEM_CHUNK}", k % SEM_CHUNK + 1)
                    k += 1
                elif op.dma_key is not None:
                    op.sig = (f"d_{op.dma_key}", op.dma_val)

        def run(e, engine):
            waited = {}
            for op in self.q[e]:
                need = {}
                for d in op.deps:
                    s, v = d.sig
                    if waited.get(s, 0) >= v:
                        continue
                    if need.get(s, 0) < v:
                        need[s] = v
                for s, v in need.items():
                    engine.wait_ge(sem(s), v)
                    waited[s] = v
                ins = op.fn(engine)
                if op.dma_key is not None:
                    ins.then_inc(sem(op.sig[0]), op.dma_inc)
                elif op.signal:
                    ins.then_inc(sem(op.sig[0]), 1)

        with nc.Block() as block:
            @block.tensor
            def _(eng):
                run("pe", eng)

            @block.scalar
            def _(eng):
                run("act", eng)

            @block.vector
            def _(eng):
                run("dve", eng)

            @block.gpsimd
            def _(eng):
                run("pool", eng)

            @block.sync
            def _(eng):
                run("sp", eng)


class KB:
    def __init__(self):
        self.nc = bass.Bass("TRN2", target_bir_lowering=False)
        self.P = Prog(self.nc)
        self.ps = [self.nc.alloc_psum_tensor(f"psb{i}", [128, 512], F32) for i in range(8)]
        self.psr = [self.P.res(f"psb{i}") for i in range(8)]
        self.out_res = []
        self.uid = 0

    def inp(self, name, shape, dt=F32):
        return self.nc.dram_tensor(name, list(shape), dt, kind="ExternalInput").ap()

    def outp(self, name, shape, dt=F32):
        return self.nc.dram_tensor(name, list(shape), dt, kind="ExternalOutput").ap()

    def sb(self, es, name, shape, dt):
        self.uid += 1
        return es.enter_context(self.nc.sbuf_tensor(f"{name}_{self.uid}", list(shape), dt))

    def store(self, dram_ap, sb_ap, reads, key):
        r = self.P.res()
        self.P.dma("sp", "st_" + key, dram_ap, sb_ap, reads=reads, writes=[r])
        self.out_res.append(r)

    def finish(self):
        self.P.op("sp", lambda e: e.nop(), reads=self.out_res)
        self.P.emit()
        return self.nc

    def load_consts(self, es, cst_ap):
        P = self.P
        self.cst = self.sb(es, "cst", [128, 3, 128], BF16)
        self.r_cst = P.res("cst")
        P.dma("pool", "cst", self.cst[:], cst_ap, writes=[self.r_cst])
        self.ones = self.cst[:, 0, :]
        self.ident = self.cst[:, 1, :]
        self.swap = self.cst[:, 2, :]


class Tile:
    def __init__(self, x, xres, n, A=None, B=None, HG=None, modres=None):
        self.x = x
        self.xres = xres
        self.n = n
        self.A = A
        self.B = B
        self.HG = HG
        self.modres = modres
        self.off = 0


def emit_mod(kb, es, modw_ap, nch, cvec_ap, nv, modb_ap, name):
    P = kb.P
    nc = kb.nc
    assert nch % 4 == 0
    cv = kb.sb(es, name + "cv", [128, 8, nv], F32)
    sc = kb.sb(es, name + "sc", [128, 8, nv], BF16)
    mb = kb.sb(es, name + "mb", [128, nch], F32)
    mvec = kb.sb(es, name + "mvec", [128, nch, nv], F32)
    r_cv, r_sc, r_mb, r_mvec = P.res(), P.res(), P.res(), P.res()
    P.dma("sp", name + "cv", cv[:], cvec_ap, writes=[r_cv])
    P.dma("sp", name + "mb", mb[:], modb_ap, writes=[r_mb])
    P.op("act", lambda e: e.activation(sc[:], cv[:], AF.Silu), reads=[r_cv], writes=[r_sc])
    psm = kb.ps[7]
    r_psm = kb.psr[7]
    with ExitStack() as es2:
        mw = [kb.sb(es2, name + f"mw{s}", [128, 8, 512], BF16) for s in range(2)]
        r_mw = [P.res(), P.res()]
        src = modw_ap.rearrange("(kc p) n -> p kc n", p=128)
        for g in range(nch // 4):
            s = g % 2
            P.dma("pool", name + f"mw{s}", mw[s][:], src[:, :, g * 512:(g + 1) * 512], writes=[r_mw[s]])
            for c4 in range(4):
                oc = g * 4 + c4
                for kc in range(8):
                    P.op("pe", lambda e, s=s, c4=c4, oc=oc, kc=kc: e.matmul(
                        psm[:, oc * nv:(oc + 1) * nv], mw[s][:, kc, c4 * 128:(c4 + 1) * 128], sc[:, kc, :],
                        start=(kc == 0), stop=(kc == 7)), reads=[r_mw[s], r_sc], writes=[r_psm])
        for v in range(nv):
            P.op("dve", lambda e, v=v: e.tensor_tensor(
                mvec[:, :, v], psm[:, 0:nch * nv].rearrange("p (c v) -> p c v", v=nv)[:, :, v], mb[:], ALU.add),
                reads=[r_psm, r_mb], writes=[r_mvec])
        P.barrier()
    return mvec, r_mvec


def emit_ab(kb, es, mvec, r_mvec, v, i_shift, i_scale, g_ap, r_g, name, i_gate=None, gate_mul=1.0):
    P = kb.P
    A = kb.sb(es, name + "A", [128, 8], F32)
    r = P.res()
    P.op("dve", lambda e: e.scalar_tensor_tensor(
        A[:], mvec[:, i_scale * 8:(i_scale + 1) * 8, v], 1.0, g_ap, ALU.add, ALU.mult),
        reads=[r_mvec, r_g], writes=[r])
    Bm = mvec[:, i_shift * 8:(i_shift + 1) * 8, v]
    G = None
    if i_gate is not None:
        G = kb.sb(es, name + "G", [128, 8], F32)
        P.op("dve", lambda e: e.tensor_scalar_mul(G[:], mvec[:, i_gate * 8:(i_gate + 1) * 8, v], float(gate_mul)),
             reads=[r_mvec], writes=[r])
    return A, Bm, G, r


def emit_normmod(kb, scr, t, A, Bv, r_ab, out_fn, r_out):
    P = kb.P
    n = t.n
    ps_stat, r_ps = kb.ps[0], kb.psr[0]
    for kc in range(NKC):
        s = kc % 2
        P.op("act", lambda e, s=s, kc=kc: e.activation(scr["sq"][s][:, :n], t.x[kc], AF.Square),
             reads=[t.xres[kc]], writes=[scr["r_sq"][s]])
        P.op("pe", lambda e, s=s, kc=kc: e.matmul(ps_stat[:, :n], kb.ones, scr["sq"][s][:, :n],
                                                  start=(kc == 0), stop=(kc == NKC - 1)),
             reads=[scr["r_sq"][s], kb.r_cst], writes=[r_ps])
    rstd = scr["rstd"]
    P.op("act", lambda e: e.activation(rstd[:, :n], ps_stat[:, :n], AF.Sqrt, bias=scr["eps"][:, 0:1], scale=1.0 / D),
         reads=[r_ps, scr["r_eps"]], writes=[scr["r_rstd"]])
    P.op("dve", lambda e: e.reciprocal(rstd[:, :n], rstd[:, :n]), reads=[scr["r_rstd"]], writes=[scr["r_rstd"]])
    for kc in range(NKC):
        s = kc % 2
        P.op("dve", lambda e, s=s, kc=kc: e.tensor_tensor(scr["tmp"][s][:, :n], t.x[kc], rstd[:, :n], ALU.mult),
             reads=[t.xres[kc], scr["r_rstd"]], writes=[scr["r_tmp"][s]])
        P.op("act", lambda e, s=s, kc=kc: e.activation(out_fn(kc), scr["tmp"][s][:, :n], AF.Identity,
                                                       bias=Bv[:, kc:kc + 1], scale=A[:, kc:kc + 1]),
             reads=[scr["r_tmp"][s], r_ab], writes=[r_out])


def make_scratch(kb, es):
    P = kb.P
    scr = {
        "sq": [kb.sb(es, "sq", [128, 512], BF16) for _ in range(2)],
        "r_sq": [P.res(), P.res()],
        "rstd": kb.sb(es, "rstd", [128, 512], F32),
        "r_rstd": P.res(),
        "tmp": [kb.sb(es, "tmp", [128, 512], F32) for _ in range(2)],
        "r_tmp": [P.res(), P.res()],
        "eps": kb.sb(es, "eps", [128, 1], F32),
        "r_eps": P.res(),
    }
    P.op("dve", lambda e: e.memset(scr["eps"][:], EPS), writes=[scr["r_eps"]])
    return scr


def emit_ffn(kb, groups, wg_ap, wu_ap, wd_ap, xn_ext=None):
    P = kb.P
    with ExitStack() as es:
        maxw = max(sum(t.n for t in g) for g in groups)
        if xn_ext is None:
            xn_t = kb.sb(es, "ffn_xn", [128, NKC, maxw], BF16)
            xn = lambda kc, a, b: xn_t[:, kc, a:b]
        else:
            xn = xn_ext
        H = kb.sb(es, "ffn_H", [128, NFC, maxw], BF16)
        wgb = [kb.sb(es, "wg", [128, NKC, 256], BF16) for _ in range(2)]
        wub = [kb.sb(es, "wu", [128, NKC, 256], BF16) for _ in range(2)]
        wdb = [kb.sb(es, "wd", [128, NFC, 256], BF16) for _ in range(2)]
        sg = [kb.sb(es, "sg", [128, 512], F32) for _ in range(2)]
        r_wg, r_wu, r_wd, r_sg = ([P.res(), P.res()] for _ in range(4))
        scr = make_scratch(kb, es)
        wg_src = wg_ap.rearrange("(kc p) n -> p kc n", p=128)
        wu_src = wu_ap.rearrange("(kc p) n -> p kc n", p=128)
        wd_src = wd_ap.rearrange("(kc p) n -> p kc n", p=128)
        gu_banks = [(1, 2), (3, 4)]
        y_banks = [5, 6]
        cnt = 0
        ycnt = 0
        wcnt = 0
        dcnt = 0
        for g in groups:
            off = 0
            for t in g:
                t.off = off
                off += t.n
            r_xn = {id(t): P.res() for t in g}
            r_H = {(id(t), j): P.res() for t in g for j in range(NFC)}
            for t in g:
                emit_normmod(kb, scr, t, t.A, t.B, t.modres,
                             lambda kc, o=t.off, n=t.n: xn(kc, o, o + n), r_xn[id(t)])
            for j2 in range(NFC // 2):
                s = wcnt % 2
                wcnt += 1
                P.dma("pool", f"wg{s}", wgb[s][:], wg_src[:, :, j2 * 256:(j2 + 1) * 256], writes=[r_wg[s]])
                P.dma("pool", f"wu{s}", wub[s][:], wu_src[:, :, j2 * 256:(j2 + 1) * 256], writes=[r_wu[s]])
                for jj in range(2):
                    j = j2 * 2 + jj
                    for t in g:
                        n = t.n
                        bg_, bu_ = gu_banks[cnt % 2]
                        q = cnt % 2
                        cnt += 1
                        for kc in range(NKC):
                            P.op("pe", lambda e, s=s, jj=jj, kc=kc, o=t.off, n=n, bg_=bg_: e.matmul(
                                kb.ps[bg_][:, :n], wgb[s][:, kc, jj * 128:(jj + 1) * 128], xn(kc, o, o + n),
                                start=(kc == 0), stop=(kc == NKC - 1)),
                                reads=[r_wg[s], r_xn[id(t)]], writes=[kb.psr[bg_]])
                        for kc in range(NKC):
                            P.op("pe", lambda e, s=s, jj=jj, kc=kc, o=t.off, n=n, bu_=bu_: e.matmul(
                                kb.ps[bu_][:, :n], wub[s][:, kc, jj * 128:(jj + 1) * 128], xn(kc, o, o + n),
                                start=(kc == 0), stop=(kc == NKC - 1)),
                                reads=[r_wu[s], r_xn[id(t)]], writes=[kb.psr[bu_]])
                        P.op("act", lambda e, q=q, n=n, bg_=bg_: e.activation(sg[q][:, :n], kb.ps[bg_][:, :n], AF.Silu),
                             reads=[kb.psr[bg_]], writes=[r_sg[q]])
                        P.op("dve", lambda e, q=q, n=n, bu_=bu_, j=j, o=t.off: e.tensor_tensor(
                            H[:, j, o:o + n], sg[q][:, :n], kb.ps[bu_][:, :n], ALU.mult),
                            reads=[r_sg[q], kb.psr[bu_]], writes=[r_H[(id(t), j)]])
            for i4 in range(4):
                s = dcnt % 2
                dcnt += 1
                P.dma("pool", f"wd{s}", wdb[s][:], wd_src[:, :, i4 * 256:(i4 + 1) * 256], writes=[r_wd[s]])
                for ii in range(2):
                    i = i4 * 2 + ii
                    for t in g:
                        n = t.n
                        by = y_banks[ycnt % 2]
                        ycnt += 1
                        for kc in range(NFC):
                            P.op("pe", lambda e, s=s, ii=ii, kc=kc, o=t.off, n=n, by=by: e.matmul(
                                kb.ps[by][:, :n], wdb[s][:, kc, ii * 128:(ii + 1) * 128], H[:, kc, o:o + n],
                                start=(kc == 0), stop=(kc == NFC - 1)),
                                reads=[r_wd[s], r_H[(id(t), kc)]], writes=[kb.psr[by]])
                        P.op("dve", lambda e, i=i, t=t, n=n, by=by, hg=t.HG: e.scalar_tensor_tensor(
                            t.x[i], kb.ps[by][:, :n], hg[:, i:i + 1], t.x[i], ALU.mult, ALU.add),
                            reads=[kb.psr[by], t.xres[i], t.modres], writes=[t.xres[i]])
        P.barrier()


def make_tiles(x_t, P, ntok, width=512):
    tiles = []
    for a in range(0, ntok, width):
        n = min(width, ntok - a)
        tiles.append(Tile([x_t[:, kc, a:a + n] for kc in range(NKC)], [P.res() for _ in range(NKC)], n))
    return tiles


def build_A():
    kb = KB()
    P = kb.P
    xT = kb.inp("xT", [NKC, 128, TPC])
    cxT = kb.inp("cxT", [NKC, 128, 256])
    cvec = kb.inp("cvec", [128, 8, 2])
    modw = kb.inp("modw", [D, 5 * D])
    modb = kb.inp("modb", [128, 40])
    normg = kb.inp("normg", [128, 2, 8])
    wg = kb.inp("wg", [D, DFF])
    wu = kb.inp("wu", [D, DFF])
    wd = kb.inp("wd", [DFF, D])
    w_in = kb.inp("w_in", [D, 3 * D])
    cst = kb.inp("cst", [128, 3, 128])
    rope = kb.inp("rope", [128, 2, TPC])
    x1T = kb.outp("x1T", [NKC, 128, TPC])
    if DEBUG_STOP in (0, 3, 4, 5, 6, 7):
        qkT = kb.outp("qkT", [16, 128, TPC], BF16)
        vtm = kb.outp("vtm", [16, 128, 1024], BF16)
        kcT = kb.outp("kcT", [8, 128, 256], BF16)
        vc = kb.outp("vc", [2, 128, 1024], BF16)

    with ExitStack() as es:
        kb.load_consts(es, cst)
        x_t = kb.sb(es, "x", [128, NKC, TPC], F32)
        xc_t = kb.sb(es, "xc", [128, NKC, 256], F32)
        ng = kb.sb(es, "ng", [128, 2, 8], F32)
        r_ng = P.res()
        P.dma("sp", "ng", ng[:], normg, writes=[r_ng])
        lat = make_tiles(x_t, P, TPC)
        ctx = make_tiles(xc_t, P, 256)
        for kc in range(NKC):
            for t in lat:
                pass
            P.dma("sp", f"xin{kc}", x_t[:, kc, :], xT[kc], writes=[t.xres[kc] for t in lat])
            P.dma("sp", f"xcin{kc}", xc_t[:, kc, :], cxT[kc], writes=[ctx[0].xres[kc]])
        if DEBUG_STOP == -1:
            for kc in range(NKC):
                kb.store(x1T[kc], x_t[:, kc, :], [t.xres[kc] for t in lat], f"x{kc}")
            kb.finish()
            return kb.nc
        mvec, r_mvec = emit_mod(kb, es, modw, 40, cvec, 2, modb, "m0")
        if DEBUG_STOP == -2:
            for kc in range(NKC):
                kb.store(x1T[kc], x_t[:, kc, :], [t.xres[kc] for t in lat], f"x{kc}")
            kb.finish()
            return kb.nc
        ab = {}
        for v in range(2):
            A1, B1, G1, r1 = emit_ab(kb, es, mvec, r_mvec, v, 0, 1, ng[:, 0, :], r_ng, f"f1v{v}", i_gate=2, gate_mul=0.5)
            A2, B2, _, r2 = emit_ab(kb, es, mvec, r_mvec, v, 3, 4, ng[:, 1, :], r_ng, f"qkv{v}")
            ab[v] = (A1, B1, G1, r1, A2, B2, r2)
        for t in lat:
            t.A, t.B, t.HG, t.modres = ab[0][0], ab[0][1], ab[0][2], ab[0][3]
        for t in ctx:
            t.A, t.B, t.HG, t.modres = ab[1][0], ab[1][1], ab[1][2], ab[1][3]
        if DEBUG_STOP != 1:
            emit_ffn(kb, [lat[0:2] + ctx, lat[2:4]], wg, wu, wd)
        for kc in range(NKC):
            kb.store(x1T[kc], x_t[:, kc, :], [t.xres[kc] for t in lat], f"x{kc}")

        if DEBUG_STOP in (1, 2):
            kb.finish()
            return kb.nc
        with ExitStack() as es2:
            wi = kb.sb(es2, "wi", [128, NKC, 3 * D], BF16)
            r_wi = [P.res() for _ in range(6)]
            wi_src = w_in.rearrange("(kc p) n -> p kc n", p=128)
            for g6 in range(6):
                P.dma("pool", f"wi{g6}", wi[:, :, g6 * 512:(g6 + 1) * 512], wi_src[:, :, g6 * 512:(g6 + 1) * 512],
                      writes=[r_wi[g6]])
            rp = kb.sb(es2, "rope", [128, 2, TPC], F32)
            r_rp = P.res()
            P.dma("sp", "rope", rp[:], rope, writes=[r_rp])
            scr = make_scratch(kb, es2)
            xn2 = [kb.sb(es2, "xn2", [128, NKC, 512], BF16) for _ in range(2)]
            r_xn2 = [P.res(), P.res()]
            stg = [kb.sb(es2, "stg", [128, 512], BF16) for _ in range(4)]
            r_stg = [P.res() for _ in range(4)]
            qs = [kb.sb(es2, "qs", [128, 512], BF16) for _ in range(2)]
            r_qs = [P.res(), P.res()]
            t1 = [kb.sb(es2, "t1", [128, 512], F32) for _ in range(2)]
            r_t1 = [P.res(), P.res()]
            t2 = [kb.sb(es2, "t2", [128, 512], F32) for _ in range(2)]
            r_t2 = [P.res(), P.res()]
            banks = [1, 2, 3, 4]
            bc = 0
            sc_ = 0
            rc = 0
            for ti, t in enumerate(lat + ctx):
                is_ctx = ti >= len(lat)
                v = 1 if is_ctx else 0
                n = t.n
                tok0 = 0 if is_ctx else ti * 512
                xb = ti % 2
                emit_normmod(kb, scr, t, ab[v][4], ab[v][5], ab[v][6],
                             lambda kc, xb=xb, n=n: xn2[xb][:, kc, :n], r_xn2[xb])
                fm = [(c, c) for c in range(0, 8)] + [(12 + c, 8 + c) for c in range(8)]
                for (wc, oc) in fm:
                    kind = oc // 4
                    if is_ctx and kind in (0, 2):
                        continue
                    b = banks[bc % 4]
                    bc += 1
                    for kc in range(NKC):
                        P.op("pe", lambda e, b=b, wc=wc, kc=kc, xb=xb, n=n: e.matmul(
                            kb.ps[b][:, :n], wi[:, kc, wc * 128:(wc + 1) * 128], xn2[xb][:, kc, :n],
                            start=(kc == 0), stop=(kc == NKC - 1)),
                            reads=[r_wi[wc // 4], r_xn2[xb]], writes=[kb.psr[b]])
                    s = sc_ % 4
                    sc_ += 1
                    if kind >= 2 and not is_ctx and DEBUG_STOP not in (4, 5):
                        r = rc % 2
                        rc += 1
                        if DEBUG_STOP != 7:
                            b2 = banks[bc % 4]
                            bc += 1
                            P.op("act", lambda e, r=r, b=b, n=n: e.activation(qs[r][:, :n], kb.ps[b][:, :n], AF.Identity),
                                 reads=[kb.psr[b]], writes=[r_qs[r]])
                            P.op("pe", lambda e, r=r, b2=b2, n=n: e.matmul(kb.ps[b2][:, :n], kb.swap, qs[r][:, :n],
                                                                           start=True, stop=True),
                                 reads=[r_qs[r], kb.r_cst], writes=[kb.psr[b2]])
                        else:
                            b2 = b
                        if DEBUG_STOP != 6:
                            P.op("dve", lambda e, r=r, b=b, n=n, tok0=tok0: e.tensor_tensor(
                                t1[r][:, :n], kb.ps[b][:, :n], rp[:, 0, tok0:tok0 + n], ALU.mult),
                                reads=[kb.psr[b], r_rp] + ([r_qs[r]] if DEBUG_STOP != 7 else []), writes=[r_t1[r]])
                            P.op("dve", lambda e, r=r, b2=b2, n=n, tok0=tok0: e.tensor_tensor(
                                t2[r][:, :n], kb.ps[b2][:, :n], rp[:, 1, tok0:tok0 + n], ALU.mult),
                                reads=[kb.psr[b2], r_rp], writes=[r_t2[r]])
                            P.op("dve", lambda e, r=r, s=s, n=n: e.tensor_tensor(
                                stg[s][:, :n], t1[r][:, :n], t2[r][:, :n], ALU.add),
                                reads=[r_t1[r], r_t2[r]], writes=[r_stg[s]])
                        else:
                            P.op("act", lambda e, s=s, b2=b2, n=n: e.activation(
                                stg[s][:, :n], kb.ps[b2][:, :n], AF.Identity),
                                reads=[kb.psr[b2]], writes=[r_stg[s]])
                    else:
                        scale = 0.125 if kind == 0 else 1.0
                        P.op("act", lambda e, s=s, b=b, n=n, scale=scale: e.activation(
                            stg[s][:, :n], kb.ps[b][:, :n], AF.Identity, scale=scale),
                            reads=[kb.psr[b]], writes=[r_stg[s]])
                    if is_ctx:
                        dst = kcT[(oc - 4) if kind == 1 else (oc - 8)]
                        kb.store(dst, stg[s][:, :n], [r_stg[s]], f"stg{s}")
                    else:
                        kb.store(qkT[oc][:, tok0:tok0 + n], stg[s][:, :n], [r_stg[s]], f"stg{s}")
                for tb in range(0 if DEBUG_STOP in (3, 5, 6, 7) else n // 128):
                    for vi, wc0 in enumerate((8, 20)):
                        b = banks[bc % 4]
                        bc += 1
                        for kc in range(NKC):
                            P.op("pe", lambda e, b=b, wc0=wc0, kc=kc, xb=xb, tb=tb: e.matmul(
                                kb.ps[b][:, :], xn2[xb][:, kc, tb * 128:(tb + 1) * 128],
                                wi[:, kc, wc0 * 128:(wc0 + 4) * 128],
                                start=(kc == 0), stop=(kc == NKC - 1)),
                                reads=[r_wi[wc0 // 4], r_xn2[xb]], writes=[kb.psr[b]])
                        s = sc_ % 4
                        sc_ += 1
                        if vi == 0:
                            P.op("act", lambda e, s=s, b=b: e.activation(stg[s][:], kb.ps[b][:], AF.Identity),
                                 reads=[kb.psr[b]], writes=[r_stg[s]])
                        else:
                            P.op("dve", lambda e, s=s, b=b: e.tensor_copy(stg[s][:], kb.ps[b][:]),
                                 reads=[kb.psr[b]], writes=[r_stg[s]])
                        if is_ctx:
                            kb.store(vc[tb][:, vi * 512:(vi + 1) * 512], stg[s][:], [r_stg[s]], f"stg{s}")
                        else:
                            kb.store(vtm[ti * 4 + tb][:, vi * 512:(vi + 1) * 512], stg[s][:], [r_stg[s]], f"stg{s}")
            P.barrier()
        kb.finish()
    return kb.nc


def build_B():
    kb = KB()
    P = kb.P
    qaT = kb.inp("qaT", [4, 128, TPC], BF16)
    qbT = kb.inp("qbT", [4, 128, TPC], BF16)
    kaB = kb.inp("kaB", [4, 128, 2816], BF16)
    vaP = kb.inp("vaP", [4, 128, 22 * 256], BF16)
    kbT = kb.inp("kbT", [4, 128, 8448], BF16)
    vbT = kb.inp("vbT", [4, 128, 66 * 128], BF16)
    TT = kb.inp("TT", [128, 8, 22 * 64], BF16)
    Rv = kb.inp("Rv", [4, 2, 8 * 512], BF16)
    E2 = kb.inp("E2", [2, 128], BF16)
    onesP = kb.inp("onesP", [128, 2, 128], BF16)
    x1T = kb.inp("x1T", [NKC, 128, TPC])
    cst = kb.inp("cst", [128, 3, 128])
    lamv = kb.inp("lamv", [128, 4, 64])
    sublng = kb.inp("sublng", [128, 1])
    cvec = kb.inp("cvec", [128, 8, 1])
    modwA = kb.inp("modwA", [D, 4 * D])
    modbA = kb.inp("modbA", [128, 32])
    modwB = kb.inp("modwB", [D, 4 * D])
    modbB = kb.inp("modbB", [128, 32])
    normg = kb.inp("normg", [128, 2, 8])
    w_out = kb.inp("w_out", [D, D])
    wg1 = kb.inp("wg1", [D, DFF])
    wu1 = kb.inp("wu1", [D, DFF])
    wd1 = kb.inp("wd1", [DFF, D])
    wg2 = kb.inp("wg2", [D, DFF])
    wu2 = kb.inp("wu2", [D, DFF])
    wd2 = kb.inp("wd2", [DFF, D])
    x4T = kb.outp("x4T", [NKC, 128, TPC])

    with ExitStack() as es:
        kb.load_consts(es, cst)
        an = kb.sb(es, "an", [128, NKC, TPC], BF16)
        r_an = [[P.res() for _ in range(4)] for _ in range(NKC)]
        ng = kb.sb(es, "ng", [128, 2, 8], F32)
        r_ng = P.res()
        P.dma("sp", "ng", ng[:], normg, writes=[r_ng])
        mvA, r_mvA = emit_mod(kb, es, modwA, 32, cvec, 1, modbA, "mA")
        mvB, r_mvB = emit_mod(kb, es, modwB, 32, cvec, 1, modbB, "mB")
        lam_t = kb.sb(es, "lam", [128, 4, 64], F32)
        lsc = kb.sb(es, "lsc", [128, 8], F32)
        sg8 = kb.sb(es, "sg8", [128, 1], F32)
        r_lam, r_lsc, r_sg8 = P.res(), P.res(), P.res()
        P.dma("sp", "lam", lam_t[:], lamv, writes=[r_lam])
        P.dma("sp", "sg8", sg8[:], sublng, writes=[r_sg8])
        P.op("dve", lambda e: e.tensor_tensor(lam_t[:, 0, :], lam_t[:, 0, :], lam_t[:, 1, :], ALU.mult),
             reads=[r_lam], writes=[r_lam])
        P.op("dve", lambda e: e.tensor_tensor(lam_t[:, 2, :], lam_t[:, 2, :], lam_t[:, 3, :], ALU.mult),
             reads=[r_lam], writes=[r_lam])
        P.op("dve", lambda e: e.reduce_sum(lsc[:, 0:1], lam_t[:, 0, :], mybir.AxisListType.X), reads=[r_lam], writes=[r_lsc])
        P.op("dve", lambda e: e.reduce_sum(lsc[:, 1:2], lam_t[:, 2, :], mybir.AxisListType.X), reads=[r_lam], writes=[r_lsc])
        P.op("act", lambda e: e.activation(lsc[:, 2:4], lsc[:, 0:2], AF.Exp), reads=[r_lsc], writes=[r_lsc])
        LAM_INIT = 0.8 - 0.6 * math.exp(-0.3 * 0)
        P.op("dve", lambda e: e.tensor_tensor(lsc[:, 4:5], lsc[:, 3:4], lsc[:, 2:3], ALU.subtract), reads=[r_lsc], writes=[r_lsc])
        P.op("dve", lambda e: e.tensor_scalar_add(lsc[:, 5:6], lsc[:, 4:5], -LAM_INIT), reads=[r_lsc], writes=[r_lsc])
        P.op("dve", lambda e: e.tensor_scalar_mul(sg8[:], sg8[:], 1.0 - LAM_INIT), reads=[r_sg8], writes=[r_sg8])
        neglam = lsc[:, 5:6]

        with ExitStack() as es2:
            tt = kb.sb(es2, "tt", [128, 8, 22 * 64], BF16)
            r_tt = P.res()
            P.dma("sp", "tt", tt[:], TT, writes=[r_tt])
            e2 = kb.sb(es2, "e2", [2, 128], BF16)
            r_e2 = P.res()
            P.dma("sp", "e2", e2[:], E2, writes=[r_e2])
            op_ = kb.sb(es2, "onesP", [128, 2, 128], BF16)
            r_op = P.res()
            P.dma("sp", "onesP", op_[:], onesP, writes=[r_op])
            rv = [kb.sb(es2, "rv", [2, 8 * 512], BF16) for _ in range(2)]
            r_rv = [P.res(), P.res()]
            qsb = [kb.sb(es2, "qsb", [128, 512], BF16) for _ in range(2)]
            r_q = [P.res(), P.res()]
            kA = [kb.sb(es2, "kA", [128, 1280], BF16) for _ in range(2)]
            r_kA = [P.res(), P.res()]
            vP = [kb.sb(es2, "vP", [128, 10 * 256], BF16) for _ in range(2)]
            r_vP = [P.res(), P.res()]
            kB_ = [kb.sb(es2, "kB", [128, 8448], BF16) for _ in range(2)]
            r_kB = [P.res(), P.res()]
            vB_ = [kb.sb(es2, "vB", [128, 66 * 128], BF16) for _ in range(2)]
            r_vB = [P.res(), P.res()]
            pT = [kb.sb(es2, "pT", [128, 512], BF16) for _ in range(4)]
            r_pT = [P.res() for _ in range(4)]
            ev = [kb.sb(es2, "ev", [128, 512], F32) for _ in range(4)]
            r_ev = [P.res() for _ in range(4)]
            sqb = kb.sb(es2, "sqb", [128, 512], BF16)
            r_sqb = P.res()
            epst = kb.sb(es2, "epst", [128, 1], F32)
            r_epst = P.res()
            P.op("dve", lambda e: e.memset(epst[:], EPS), writes=[r_epst])
            sb_i = 0
            p_i = 0
            qcnt = 0
            acnt = 0
            for qt in range(4):
                rs = qt % 2
                P.dma("sp", f"rv{rs}", rv[rs][:], Rv[qt], writes=[r_rv[rs]])
                for g in range(4):
                    a = acnt % 2
                    acnt += 1
                    P.dma("sp", f"kA{a}", kA[a][:, 0:1024], kaB[g][:, qt * 512:qt * 512 + 1024], writes=[r_kA[a]])
                    P.dma("sp", f"kA{a}", kA[a][:, 1024:1280], kaB[g][:, 2560:2816], writes=[r_kA[a]])
                    P.dma("sp", f"vP{a}", vP[a][:, 0:8 * 256], vaP[g][:, qt * 4 * 256:(qt * 4 + 8) * 256], writes=[r_vP[a]])
                    P.dma("sp", f"vP{a}", vP[a][:, 8 * 256:10 * 256], vaP[g][:, 20 * 256:22 * 256], writes=[r_vP[a]])
                    qq = qcnt % 2
                    qcnt += 1
                    P.dma("sp", f"q{qq}", qsb[qq][:], qaT[g][:, qt * 512:(qt + 1) * 512], writes=[r_q[qq]])
                    nO, nL = 4, 5
                    its = [(jt, hh) for jt in range(10) for hh in range(2)]
                    NIT = len(its)
                    LA = 2
                    stash = {}
                    for step in range(NIT + LA):
                        if step < NIT:
                            jt, hh = its[step]
                            head = 2 * g + hh
                            rows = slice(hh * 64, (hh + 1) * 64)
                            b = sb_i % 4
                            sb_i += 1
                            band = jt < 8
                            P.op("pe", lambda e, b=b, a=a, rows=rows, jt=jt, qq=qq, band=band: e.matmul(
                                kb.ps[b][:], kA[a][rows, jt * 128:(jt + 1) * 128], qsb[qq][rows, :],
                                start=True, stop=(not band)),
                                reads=[r_kA[a], r_q[qq]], writes=[kb.psr[b]])
                            if band:
                                s_j = 14 - 2 * jt
                                P.op("pe", lambda e, b=b, head=head, s_j=s_j: e.matmul(
                                    kb.ps[b][:], kb.ident, tt[:, head, s_j * 64:(s_j + 8) * 64], start=False, stop=False),
                                    reads=[r_tt, kb.r_cst], writes=[kb.psr[b]])
                                P.op("pe", lambda e, b=b, rs=rs, jt=jt: e.matmul(
                                    kb.ps[b][:], e2[:, :], rv[rs][:, jt * 512:(jt + 1) * 512], start=False, stop=True),
                                    reads=[r_e2, r_rv[rs]], writes=[kb.psr[b]])
                            pi = p_i % 4
                            p_i += 1
                            P.op("act", lambda e, pi=pi, b=b: e.activation(pT[pi][:], kb.ps[b][:], AF.Exp),
                                 reads=[kb.psr[b]], writes=[r_pT[pi]])
                            stash[step] = pi
                        if step >= LA:
                            jt, hh = its[step - LA]
                            pi = stash.pop(step - LA)
                            first = (step - LA == 0)
                            last = (step - LA == NIT - 1)
                            P.op("pe", lambda e, pi=pi, a=a, jt=jt, hh=hh, first=first, last=last: e.matmul(
                                kb.ps[nO][:], vP[a][:, (jt * 2 + hh) * 128:(jt * 2 + hh + 1) * 128], pT[pi][:], start=first, stop=last),
                                reads=[r_vP[a], r_pT[pi]], writes=[kb.psr[nO]])
                            P.op("pe", lambda e, pi=pi, hh=hh, first=first, last=last: e.matmul(
                                kb.ps[nL][:], op_[:, hh, :], pT[pi][:], start=first, stop=last),
                                reads=[r_op, r_pT[pi]], writes=[kb.psr[nL]])
                    P.op("dve", lambda e: e.reciprocal(ev[0][:], kb.ps[nL][:]), reads=[kb.psr[nL]], writes=[r_ev[0]])
                    P.op("dve", lambda e, g=g, qt=qt: e.tensor_tensor(
                        an[:, g, qt * 512:(qt + 1) * 512], kb.ps[nO][:], ev[0][:], ALU.mult),
                        reads=[kb.psr[nO], r_ev[0]], writes=[r_an[g][qt]])
            for h in range(4):
                a = h % 2
                P.dma("sp", f"kB{a}", kB_[a][:], kbT[h], writes=[r_kB[a]])
                P.dma("sp", f"vB{a}", vB_[a][:], vbT[h], writes=[r_vB[a]])
                for qt in range(4):
                    qq = qcnt % 2
                    qcnt += 1
                    P.dma("sp", f"q{qq}", qsb[qq][:], qbT[h][:, qt * 512:(qt + 1) * 512], writes=[r_q[qq]])
                    bO = [4, 5]
                    bL = [6, 7]
                    its = [(kt, m) for kt in range(66) for m in range(2)]
                    NIT = len(its)
                    LA = 2
                    stash = {}
                    for step in range(NIT + LA):
                        if step < NIT:
                            kt, m = its[step]
                            rows = slice(m * 64, (m + 1) * 64)
                            b = sb_i % 4
                            sb_i += 1
                            P.op("pe", lambda e, b=b, a=a, rows=rows, kt=kt, qq=qq: e.matmul(
                                kb.ps[b][:], kB_[a][rows, kt * 128:(kt + 1) * 128], qsb[qq][rows, :],
                                start=True, stop=True),
                                reads=[r_kB[a], r_q[qq]], writes=[kb.psr[b]])
                            pi = p_i % 4
                            p_i += 1
                            P.op("act", lambda e, pi=pi, b=b: e.activation(pT[pi][:], kb.ps[b][:], AF.Exp, scale=0.125),
                                 reads=[kb.psr[b]], writes=[r_pT[pi]])
                            stash[step] = pi
                        if step >= LA:
                            kt, m = its[step - LA]
                            pi = stash.pop(step - LA)
                            P.op("pe", lambda e, pi=pi, a=a, kt=kt, m=m: e.matmul(
                                kb.ps[bO[m]][:], vB_[a][:, kt * 128:(kt + 1) * 128], pT[pi][:], start=(kt == 0), stop=(kt == 65)),
                                reads=[r_vB[a], r_pT[pi]], writes=[kb.psr[bO[m]]])
                            P.op("pe", lambda e, pi=pi, kt=kt, m=m: e.matmul(
                                kb.ps[bL[m]][:], kb.ones, pT[pi][:], start=(kt == 0), stop=(kt == 65)),
                                reads=[kb.r_cst, r_pT[pi]], writes=[kb.psr[bL[m]]])
                    for m in range(2):
                        P.op("dve", lambda e, m=m: e.reciprocal(ev[m][:], kb.ps[bL[m]][:]),
                             reads=[kb.psr[bL[m]]], writes=[r_ev[m]])
                        P.op("dve", lambda e, m=m: e.tensor_tensor(ev[m][:], kb.ps[bO[m]][:], ev[m][:], ALU.mult),
                             reads=[kb.psr[bO[m]], r_ev[m]], writes=[r_ev[m]])
                    P.op("dve", lambda e: e.scalar_tensor_tensor(ev[2][:], ev[1][:], neglam, ev[0][:], ALU.mult, ALU.add),
                         reads=[r_ev[0], r_ev[1], r_lsc], writes=[r_ev[2]])
                    P.op("act", lambda e: e.activation(sqb[:], ev[2][:], AF.Square), reads=[r_ev[2]], writes=[r_sqb])
                    b = sb_i % 4
                    sb_i += 1
                    P.op("pe", lambda e, b=b: e.matmul(kb.ps[b][:], kb.ones, sqb[:], start=True, stop=True),
                         reads=[r_sqb, kb.r_cst], writes=[kb.psr[b]])
                    P.op("act", lambda e, b=b: e.activation(ev[3][:], kb.ps[b][:], AF.Sqrt, bias=epst[:, 0:1], scale=1.0 / 128),
                         reads=[kb.psr[b], r_epst], writes=[r_ev[3]])
                    P.op("dve", lambda e: e.reciprocal(ev[3][:], ev[3][:]), reads=[r_ev[3]], writes=[r_ev[3]])
                    P.op("dve", lambda e: e.tensor_tensor(ev[2][:], ev[2][:], ev[3][:], ALU.mult),
                         reads=[r_ev[2], r_ev[3]], writes=[r_ev[2]])
                    P.op("act", lambda e, h=h, qt=qt: e.activation(
                        an[:, 4 + h, qt * 512:(qt + 1) * 512], ev[2][:], AF.Identity, scale=sg8[:, 0:1]),
                        reads=[r_ev[2], r_sg8], writes=[r_an[4 + h][qt]])
            P.barrier()

        with ExitStack() as es3:
            x_t = kb.sb(es3, "x", [128, NKC, TPC], F32)
            lat = make_tiles(x_t, P, TPC)
            for kc in range(NKC):
                P.dma("sp", f"xin{kc}", x_t[:, kc, :], x1T[kc], writes=[t.xres[kc] for t in lat])
            G5 = mvA[:, 0:8, 0]
            with ExitStack() as es4:
                wo = kb.sb(es4, "wo", [128, NKC, D], BF16)
                r_wo = P.res()
                P.dma("pool", "wo", wo[:], w_out.rearrange("(kc p) n -> p kc n", p=128), writes=[r_wo])
                bc = 0
                for ti, t in enumerate(lat):
                    for oc in range(NKC):
                        b = 1 + bc % 4
                        bc += 1
                        for kc in range(NKC):
                            P.op("pe", lambda e, b=b, oc=oc, kc=kc, ti=ti: e.matmul(
                                kb.ps[b][:], wo[:, kc, oc * 128:(oc + 1) * 128], an[:, kc, ti * 512:(ti + 1) * 512],
                                start=(kc == 0), stop=(kc == NKC - 1)),
                                reads=[r_wo, r_an[kc][ti]], writes=[kb.psr[b]])
                        P.op("dve", lambda e, b=b, oc=oc, t=t: e.scalar_tensor_tensor(
                            t.x[oc], kb.ps[b][:], G5[:, oc:oc + 1], t.x[oc], ALU.mult, ALU.add),
                            reads=[kb.psr[b], t.xres[oc], r_mvA], writes=[t.xres[oc]])
                P.barrier()
            A1, B1, G1, r1 = emit_ab(kb, es3, mvA, r_mvA, 0, 1, 2, ng[:, 0, :], r_ng, "f2", i_gate=3, gate_mul=0.5)
            for t in lat:
                t.A, t.B, t.HG, t.modres = A1, B1, G1, r1
            xn_ext = lambda kc, a, b: an[:, kc, a:b]
            emit_ffn(kb, [lat[0:2], lat[2:4]], wg1, wu1, wd1, xn_ext=xn_ext)
            A2, B2, G2, r2 = emit_ab(kb, es3, mvB, r_mvB, 0, 0, 1, ng[:, 1, :], r_ng, "f3", i_gate=2, gate_mul=0.5)
            for t in lat:
                t.A, t.B, t.HG, t.modres = A2, B2, G2, r2
            emit_ffn(kb, [lat[0:2], lat[2:4]], wg2, wu2, wd2, xn_ext=xn_ext)
            for kc in range(NKC):
                kb.store(x4T[kc], x_t[:, kc, :], [t.xres[kc] for t in lat], f"x{kc}")
            kb.finish()
    return kb.nc


def build_C():
    kb = KB()
    P = kb.P
    x4T = kb.inp("x4T", [NKC, 128, TPC])
    xhT = kb.inp("xhT", [128, NKC, 2])
    hmask = kb.inp("hmask", [128, 2])
    cvec = kb.inp("cvec", [128, 8, 1])
    modw = kb.inp("modw", [D, 6 * D])
    modb = kb.inp("modb", [128, 48])
    normg = kb.inp("normg", [128, 3, 8])
    cw = kb.inp("cw", [128, 3, 8])
    cw_in = kb.inp("cw_in", [D, 3 * D])
    cw_out = kb.inp("cw_out", [D, D])
    wg = kb.inp("wg", [D, DFF])
    wu = kb.inp("wu", [D, DFF])
    wd = kb.inp("wd", [DFF, D])
    cst = kb.inp("cst", [128, 3, 128])
    outT = kb.outp("outT", [NKC, 128, TPC])

    with ExitStack() as es:
        kb.load_consts(es, cst)
        x_t = kb.sb(es, "x", [128, NKC, TPC], F32)
        xh_t = kb.sb(es, "xh", [128, NKC, 2], F32)
        hm = kb.sb(es, "hm", [128, 2], F32)
        ng = kb.sb(es, "ng", [128, 3, 8], F32)
        cwt = kb.sb(es, "cwt", [128, 3, 8], F32)
        r_ng, r_hm, r_cw = P.res(), P.res(), P.res()
        P.dma("sp", "ng", ng[:], normg, writes=[r_ng])
        P.dma("sp", "hm", hm[:], hmask, writes=[r_hm])
        P.dma("sp", "cw", cwt[:], cw, writes=[r_cw])
        lat = make_tiles(x_t, P, TPC)
        halo = make_tiles(xh_t, P, 2)
        for kc in range(NKC):
            P.dma("sp", f"xin{kc}", x_t[:, kc, :], x4T[kc], writes=[t.xres[kc] for t in lat])
        P.dma("sp", "xh", xh_t[:], xhT, writes=halo[0].xres)
        mv, r_mv = emit_mod(kb, es, modw, 48, cvec, 1, modb, "mC")
        Am, Bm, _, rm = emit_ab(kb, es, mv, r_mv, 0, 0, 1, ng[:, 0, :], r_ng, "cm")
        G5 = mv[:, 16:24, 0]
        with ExitStack() as es2:
            U = kb.sb(es2, "U", [128, NKC, TPC + 2], F32)
            r_U = [[P.res() for _ in range(6)] for _ in range(NKC)]
            scr = make_scratch(kb, es2)
            xn = [kb.sb(es2, "xnc", [128, NKC, 512], BF16)] * 2
            r_xn = [P.res()] * 2
            tmpc = [kb.sb(es2, "tmpc", [128, 512], F32) for _ in range(2)]
            r_tmpc = [P.res(), P.res()]
            wi_src = cw_in.rearrange("(kc p) n -> p kc n", p=128)
            with ExitStack() as es3:
                wch = kb.sb(es3, "wch", [128, NKC, 2 * D], BF16)
                r_wch = [P.res() for _ in range(4)]
                for g4 in range(4):
                    P.dma("pool", f"wch{g4}", wch[:, :, g4 * 512:(g4 + 1) * 512],
                          wi_src[:, :, D + g4 * 512:D + (g4 + 1) * 512], writes=[r_wch[g4]])
                bc = 0
                for ti, t in enumerate(lat + halo):
                    is_h = ti >= 4
                    n = t.n
                    xb = ti % 2
                    emit_normmod(kb, scr, t, Am, Bm, rm, lambda kc, xb=xb, n=n: xn[xb][:, kc, :n], r_xn[xb])
                    for c in range(NKC):
                        b1 = 1 + bc % 4
                        b2 = 1 + (bc + 1) % 4
                        bc += 2
                        for (bb, wc) in ((b1, c), (b2, 8 + c)):
                            for kc in range(NKC):
                                P.op("pe", lambda e, bb=bb, wc=wc, kc=kc, xb=xb, n=n: e.matmul(
                                    kb.ps[bb][:, :n], wch[:, kc, wc * 128:(wc + 1) * 128], xn[xb][:, kc, :n],
                                    start=(kc == 0), stop=(kc == NKC - 1)),
                                    reads=[r_wch[wc // 4], r_xn[xb]], writes=[kb.psr[bb]])
                        q = c % 2
                        P.op("act", lambda e, q=q, b1=b1, n=n: e.activation(tmpc[q][:, :n], kb.ps[b1][:, :n], AF.Identity),
                             reads=[kb.psr[b1]], writes=[r_tmpc[q]])
                        if not is_h:
                            P.op("dve", lambda e, q=q, b2=b2, c=c, ti=ti: e.tensor_tensor(
                                U[:, c, 1 + ti * 512:1 + (ti + 1) * 512], tmpc[q][:], kb.ps[b2][:], ALU.mult),
                                reads=[r_tmpc[q], kb.psr[b2]], writes=[r_U[c][1 + ti]])
                        else:
                            P.op("dve", lambda e, q=q, b2=b2: e.tensor_tensor(
                                tmpc[q][:, 0:2], tmpc[q][:, 0:2], kb.ps[b2][:, 0:2], ALU.mult),
                                reads=[r_tmpc[q], kb.psr[b2]], writes=[r_tmpc[q]])
                            P.op("dve", lambda e, q=q, c=c: e.tensor_tensor(
                                U[:, c, 0:1], tmpc[q][:, 0:1], hm[:, 0:1], ALU.mult),
                                reads=[r_tmpc[q], r_hm], writes=[r_U[c][0]])
                            P.op("dve", lambda e, q=q, c=c: e.tensor_tensor(
                                U[:, c, TPC + 1:TPC + 2], tmpc[q][:, 1:2], hm[:, 1:2], ALU.mult),
                                reads=[r_tmpc[q], r_hm], writes=[r_U[c][5]])
                P.barrier()
            with ExitStack() as es3:
                wbg = kb.sb(es3, "wbg", [128, NKC, D], BF16)
                wo = kb.sb(es3, "wo", [128, NKC, D], BF16)
                r_wbg, r_wo = P.res(), P.res()
                P.dma("pool", "wbg", wbg[:], wi_src[:, :, 0:D], writes=[r_wbg])
                P.dma("pool", "wo", wo[:], cw_out.rearrange("(kc p) n -> p kc n", p=128), writes=[r_wo])
                Z = [kb.sb(es3, "Z", [128, NKC, 512], BF16)] * 2
                r_Z = [[P.res() for _ in range(NKC)]] * 2
                bc = 0
                for ti, t in enumerate(lat):
                    xb = ti % 2
                    zb = ti % 2
                    emit_normmod(kb, scr, t, Am, Bm, rm, lambda kc, xb=xb: xn[xb][:, kc, :], r_xn[xb])
                    for c in range(NKC):
                        b = 1 + bc % 4
                        bc += 1
                        for kc in range(NKC):
                            P.op("pe", lambda e, b=b, c=c, kc=kc, xb=xb: e.matmul(
                                kb.ps[b][:], wbg[:, kc, c * 128:(c + 1) * 128], xn[xb][:, kc, :],
                                start=(kc == 0), stop=(kc == NKC - 1)),
                                reads=[r_wbg, r_xn[xb]], writes=[kb.psr[b]])
                        q = c % 2
                        base = 1 + ti * 512
                        ru = [r_U[c][k] for k in range(6)]
                        P.op("dve", lambda e, q=q, c=c, base=base: e.tensor_scalar_mul(
                            tmpc[q][:], U[:, c, base:base + 512], cwt[:, 1, c:c + 1]),
                            reads=ru + [r_cw], writes=[r_tmpc[q]])
                        P.op("dve", lambda e, q=q, c=c, base=base: e.scalar_tensor_tensor(
                            tmpc[q][:], U[:, c, base - 1:base + 511], cwt[:, 0, c:c + 1], tmpc[q][:], ALU.mult, ALU.add),
                            reads=ru + [r_cw, r_tmpc[q]], writes=[r_tmpc[q]])
                        P.op("dve", lambda e, q=q, c=c, base=base: e.scalar_tensor_tensor(
                            tmpc[q][:], U[:, c, base + 1:base + 513], cwt[:, 2, c:c + 1], tmpc[q][:], ALU.mult, ALU.add),
                            reads=ru + [r_cw, r_tmpc[q]], writes=[r_tmpc[q]])
                        P.op("dve", lambda e, q=q, c=c, b=b, zb=zb: e.tensor_tensor(
                            Z[zb][:, c, :], tmpc[q][:], kb.ps[b][:], ALU.mult),
                            reads=[r_tmpc[q], kb.psr[b]], writes=[r_Z[zb][c]])
                    for oc in range(NKC):
                        b = 1 + bc % 4
                        bc += 1
                        for kc in range(NKC):
                            P.op("pe", lambda e, b=b, oc=oc, kc=kc, zb=zb: e.matmul(
                                kb.ps[b][:], wo[:, kc, oc * 128:(oc + 1) * 128], Z[zb][:, kc, :],
                                start=(kc == 0), stop=(kc == NKC - 1)),
                                reads=[r_wo, r_Z[zb][kc]], writes=[kb.psr[b]])
                        P.op("dve", lambda e, b=b, oc=oc, t=t: e.scalar_tensor_tensor(
                            t.x[oc], kb.ps[b][:], G5[:, oc:oc + 1], t.x[oc], ALU.mult, ALU.add),
                            reads=[kb.psr[b], t.xres[oc], r_mv], writes=[t.xres[oc]])
                P.barrier()
        A1, B1, G1, r1 = emit_ab(kb, es, mv, r_mv, 0, 3, 4, ng[:, 1, :], r_ng, "f4", i_gate=5, gate_mul=0.5)
        for t in lat:
            t.A, t.B, t.HG, t.modres = A1, B1, G1, r1
        emit_ffn(kb, [lat[0:2], lat[2:4]], wg, wu, wd)
        with ExitStack() as es2:
            scr = make_scratch(kb, es2)
            ob = [kb.sb(es2, "ob", [128, 512], F32) for _ in range(4)]
            r_ob = [P.res() for _ in range(4)]
            oc_ = 0
            for ti, t in enumerate(lat):
                n = t.n
                ps_stat, r_ps = kb.ps[0], kb.psr[0]
                for kc in range(NKC):
                    s = kc % 2
                    P.op("act", lambda e, s=s, kc=kc, t=t: e.activation(scr["sq"][s][:], t.x[kc], AF.Square),
                         reads=[t.xres[kc]], writes=[scr["r_sq"][s]])
                    P.op("pe", lambda e, s=s, kc=kc: e.matmul(ps_stat[:], kb.ones, scr["sq"][s][:],
                                                              start=(kc == 0), stop=(kc == NKC - 1)),
                         reads=[scr["r_sq"][s], kb.r_cst], writes=[r_ps])
                rstd = scr["rstd"]
                P.op("act", lambda e: e.activation(rstd[:], ps_stat[:], AF.Sqrt, bias=scr["eps"][:, 0:1], scale=1.0 / D),
                     reads=[r_ps, scr["r_eps"]], writes=[scr["r_rstd"]])
                P.op("dve", lambda e: e.reciprocal(rstd[:], rstd[:]), reads=[scr["r_rstd"]], writes=[scr["r_rstd"]])
                for kc in range(NKC):
                    o = oc_ % 4
                    oc_ += 1
                    P.op("dve", lambda e, o=o, kc=kc, t=t: e.scalar_tensor_tensor(
                        ob[o][:], t.x[kc], ng[:, 2, kc:kc + 1], rstd[:], ALU.mult, ALU.mult),
                        reads=[t.xres[kc], scr["r_rstd"], r_ng], writes=[r_ob[o]])
                    kb.store(outT[kc][:, ti * 512:(ti + 1) * 512], ob[o][:], [r_ob[o]], f"ob{o}")
            kb.finish()
    return kb.nc


_CACHE = {}


def _get(name, fn):
    if name not in _CACHE:
        _CACHE[name] = fn()
    return _CACHE[name]


def _pk(v, nch):
    return np.ascontiguousarray(np.asarray(v, np.float32).reshape(nch, 128).T)


def _consts():
    c = np.zeros((128, 3, 128), np.float32)
    c[:, 0, :] = 1.0
    c[:, 1, :] = np.eye(128, dtype=np.float32)
    idx = np.arange(128)
    c[idx, 2, idx ^ 1] = 1.0
    return c


def _rope_tables(t0):
    t = np.arange(t0, t0 + TPC)
    row = (t // 64).astype(np.float64)
    col = (t % 64).astype(np.float64)
    inv = 10000.0 ** (-np.arange(16, dtype=np.float64) / 16)
    ang = np.concatenate([row[:, None] * inv, col[:, None] * inv], -1)
    cos = np.cos(ang.astype(np.float32).astype(np.float64))
    sin = np.sin(ang.astype(np.float32).astype(np.float64))
    ang32 = np.concatenate([row[:, None].astype(np.float32) * inv.astype(np.float32),
                            col[:, None].astype(np.float32) * inv.astype(np.float32)], -1)
    cos = np.cos(ang32)
    sin = np.sin(ang32)
    tab = np.zeros((128, 2, TPC), np.float32)
    for p in range(128):
        d = p % 64
        i = d // 2
        tab[p, 0] = cos[:, i]
        tab[p, 1] = (-sin[:, i]) if d % 2 == 0 else sin[:, i]
    return tab


def _bias_table(rpb):
    rpb = np.asarray(rpb, np.float32)
    qc = np.arange(64)
    kc = np.arange(64)
    col_start = np.clip(qc - 8, 0, 48)
    cmask = (kc[None, :] >= col_start[:, None]) & (kc[None, :] < col_start[:, None] + 16)
    dc = np.clip(kc[None, :] - qc[:, None] + 15, 0, 30)
    TT = np.zeros((128, 8, 22, 64), np.float32)
    for half in range(2):
        for jj in range(22):
            dr = 10 - jj + half
            if -7 <= dr <= 7:
                val = rpb[:, dr + 7, :][:, dc]
            else:
                val = np.zeros((8, 64, 64), np.float32)
            val = np.where(cmask[None], val, NEG)
            TT[half * 64:(half + 1) * 64, :, jj, :] = np.transpose(val, (2, 0, 1))
    return TT.reshape(128, 8, 22 * 64).astype(NPBF)


def _row_valid(r0):
    Rv = np.zeros((4, 2, 8, 8, 64), np.float32)
    for qt in range(4):
        R0 = r0 + 8 * qt
        for j in range(8):
            for a in range(2):
                kr = R0 - 4 + 2 * j + a
                for i in range(8):
                    qr = R0 + i
                    st = min(max(qr - 4, 0), 120)
                    ok = (st <= kr < st + 8)
                    Rv[qt, a, j, i, :] = 0.0 if ok else NEG
    return Rv.reshape(4, 2, 8 * 512).astype(NPBF)


def _run(nc, in_maps):
    res = run_bass_kernel_spmd(nc, in_maps, core_ids=list(range(8)))
    return res.results


def kernel(x, c, ctx, c_ctx, mod_w, mod_b, norm_g, ffn_w_gate, ffn_w_up, ffn_w_down,
           attn_w_in, attn_w_out, na_rpb, diff_lambda, diff_subln_g,
           conv_w_in, conv_w_out, conv_w, final_g):
    f32 = lambda a: np.ascontiguousarray(np.asarray(a, np.float32))
    x, c, ctx, c_ctx = f32(x), f32(c), f32(ctx), f32(c_ctx)
    mod_w, mod_b, norm_g = f32(mod_w), f32(mod_b), f32(norm_g)
    cst = _consts()
    NCORE = 8

    ncA = build_A()
    mapsA = []
    wA = dict(modw=f32(mod_w[0][:, :5 * D]), modb=_pk(mod_b[0][:5 * D], 40),
              normg=f32(np.stack([_pk(norm_g[0, 0], 8), _pk(norm_g[0, 1], 8)], 1)),
              wg=f32(ffn_w_gate[0, 0]), wu=f32(ffn_w_up[0, 0]), wd=f32(ffn_w_down[0, 0]),
              w_in=f32(attn_w_in[0]), cst=cst)
    for core in range(NCORE):
        b, t0 = core // 4, (core % 4) * TPC
        xs = x[b, t0:t0 + TPC]
        xT = f32(xs.T.reshape(NKC, 128, TPC))
        cxT = f32(ctx[b].T.reshape(NKC, 128, 256))
        cv = np.stack([c[b].reshape(8, 128).T, c_ctx.reshape(8, 128).T], -1)
        mapsA.append(dict(xT=xT, cxT=cxT, cvec=f32(cv), rope=_rope_tables(t0), **wA))
    resA = _run(ncA, mapsA)

    ncB = build_B()
    mapsB = []
    E2 = np.zeros((2, 128), np.float32)
    E2[0, :64] = 1.0
    E2[1, 64:] = 1.0
    onesP = np.zeros((128, 2, 128), np.float32)
    onesP[:, 0, :64] = 1.0
    onesP[:, 1, 64:] = 1.0
    TTh = _bias_table(na_rpb[0])
    lamv = f32(np.broadcast_to(np.asarray(diff_lambda[0], np.float32)[None], (128, 4, 64)))
    modwB = np.zeros((D, 4 * D), np.float32)
    modwB[:, :3 * D] = mod_w[1][:, :3 * D]
    modbB = np.zeros((128, 32), np.float32)
    modbB[:, :24] = _pk(mod_b[1][:3 * D], 24)
    wB = dict(TT=TTh, E2=E2.astype(NPBF), onesP=onesP.astype(NPBF), cst=cst, lamv=lamv,
              sublng=f32(np.asarray(diff_subln_g[0], np.float32).reshape(128, 1)),
              modwA=f32(mod_w[0][:, 5 * D:]), modbA=_pk(mod_b[0][5 * D:], 32),
              modwB=modwB, modbB=modbB,
              normg=f32(np.stack([_pk(norm_g[0, 2], 8), _pk(norm_g[1, 0], 8)], 1)),
              w_out=f32(attn_w_out[0]),
              wg1=f32(ffn_w_gate[0, 1]), wu1=f32(ffn_w_up[0, 1]), wd1=f32(ffn_w_down[0, 1]),
              wg2=f32(ffn_w_gate[1, 0]), wu2=f32(ffn_w_up[1, 0]), wd2=f32(ffn_w_down[1, 0]))
    per_batch = {}
    for b in range(2):
        cores = [4 * b + i for i in range(4)]
        qk = [np.asarray(resA[cc]["qkT"]) for cc in cores]
        vt = [np.asarray(resA[cc]["vtm"]).reshape(TPC, 1024) for cc in cores]
        kc_ = np.asarray(resA[cores[0]]["kcT"])
        vc_ = np.asarray(resA[cores[0]]["vc"]).reshape(256, 1024)
        ka_full = np.concatenate([q[4:8] for q in qk], axis=2)
        z = np.zeros((4, 128, 256), ka_full.dtype)
        ka_pad = np.concatenate([z, ka_full, z], axis=2)
        va_full = np.concatenate([v[:, 0:512] for v in vt], axis=0)
        zv = np.zeros((256, 512), va_full.dtype)
        va_pad = np.concatenate([zv, va_full, zv], axis=0)
        kb_full = np.concatenate([q[12:16] for q in qk] , axis=2)
        kbT = np.ascontiguousarray(np.concatenate([kb_full, kc_[4:8]], axis=2))
        vb_full = np.concatenate([v[:, 512:1024] for v in vt] + [vc_[:, 512:1024]], axis=0)
        vbT = np.ascontiguousarray(vb_full.reshape(66, 128, 4, 128).transpose(2, 1, 0, 3).reshape(4, 128, 66 * 128))
        per_batch[b] = (ka_pad, va_pad, kc_, vc_, kbT, vbT)
    for core in range(NCORE):
        b, t0 = core // 4, (core % 4) * TPC
        ka_pad, va_pad, kc_, vc_, kbT, vbT = per_batch[b]
        qk = np.asarray(resA[core]["qkT"])
        kaB = np.ascontiguousarray(np.concatenate([ka_pad[:, :, t0:t0 + 2560], kc_[0:4]], axis=2))
        vband = np.concatenate([va_pad[t0:t0 + 2560], vc_[:, 0:512]], axis=0).reshape(2816, 8, 64)
        vP = np.zeros((2816, 8, 128), vband.dtype)
        vP[:, 0::2, 0:64] = vband[:, 0::2]
        vP[:, 1::2, 64:128] = vband[:, 1::2]
        vaP = np.ascontiguousarray(vP.reshape(22, 128, 4, 256).transpose(2, 1, 0, 3).reshape(4, 128, 22 * 256))
        cv = c[b].reshape(8, 128).T[:, :, None]
        mapsB.append(dict(qaT=np.ascontiguousarray(qk[0:4]), qbT=np.ascontiguousarray(qk[8:12]),
                          kaB=kaB, vaP=vaP, kbT=kbT, vbT=vbT, Rv=_row_valid((core % 4) * 32),
                          x1T=np.asarray(resA[core]["x1T"]), cvec=f32(cv), **wB))
    resB = _run(ncB, mapsB)

    ncC = build_C()
    mapsC = []
    wC = dict(modw=f32(mod_w[1][:, 3 * D:]), modb=_pk(mod_b[1][3 * D:], 48),
              normg=f32(np.stack([_pk(norm_g[1, 1], 8), _pk(norm_g[1, 2], 8), _pk(final_g, 8)], 1)),
              cw=f32(np.stack([_pk(np.asarray(conv_w[0], np.float32)[k], 8) for k in range(3)], 1)),
              cw_in=f32(conv_w_in[0]), cw_out=f32(conv_w_out[0]),
              wg=f32(ffn_w_gate[1, 1]), wu=f32(ffn_w_up[1, 1]), wd=f32(ffn_w_down[1, 1]), cst=cst)
    x4 = [np.asarray(resB[cc]["x4T"]) for cc in range(NCORE)]
    for core in range(NCORE):
        b = core // 4
        xh = np.zeros((128, NKC, 2), np.float32)
        hmask = np.zeros((128, 2), np.float32)
        if core % 4 != 0:
            xh[:, :, 0] = x4[core - 1][:, :, TPC - 1].T
            hmask[:, 0] = 1.0
        if core % 4 != 3:
            xh[:, :, 1] = x4[core + 1][:, :, 0].T
            hmask[:, 1] = 1.0
        cv = c[b].reshape(8, 128).T[:, :, None]
        mapsC.append(dict(x4T=x4[core], xhT=xh, hmask=hmask, cvec=f32(cv), **wC))
    resC = _run(ncC, mapsC)

    out = np.zeros((2, 8192, D), np.float32)
    for core in range(NCORE):
        b, t0 = core // 4, (core % 4) * TPC
        oT = np.asarray(resC[core]["outT"]).reshape(D, TPC)
        out[b, t0:t0 + TPC] = oT.T
    return out
```

```python
import math
from contextlib import ExitStack

import numpy as np
import ml_dtypes
import concourse.bass as bass
import concourse.mybir as mybir
from concourse.bass_utils import run_bass_kernel_spmd

F32 = mybir.dt.float32
BF16 = mybir.dt.bfloat16
ALU = mybir.AluOpType
AF = mybir.ActivationFunctionType
NPBF = ml_dtypes.bfloat16

D = 1024
DFF = 2816
NKC = 8
NFC = 22
TPC = 2048
EPS = 1e-6
NEG = -30000.0
DEBUG_STOP = 0

ENGS = ["pe", "act", "dve", "pool", "sp"]
SEM_CHUNK = 30000


class Res:
    __slots__ = ("name", "w", "r")

    def __init__(self, name):
        self.name = name
        self.w = None
        self.r = []


class Op:
    __slots__ = ("eng", "fn", "deps", "signal", "sig", "dma_key", "dma_val", "dma_inc")

    def __init__(self, eng, fn):
        self.eng = eng
        self.fn = fn
        self.deps = []
        self.signal = False
        self.sig = None
        self.dma_key = None
        self.dma_val = 0
        self.dma_inc = 16


class Prog:
    def __init__(self, nc):
        self.nc = nc
        self.q = {e: [] for e in ENGS}
        self.dma_cnt = {}
        self.nres = 0
        self.pending = {e: [] for e in ENGS}
        self.dmas = []

    def res(self, name=None):
        self.nres += 1
        return Res(name or f"r{self.nres}")

    def _add(self, eng, fn, reads, writes, dma_key=None, inc=16):
        op = Op(eng, fn)
        deps = []
        for r in reads:
            if r.w is not None:
                deps.append(r.w)
        for w in writes:
            if w.w is not None:
                deps.append(w.w)
            deps.extend(w.r)
        if self.pending[eng]:
            deps.extend(self.pending[eng])
            self.pending[eng] = []
        seen = set()
        for d in deps:
            if id(d) in seen:
                continue
            seen.add(id(d))
            if d.dma_key is None:
                if eng == "pe" and d.eng == "pe":
                    continue
                d.signal = True
            op.deps.append(d)
        if dma_key is not None:
            op.dma_key = dma_key
            self.dma_cnt[dma_key] = self.dma_cnt.get(dma_key, 0) + inc
            op.dma_val = self.dma_cnt[dma_key]
            op.dma_inc = inc
            self.dmas.append(op)
        for r in reads:
            r.r.append(op)
        for w in writes:
            w.w = op
            w.r = []
        self.q[eng].append(op)
        return op

    def op(self, eng, fn, reads=(), writes=()):
        return self._add(eng, fn, reads, writes)

    def dma(self, eng, key, out, in_, reads=(), writes=()):
        return self._add(eng, lambda e: e.dma_start(out=out, in_=in_), reads, writes, dma_key=key)

    def barrier(self):
        lasts = []
        for e in ENGS:
            for op in reversed(self.q[e]):
                if op.dma_key is None:
                    lasts.append(op)
                    break
        lasts.extend(self.dmas)
        self.dmas = []
        for e in ENGS:
            self.pending[e] = list(lasts)

    def emit(self):
        nc = self.nc
        sems = {}

        def sem(name):
            if name not in sems:
                sems[name] = nc.alloc_semaphore(name)
            return sems[name]

        for e in ENGS:
            k = 0
            for op in self.q[e]:
                if op.dma_key is None and op.signal:
                    op.sig = (f"s_{e}_{k // SEM_CHUNK}", k % SEM_CHUNK + 1)
                    k += 1
                elif op.dma_key is not None:
                    op.sig = (f"d_{op.dma_key}", op.dma_val)

        def run(e, engine):
            waited = {}
            for op in self.q[e]:
                need = {}
                for d in op.deps:
                    s, v = d.sig
                    if waited.get(s, 0) >= v:
                        continue
                    if need.get(s, 0) < v:
                        need[s] = v
                for s, v in need.items():
                    engine.wait_ge(sem(s), v)
                    waited[s] = v
                ins = op.fn(engine)
                if op.dma_key is not None:
                    ins.then_inc(sem(op.sig[0]), op.dma_inc)
                elif op.signal:
                    ins.then_inc(sem(op.sig[0]), 1)

        with nc.Block() as block:
            @block.tensor
            def _(eng):
                run("pe", eng)

            @block.scalar
            def _(eng):
                run("act", eng)

            @block.vector
            def _(eng):
                run("dve", eng)

            @block.gpsimd
            def _(eng):
                run("pool", eng)

            @block.sync
            def _(eng):
                run("sp", eng)


class KB:
    def __init__(self):
        self.nc = bass.Bass("TRN2", target_bir_lowering=False)
        self.P = Prog(self.nc)
        self.ps = [self.nc.alloc_psum_tensor(f"psb{i}", [128, 512], F32) for i in range(8)]
        self.psr = [self.P.res(f"psb{i}") for i in range(8)]
        self.out_res = []
        self.uid = 0

    def inp(self, name, shape, dt=F32):
        return self.nc.dram_tensor(name, list(shape), dt, kind="ExternalInput").ap()

    def outp(self, name, shape, dt=F32):
        return self.nc.dram_tensor(name, list(shape), dt, kind="ExternalOutput").ap()

    def sb(self, es, name, shape, dt):
        self.uid += 1
        return es.enter_context(self.nc.sbuf_tensor(f"{name}_{self.uid}", list(shape), dt))

    def store(self, dram_ap, sb_ap, reads, key):
        r = self.P.res()
        self.P.dma("sp", "st_" + key, dram_ap, sb_ap, reads=reads, writes=[r])
        self.out_res.append(r)

    def finish(self):
        self.P.op("sp", lambda e: e.nop(), reads=self.out_res)
        self.P.emit()
        return self.nc

    def load_consts(self, es, cst_ap):
        P = self.P
        self.cst = self.sb(es, "cst", [128, 3, 128], BF16)
        self.r_cst = P.res("cst")
        P.dma("pool", "cst", self.cst[:], cst_ap, writes=[self.r_cst])
        self.ones = self.cst[:, 0, :]
        self.ident = self.cst[:, 1, :]
        self.swap = self.cst[:, 2, :]


class Tile:
    def __init__(self, x, xres, n, A=None, B=None, HG=None, modres=None):
        self.x = x
        self.xres = xres
        self.n = n
        self.A = A
        self.B = B
        self.HG = HG
        self.modres = modres
        self.off = 0


def emit_mod(kb, es, modw_ap, nch, cvec_ap, nv, modb_ap, name):
    P = kb.P
    nc = kb.nc
    assert nch % 4 == 0
    cv = kb.sb(es, name + "cv", [128, 8, nv], F32)
    sc = kb.sb(es, name + "sc", [128, 8, nv], BF16)
    mb = kb.sb(es, name + "mb", [128, nch], F32)
    mvec = kb.sb(es, name + "mvec", [128, nch, nv], F32)
    r_cv, r_sc, r_mb, r_mvec = P.res(), P.res(), P.res(), P.res()
    P.dma("sp", name + "cv", cv[:], cvec_ap, writes=[r_cv])
    P.dma("sp", name + "mb", mb[:], modb_ap, writes=[r_mb])
    P.op("act", lambda e: e.activation(sc[:], cv[:], AF.Silu), reads=[r_cv], writes=[r_sc])
    psm = kb.ps[7]
    r_psm = kb.psr[7]
    with ExitStack() as es2:
        mw = [kb.sb(es2, name + f"mw{s}", [128, 8, 512], BF16) for s in range(2)]
        r_mw = [P.res(), P.res()]
        src = modw_ap.rearrange("(kc p) n -> p kc n", p=128)
        for g in range(nch // 4):
            s = g % 2
            P.dma("pool", name + f"mw{s}", mw[s][:], src[:, :, g * 512:(g + 1) * 512], writes=[r_mw[s]])
            for c4 in range(4):
                oc = g * 4 + c4
                for kc in range(8):
                    P.op("pe", lambda e, s=s, c4=c4, oc=oc, kc=kc: e.matmul(
                        psm[:, oc * nv:(oc + 1) * nv], mw[s][:, kc, c4 * 128:(c4 + 1) * 128], sc[:, kc, :],
                        start=(kc == 0), stop=(kc == 7)), reads=[r_mw[s], r_sc], writes=[r_psm])
        for v in range(nv):
            P.op("dve", lambda e, v=v: e.tensor_tensor(
                mvec[:, :, v], psm[:, 0:nch * nv].rearrange("p (c v) -> p c v", v=nv)[:, :, v], mb[:], ALU.add),
                reads=[r_psm, r_mb], writes=[r_mvec])
        P.barrier()
    return mvec, r_mvec


def emit_ab(kb, es, mvec, r_mvec, v, i_shift, i_scale, g_ap, r_g, name, i_gate=None, gate_mul=1.0):
    P = kb.P
    A = kb.sb(es, name + "A", [128, 8], F32)
    r = P.res()
    P.op("dve", lambda e: e.scalar_tensor_tensor(
        A[:], mvec[:, i_scale * 8:(i_scale + 1) * 8, v], 1.0, g_ap, ALU.add, ALU.mult),
        reads=[r_mvec, r_g], writes=[r])
    Bm = mvec[:, i_shift * 8:(i_shift + 1) * 8, v]
    G = None
    if i_gate is not None:
        G = kb.sb(es, name + "G", [128, 8], F32)
        P.op("dve", lambda e: e.tensor_scalar_mul(G[:], mvec[:, i_gate * 8:(i_gate + 1) * 8, v], float(gate_mul)),
             reads=[r_mvec], writes=[r])
    return A, Bm, G, r


def emit_normmod(kb, scr, t, A, Bv, r_ab, out_fn, r_out):
    P = kb.P
    n = t.n
    ps_stat, r_ps = kb.ps[0], kb.psr[0]
    for kc in range(NKC):
        s = kc % 2
        P.op("act", lambda e, s=s, kc=kc: e.activation(scr["sq"][s][:, :n], t.x[kc], AF.Square),
             reads=[t.xres[kc]], writes=[scr["r_sq"][s]])
        P.op("pe", lambda e, s=s, kc=kc: e.matmul(ps_stat[:, :n], kb.ones, scr["sq"][s][:, :n],
                                                  start=(kc == 0), stop=(kc == NKC - 1)),
             reads=[scr["r_sq"][s], kb.r_cst], writes=[r_ps])
    rstd = scr["rstd"]
    P.op("act", lambda e: e.activation(rstd[:, :n], ps_stat[:, :n], AF.Sqrt, bias=scr["eps"][:, 0:1], scale=1.0 / D),
         reads=[r_ps, scr["r_eps"]], writes=[scr["r_rstd"]])
    P.op("dve", lambda e: e.reciprocal(rstd[:, :n], rstd[:, :n]), reads=[scr["r_rstd"]], writes=[scr["r_rstd"]])
    for kc in range(NKC):
        s = kc % 2
        P.op("dve", lambda e, s=s, kc=kc: e.tensor_tensor(scr["tmp"][s][:, :n], t.x[kc], rstd[:, :n], ALU.mult),
             reads=[t.xres[kc], scr["r_rstd"]], writes=[scr["r_tmp"][s]])
        P.op("act", lambda e, s=s, kc=kc: e.activation(out_fn(kc), scr["tmp"][s][:, :n], AF.Identity,
                                                       bias=Bv[:, kc:kc + 1], scale=A[:, kc:kc + 1]),
             reads=[scr["r_tmp"][s], r_ab], writes=[r_out])


def make_scratch(kb, es):
    P = kb.P
    scr = {
        "sq": [kb.sb(es, "sq", [128, 512], BF16) for _ in range(2)],
        "r_sq": [P.res(), P.res()],
        "rstd": kb.sb(es, "rstd", [128, 512], F32),
        "r_rstd": P.res(),
        "tmp": [kb.sb(es, "tmp", [128, 512], F32) for _ in range(2)],
        "r_tmp": [P.res(), P.res()],
        "eps": kb.sb(es, "eps", [128, 1], F32),
        "r_eps": P.res(),
    }
    P.op("dve", lambda e: e.memset(scr["eps"][:], EPS), writes=[scr["r_eps"]])
    return scr


def emit_ffn(kb, groups, wg_ap, wu_ap, wd_ap, xn_ext=None):
    P = kb.P
    with ExitStack() as es:
        maxw = max(sum(t.n for t in g) for g in groups)
        if xn_ext is None:
            xn_t = kb.sb(es, "ffn_xn", [128, NKC, maxw], BF16)
            xn = lambda kc, a, b: xn_t[:, kc, a:b]
        else:
            xn = xn_ext
        H = kb.sb(es, "ffn_H", [128, NFC, maxw], BF16)
        wgb = [kb.sb(es, "wg", [128, NKC, 256], BF16) for _ in range(2)]
        wub = [kb.sb(es, "wu", [128, NKC, 256], BF16) for _ in range(2)]
        wdb = [kb.sb(es, "wd", [128, NFC, 256], BF16) for _ in range(2)]
        sg = [kb.sb(es, "sg", [128, 512], F32) for _ in range(2)]
        r_wg, r_wu, r_wd, r_sg = ([P.res(), P.res()] for _ in range(4))
        scr = make_scratch(kb, es)
        wg_src = wg_ap.rearrange("(kc p) n -> p kc n", p=128)
        wu_src = wu_ap.rearrange("(kc p) n -> p kc n", p=128)
        wd_src = wd_ap.rearrange("(kc p) n -> p kc n", p=128)
        gu_banks = [(1, 2), (3, 4)]
        y_banks = [5, 6]
        cnt = 0
        ycnt = 0
        wcnt = 0
        dcnt = 0
        for g in groups:
            off = 0
            for t in g:
                t.off = off
                off += t.n
            r_xn = {id(t): P.res() for t in g}
            r_H = {(id(t), j): P.res() for t in g for j in range(NFC)}
            for t in g:
                emit_normmod(kb, scr, t, t.A, t.B, t.modres,
                             lambda kc, o=t.off, n=t.n: xn(kc, o, o + n), r_xn[id(t)])
            for j2 in range(NFC // 2):
                s = wcnt % 2
                wcnt += 1
                P.dma("pool", f"wg{s}", wgb[s][:], wg_src[:, :, j2 * 256:(j2 + 1) * 256], writes=[r_wg[s]])
                P.dma("pool", f"wu{s}", wub[s][:], wu_src[:, :, j2 * 256:(j2 + 1) * 256], writes=[r_wu[s]])
                for jj in range(2):
                    j = j2 * 2 + jj
                    for t in g:
                        n = t.n
                        bg_, bu_ = gu_banks[cnt % 2]
                        q = cnt % 2
                        cnt += 1
                        for kc in range(NKC):
                            P.op("pe", lambda e, s=s, jj=jj, kc=kc, o=t.off, n=n, bg_=bg_: e.matmul(
                                kb.ps[bg_][:, :n], wgb[s][:, kc, jj * 128:(jj + 1) * 128], xn(kc, o, o + n),
                                start=(kc == 0), stop=(kc == NKC - 1)),
                                reads=[r_wg[s], r_xn[id(t)]], writes=[kb.psr[bg_]])
                        for kc in range(NKC):
                            P.op("pe", lambda e, s=s, jj=jj, kc=kc, o=t.off, n=n, bu_=bu_: e.matmul(
                                kb.ps[bu_][:, :n], wub[s][:, kc, jj * 128:(jj + 1) * 128], xn(kc, o, o + n),
                                start=(kc == 0), stop=(kc == NKC - 1)),
                                reads=[r_wu[s], r_xn[id(t)]], writes=[kb.psr[bu_]])
                        P.op("act", lambda e, q=q, n=n, bg_=bg_: e.activation(sg[q][:, :n], kb.ps[bg_][:, :n], AF.Silu),
                             reads=[kb.psr[bg_]], writes=[r_sg[q]])
                        P.op("dve", lambda e, q=q, n=n, bu_=bu_, j=j, o=t.off: e.tensor_tensor(
                            H[:, j, o:o + n], sg[q][:, :n], kb.ps[bu_][:, :n], ALU.mult),
                            reads=[r_sg[q], kb.psr[bu_]], writes=[r_H[(id(t), j)]])
            for i4 in range(4):
                s = dcnt % 2
                dcnt += 1
                P.dma("pool", f"wd{s}", wdb[s][:], wd_src[:, :, i4 * 256:(i4 + 1) * 256], writes=[r_wd[s]])
                for ii in range(2):
                    i = i4 * 2 + ii
                    for t in g:
                        n = t.n
                        by = y_banks[ycnt % 2]
                        ycnt += 1
                        for kc in range(NFC):
                            P.op("pe", lambda e, s=s, ii=ii, kc=kc, o=t.off, n=n, by=by: e.matmul(
                                kb.ps[by][:, :n], wdb[s][:, kc, ii * 128:(ii + 1) * 128], H[:, kc, o:o + n],
                                start=(kc == 0), stop=(kc == NFC - 1)),
                                reads=[r_wd[s], r_H[(id(t), kc)]], writes=[kb.psr[by]])
                        P.op("dve", lambda e, i=i, t=t, n=n, by=by, hg=t.HG: e.scalar_tensor_tensor(
                            t.x[i], kb.ps[by][:, :n], hg[:, i:i + 1], t.x[i], ALU.mult, ALU.add),
                            reads=[kb.psr[by], t.xres[i], t.modres], writes=[t.xres[i]])
        P.barrier()


def make_tiles(x_t, P, ntok, width=512):
    tiles = []
    for a in range(0, ntok, width):
        n = min(width, ntok - a)
        tiles.append(Tile([x_t[:, kc, a:a + n] for kc in range(NKC)], [P.res() for _ in range(NKC)], n))
    return tiles


def build_A():
    kb = KB()
    P = kb.P
    xT = kb.inp("xT", [NKC, 128, TPC])
    cxT = kb.inp("cxT", [NKC, 128, 256])
    cvec = kb.inp("cvec", [128, 8, 2])
    modw = kb.inp("modw", [D, 5 * D])
    modb = kb.inp("modb", [128, 40])
    normg = kb.inp("normg", [128, 2, 8])
    wg = kb.inp("wg", [D, DFF])
    wu = kb.inp("wu", [D, DFF])
    wd = kb.inp("wd", [DFF, D])
    w_in = kb.inp("w_in", [D, 3 * D])
    cst = kb.inp("cst", [128, 3, 128])
    rope = kb.inp("rope", [128, 2, TPC])
    x1T = kb.outp("x1T", [NKC, 128, TPC])
    if DEBUG_STOP in (0, 3, 4, 5, 6, 7):
        qkT = kb.outp("qkT", [16, 128, TPC], BF16)
        vtm = kb.outp("vtm", [16, 128, 1024], BF16)
        kcT = kb.outp("kcT", [8, 128, 256], BF16)
        vc = kb.outp("vc", [2, 128, 1024], BF16)

    with ExitStack() as es:
        kb.load_consts(es, cst)
        x_t = kb.sb(es, "x", [128, NKC, TPC], F32)
        xc_t = kb.sb(es, "xc", [128, NKC, 256], F32)
        ng = kb.sb(es, "ng", [128, 2, 8], F32)
        r_ng = P.res()
        P.dma("sp", "ng", ng[:], normg, writes=[r_ng])
        lat = make_tiles(x_t, P, TPC)
        ctx = make_tiles(xc_t, P, 256)
        for kc in range(NKC):
            for t in lat:
                pass
            P.dma("sp", f"xin{kc}", x_t[:, kc, :], xT[kc], writes=[t.xres[kc] for t in lat])
            P.dma("sp", f"xcin{kc}", xc_t[:, kc, :], cxT[kc], writes=[ctx[0].xres[kc]])
        if DEBUG_STOP == -1:
            for kc in range(NKC):
                kb.store(x1T[kc], x_t[:, kc, :], [t.xres[kc] for t in lat], f"x{kc}")
            kb.finish()
            return kb.nc
        mvec, r_mvec = emit_mod(kb, es, modw, 40, cvec, 2, modb, "m0")
        if DEBUG_STOP == -2:
            for kc in range(NKC):
                kb.store(x1T[kc], x_t[:, kc, :], [t.xres[kc] for t in lat], f"x{kc}")
            kb.finish()
            return kb.nc
        ab = {}
        for v in range(2):
            A1, B1, G1, r1 = emit_ab(kb, es, mvec, r_mvec, v, 0, 1, ng[:, 0, :], r_ng, f"f1v{v}", i_gate=2, gate_mul=0.5)
            A2, B2, _, r2 = emit_ab(kb, es, mvec, r_mvec, v, 3, 4, ng[:, 1, :], r_ng, f"qkv{v}")
            ab[v] = (A1, B1, G1, r1, A2, B2, r2)
        for t in lat:
            t.A, t.B, t.HG, t.modres = ab[0][0], ab[0][1], ab[0][2], ab[0][3]
        for t in ctx:
            t.A, t.B, t.HG, t.modres = ab[1][0], ab[1][1], ab[1][2], ab[1][3]
        if DEBUG_STOP != 1:
            emit_ffn(kb, [lat[0:2] + ctx, lat[2:4]], wg, wu, wd)
        for kc in range(NKC):
            kb.store(x1T[kc], x_t[:, kc, :], [t.xres[kc] for t in lat], f"x{kc}")

        if DEBUG_STOP in (1, 2):
            kb.finish()
            return kb.nc
        with ExitStack() as es2:
            wi = kb.sb(es2, "wi", [128, NKC, 3 * D], BF16)
            r_wi = [P.res() for _ in range(6)]
            wi_src = w_in.rearrange("(kc p) n -> p kc n", p=128)
            for g6 in range(6):
                P.dma("pool", f"wi{g6}", wi[:, :, g6 * 512:(g6 + 1) * 512], wi_src[:, :, g6 * 512:(g6 + 1) * 512],
                      writes=[r_wi[g6]])
            rp = kb.sb(es2, "rope", [128, 2, TPC], F32)
            r_rp = P.res()
            P.dma("sp", "rope", rp[:], rope, writes=[r_rp])
            scr = make_scratch(kb, es2)
            xn2 = [kb.sb(es2, "xn2", [128, NKC, 512], BF16) for _ in range(2)]
            r_xn2 = [P.res(), P.res()]
            stg = [kb.sb(es2, "stg", [128, 512], BF16) for _ in range(4)]
            r_stg = [P.res() for _ in range(4)]
            qs = [kb.sb(es2, "qs", [128, 512], BF16) for _ in range(2)]
            r_qs = [P.res(), P.res()]
            t1 = [kb.sb(es2, "t1", [128, 512], F32) for _ in range(2)]
            r_t1 = [P.res(), P.res()]
            t2 = [kb.sb(es2, "t2", [128, 512], F32) for _ in range(2)]
            r_t2 = [P.res(), P.res()]
            banks = [1, 2, 3, 4]
            bc = 0
            sc_ = 0
            rc = 0
            for ti, t in enumerate(lat + ctx):
                is_ctx = ti >= len(lat)
                v = 1 if is_ctx else 0
                n = t.n
                tok0 = 0 if is_ctx else ti * 512
                xb = ti % 2
                emit_normmod(kb, scr, t, ab[v][4], ab[v][5], ab[v][6],
                             lambda kc, xb=xb, n=n: xn2[xb][:, kc, :n], r_xn2[xb])
                fm = [(c, c) for c in range(0, 8)] + [(12 + c, 8 + c) for c in range(8)]
                for (wc, oc) in fm:
                    kind = oc // 4
                    if is_ctx and kind in (0, 2):
                        continue
                    b = banks[bc % 4]
                    bc += 1
                    for kc in range(NKC):
                        P.op("pe", lambda e, b=b, wc=wc, kc=kc, xb=xb, n=n: e.matmul(
                            kb.ps[b][:, :n], wi[:, kc, wc * 128:(wc + 1) * 128], xn2[xb][:, kc, :n],
                            start=(kc == 0), stop=(kc == NKC - 1)),
                            reads=[r_wi[wc // 4], r_xn2[xb]], writes=[kb.psr[b]])
                    s = sc_ % 4
                    sc_ += 1
                    if kind >= 2 and not is_ctx and DEBUG_STOP not in (4, 5):
                        r = rc % 2
                        rc += 1
                        if DEBUG_STOP != 7:
                            b2 = banks[bc % 4]
                            bc += 1
                            P.op("act", lambda e, r=r, b=b, n=n: e.activation(qs[r][:, :n], kb.ps[b][:, :n], AF.Identity),
                                 reads=[kb.psr[b]], writes=[r_qs[r]])
                            P.op("pe", lambda e, r=r, b2=b2, n=n: e.matmul(kb.ps[b2][:, :n], kb.swap, qs[r][:, :n],
                                                                           start=True, stop=True),
                                 reads=[r_qs[r], kb.r_cst], writes=[kb.psr[b2]])
                        else:
                            b2 = b
                        if DEBUG_STOP != 6:
                            P.op("dve", lambda e, r=r, b=b, n=n, tok0=tok0: e.tensor_tensor(
                                t1[r][:, :n], kb.ps[b][:, :n], rp[:, 0, tok0:tok0 + n], ALU.mult),
                                reads=[kb.psr[b], r_rp] + ([r_qs[r]] if DEBUG_STOP != 7 else []), writes=[r_t1[r]])
                            P.op("dve", lambda e, r=r, b2=b2, n=n, tok0=tok0: e.tensor_tensor(
                                t2[r][:, :n], kb.ps[b2][:, :n], rp[:, 1, tok0:tok0 + n], ALU.mult),
                                reads=[kb.psr[b2], r_rp], writes=[r_t2[r]])
                            P.op("dve", lambda e, r=r, s=s, n=n: e.tensor_tensor(
                                stg[s][:, :n], t1[r][:, :n], t2[r][:, :n], ALU.add),
                                reads=[r_t1[r], r_t2[r]], writes=[r_stg[s]])
                        else:
                            P.op("act", lambda e, s=s, b2=b2, n=n: e.activation(
                                stg[s][:, :n], kb.ps[b2][:, :n], AF.Identity),
                                reads=[kb.psr[b2]], writes=[r_stg[s]])
                    else:
                        scale = 0.125 if kind == 0 else 1.0
                        P.op("act", lambda e, s=s, b=b, n=n, scale=scale: e.activation(
                            stg[s][:, :n], kb.ps[b][:, :n], AF.Identity, scale=scale),
                            reads=[kb.psr[b]], writes=[r_stg[s]])
                    if is_ctx:
                        dst = kcT[(oc - 4) if kind == 1 else (oc - 8)]
                        kb.store(dst, stg[s][:, :n], [r_stg[s]], f"stg{s}")
                    else:
                        kb.store(qkT[oc][:, tok0:tok0 + n], stg[s][:, :n], [r_stg[s]], f"stg{s}")
                for tb in range(0 if DEBUG_STOP in (3, 5, 6, 7) else n // 128):
                    for vi, wc0 in enumerate((8, 20)):
                        b = banks[bc % 4]
                        bc += 1
                        for kc in range(NKC):
                            P.op("pe", lambda e, b=b, wc0=wc0, kc=kc, xb=xb, tb=tb: e.matmul(
                                kb.ps[b][:, :], xn2[xb][:, kc, tb * 128:(tb + 1) * 128],
                                wi[:, kc, wc0 * 128:(wc0 + 4) * 128],
                                start=(kc == 0), stop=(kc == NKC - 1)),
                                reads=[r_wi[wc0 // 4], r_xn2[xb]], writes=[kb.psr[b]])
                        s = sc_ % 4
                        sc_ += 1
                        if vi == 0:
                            P.op("act", lambda e, s=s, b=b: e.activation(stg[s][:], kb.ps[b][:], AF.Identity),
                                 reads=[kb.psr[b]], writes=[r_stg[s]])
                        else:
                            P.op("dve", lambda e, s=s, b=b: e.tensor_copy(stg[s][:], kb.ps[b][:]),
                                 reads=[kb.psr[b]], writes=[r_stg[s]])
                        if is_ctx:
                            kb.store(vc[tb][:, vi * 512:(vi + 1) * 512], stg[s][:], [r_stg[s]], f"stg{s}")
                        else:
                            kb.store(vtm[ti * 4 + tb][:, vi * 512:(vi + 1) * 512], stg[s][:], [r_stg[s]], f"stg{s}")
            P.barrier()
        kb.finish()
    return kb.nc


def build_B():
    kb = KB()
    P = kb.P
    qaT = kb.inp("qaT", [4, 128, TPC], BF16)
    qbT = kb.inp("qbT", [4, 128, TPC], BF16)
    kaB = kb.inp("kaB", [4, 128, 2816], BF16)
    vaP = kb.inp("vaP", [4, 128, 22 * 256], BF16)
    kbT = kb.inp("kbT", [4, 128, 8448], BF16)
    vbT = kb.inp("vbT", [4, 128, 66 * 128], BF16)
    TT = kb.inp("TT", [128, 8, 22 * 64], BF16)
    Rv = kb.inp("Rv", [4, 2, 8 * 512], BF16)
    E2 = kb.inp("E2", [2, 128], BF16)
    onesP = kb.inp("onesP", [128, 2, 128], BF16)
    x1T = kb.inp("x1T", [NKC, 128, TPC])
    cst = kb.inp("cst", [128, 3, 128])
    lamv = kb.inp("lamv", [128, 4, 64])
    sublng = kb.inp("sublng", [128, 1])
    cvec = kb.inp("cvec", [128, 8, 1])
    modwA = kb.inp("modwA", [D, 4 * D])
    modbA = kb.inp("modbA", [128, 32])
    modwB = kb.inp("modwB", [D, 4 * D])
    modbB = kb.inp("modbB", [128, 32])
    normg = kb.inp("normg", [128, 2, 8])
    w_out = kb.inp("w_out", [D, D])
    wg1 = kb.inp("wg1", [D, DFF])
    wu1 = kb.inp("wu1", [D, DFF])
    wd1 = kb.inp("wd1", [DFF, D])
    wg2 = kb.inp("wg2", [D, DFF])
    wu2 = kb.inp("wu2", [D, DFF])
    wd2 = kb.inp("wd2", [DFF, D])
    x4T = kb.outp("x4T", [NKC, 128, TPC])

    with ExitStack() as es:
        kb.load_consts(es, cst)
        an = kb.sb(es, "an", [128, NKC, TPC], BF16)
        r_an = [[P.res() for _ in range(4)] for _ in range(NKC)]
        ng = kb.sb(es, "ng", [128, 2, 8], F32)
        r_ng = P.res()
        P.dma("sp", "ng", ng[:], normg, writes=[r_ng])
        mvA, r_mvA = emit_mod(kb, es, modwA, 32, cvec, 1, modbA, "mA")
        mvB, r_mvB = emit_mod(kb, es, modwB, 32, cvec, 1, modbB, "mB")
        lam_t = kb.sb(es, "lam", [128, 4, 64], F32)
        lsc = kb.sb(es, "lsc", [128, 8], F32)
        sg8 = kb.sb(es, "sg8", [128, 1], F32)
        r_lam, r_lsc, r_sg8 = P.res(), P.res(), P.res()
        P.dma("sp", "lam", lam_t[:], lamv, writes=[r_lam])
        P.dma("sp", "sg8", sg8[:], sublng, writes=[r_sg8])
        P.op("dve", lambda e: e.tensor_tensor(lam_t[:, 0, :], lam_t[:, 0, :], lam_t[:, 1, :], ALU.mult),
             reads=[r_lam], writes=[r_lam])
        P.op("dve", lambda e: e.tensor_tensor(lam_t[:, 2, :], lam_t[:, 2, :], lam_t[:, 3, :], ALU.mult),
             reads=[r_lam], writes=[r_lam])
        P.op("dve", lambda e: e.reduce_sum(lsc[:, 0:1], lam_t[:, 0, :], mybir.AxisListType.X), reads=[r_lam], writes=[r_lsc])
        P.op("dve", lambda e: e.reduce_sum(lsc[:, 1:2], lam_t[:, 2, :], mybir.AxisListType.X), reads=[r_lam], writes=[r_lsc])
        P.op("act", lambda e: e.activation(lsc[:, 2:4], lsc[:, 0:2], AF.Exp), reads=[r_lsc], writes=[r_lsc])
        LAM_INIT = 0.8 - 0.6 * math.exp(-0.3 * 0)
        P.op("dve", lambda e: e.tensor_tensor(lsc[:, 4:5], lsc[:, 3:4], lsc[:, 2:3], ALU.subtract), reads=[r_lsc], writes=[r_lsc])
        P.op("dve", lambda e: e.tensor_scalar_add(lsc[:, 5:6], lsc[:, 4:5], -LAM_INIT), reads=[r_lsc], writes=[r_lsc])
        P.op("dve", lambda e: e.tensor_scalar_mul(sg8[:], sg8[:], 1.0 - LAM_INIT), reads=[r_sg8], writes=[r_sg8])
        neglam = lsc[:, 5:6]

        with ExitStack() as es2:
            tt = kb.sb(es2, "tt", [128, 8, 22 * 64], BF16)
            r_tt = P.res()
            P.dma("sp", "tt", tt[:], TT, writes=[r_tt])
            e2 = kb.sb(es2, "e2", [2, 128], BF16)
            r_e2 = P.res()
            P.dma("sp", "e2", e2[:], E2, writes=[r_e2])
            op_ = kb.sb(es2, "onesP", [128, 2, 128], BF16)
            r_op = P.res()
            P.dma("sp", "onesP", op_[:], onesP, writes=[r_op])
            rv = [kb.sb(es2, "rv", [2, 8 * 512], BF16) for _ in range(2)]
            r_rv = [P.res(), P.res()]
            qsb = [kb.sb(es2, "qsb", [128, 512], BF16) for _ in range(2)]
            r_q = [P.res(), P.res()]
            kA = [kb.sb(es2, "kA", [128, 1280], BF16) for _ in range(2)]
            r_kA = [P.res(), P.res()]
            vP = [kb.sb(es2, "vP", [128, 10 * 256], BF16) for _ in range(2)]
            r_vP = [P.res(), P.res()]
            kB_ = [kb.sb(es2, "kB", [128, 8448], BF16) for _ in range(2)]
            r_kB = [P.res(), P.res()]
            vB_ = [kb.sb(es2, "vB", [128, 66 * 128], BF16) for _ in range(2)]
            r_vB = [P.res(), P.res()]
            pT = [kb.sb(es2, "pT", [128, 512], BF16) for _ in range(4)]
            r_pT = [P.res() for _ in range(4)]
            ev = [kb.sb(es2, "ev", [128, 512], F32) for _ in range(4)]
            r_ev = [P.res() for _ in range(4)]
            sqb = kb.sb(es2, "sqb", [128, 512], BF16)
            r_sqb = P.res()
            epst = kb.sb(es2, "epst", [128, 1], F32)
            r_epst = P.res()
            P.op("dve", lambda e: e.memset(epst[:], EPS), writes=[r_epst])
            acc = [kb.sb(es2, "acc", [128, 512], F32) for _ in range(2)]
            r_acc = [P.res(), P.res()]
            onesf = kb.sb(es2, "onesf", [128, 128], F32)
            r_onesf = P.res()
            P.op("dve", lambda e: e.memset(onesf[:], 1.0), writes=[r_onesf])
            sb_i = 0
            p_i = 0
            qcnt = 0
            acnt = 0
            for qt in range(4):
                rs = qt % 2
                P.dma("sp", f"rv{rs}", rv[rs][:], Rv[qt], writes=[r_rv[rs]])
                for g in range(4):
                    a = acnt % 2
                    acnt += 1
                    P.dma("sp", f"kA{a}", kA[a][:, 0:1024], kaB[g][:, qt * 512:qt * 512 + 1024], writes=[r_kA[a]])
                    P.dma("sp", f"kA{a}", kA[a][:, 1024:1280], kaB[g][:, 2560:2816], writes=[r_kA[a]])
                    P.dma("sp", f"vP{a}", vP[a][:, 0:8 * 256], vaP[g][:, qt * 4 * 256:(qt * 4 + 8) * 256], writes=[r_vP[a]])
                    P.dma("sp", f"vP{a}", vP[a][:, 8 * 256:10 * 256], vaP[g][:, 20 * 256:22 * 256], writes=[r_vP[a]])
                    qq = qcnt % 2
                    qcnt += 1
                    P.dma("sp", f"q{qq}", qsb[qq][:], qaT[g][:, qt * 512:(qt + 1) * 512], writes=[r_q[qq]])
                    nO, nL = 4, 5
                    its = [(jt, hh) for jt in range(10) for hh in range(2)]
                    NIT = len(its)
                    LA = 3
                    stash = {}
                    for step in range(NIT + LA):
                        if step < NIT:
                            jt, hh = its[step]
                            head = 2 * g + hh
                            rows = slice(hh * 64, (hh + 1) * 64)
                            b = sb_i % 4
                            sb_i += 1
                            band = jt < 8
                            P.op("pe", lambda e, b=b, a=a, rows=rows, jt=jt, qq=qq, band=band: e.matmul(
                                kb.ps[b][:], kA[a][rows, jt * 128:(jt + 1) * 128], qsb[qq][rows, :],
                                start=True, stop=(not band)),
                                reads=[r_kA[a], r_q[qq]], writes=[kb.psr[b]])
                            if band:
                                s_j = 14 - 2 * jt
                                P.op("pe", lambda e, b=b, head=head, s_j=s_j: e.matmul(
                                    kb.ps[b][:], kb.ident, tt[:, head, s_j * 64:(s_j + 8) * 64], start=False, stop=False),
                                    reads=[r_tt, kb.r_cst], writes=[kb.psr[b]])
                                P.op("pe", lambda e, b=b, rs=rs, jt=jt: e.matmul(
                                    kb.ps[b][:], e2[:, :], rv[rs][:, jt * 512:(jt + 1) * 512], start=False, stop=True),
                                    reads=[r_e2, r_rv[rs]], writes=[kb.psr[b]])
                            pi = p_i % 4
                            p_i += 1
                            P.op("act", lambda e, pi=pi, b=b: e.activation(pT[pi][:], kb.ps[b][:], AF.Exp),
                                 reads=[kb.psr[b]], writes=[r_pT[pi]])
                            stash[step] = pi
                        if step >= LA:
                            jt, hh = its[step - LA]
                            pi = stash.pop(step - LA)
                            first = (step - LA == 0)
                            last = (step - LA == NIT - 1)
                            P.op("pe", lambda e, pi=pi, a=a, jt=jt, hh=hh, first=first, last=last: e.matmul(
                                kb.ps[nO][:], vP[a][:, (jt * 2 + hh) * 128:(jt * 2 + hh + 1) * 128], pT[pi][:], start=first, stop=last),
                                reads=[r_vP[a], r_pT[pi]], writes=[kb.psr[nO]])
                            P.op("pe", lambda e, pi=pi, hh=hh, first=first, last=last: e.matmul(
                                kb.ps[nL][:], op_[:, hh, :], pT[pi][:], start=first, stop=last),
                                reads=[r_op, r_pT[pi]], writes=[kb.psr[nL]])
                    P.op("dve", lambda e: e.reciprocal(ev[0][:], kb.ps[nL][:]), reads=[kb.psr[nL]], writes=[r_ev[0]])
                    P.op("dve", lambda e, g=g, qt=qt: e.tensor_tensor(
                        an[:, g, qt * 512:(qt + 1) * 512], kb.ps[nO][:], ev[0][:], ALU.mult),
                        reads=[kb.psr[nO], r_ev[0]], writes=[r_an[g][qt]])
            for h in range(4):
                a = h % 2
                P.dma("sp", f"kB{a}", kB_[a][:], kbT[h], writes=[r_kB[a]])
                P.dma("sp", f"vB{a}", vB_[a][:], vbT[h], writes=[r_vB[a]])
                for qt in range(4):
                    qq = qcnt % 2
                    qcnt += 1
                    P.dma("sp", f"q{qq}", qsb[qq][:], qbT[h][:, qt * 512:(qt + 1) * 512], writes=[r_q[qq]])
                    bO = [4, 5]
                    bL = [6, 7]
                    its = [(kt, m) for kt in range(66) for m in range(2)]
                    NIT = len(its)
                    LA = 3
                    stash = {}
                    for step in range(NIT + LA):
                        if step < NIT:
                            kt, m = its[step]
                            rows = slice(m * 64, (m + 1) * 64)
                            b = sb_i % 4
                            sb_i += 1
                            P.op("pe", lambda e, b=b, a=a, rows=rows, kt=kt, qq=qq: e.matmul(
                                kb.ps[b][:], kB_[a][rows, kt * 128:(kt + 1) * 128], qsb[qq][rows, :],
                                start=True, stop=True),
                                reads=[r_kB[a], r_q[qq]], writes=[kb.psr[b]])
                            pi = p_i % 4
                            p_i += 1
                            P.op("act", lambda e, pi=pi, b=b: e.activation(pT[pi][:], kb.ps[b][:], AF.Exp, scale=0.125),
                                 reads=[kb.psr[b]], writes=[r_pT[pi]])
                            stash[step] = pi
                        if step >= LA:
                            kt, m = its[step - LA]
                            pi = stash.pop(step - LA)
                            P.op("pe", lambda e, pi=pi, a=a, kt=kt, m=m: e.matmul(
                                kb.ps[bO[m]][:], vB_[a][:, kt * 128:(kt + 1) * 128], pT[pi][:], start=(kt == 0), stop=(kt == 65)),
                                reads=[r_vB[a], r_pT[pi]], writes=[kb.psr[bO[m]]])
                            if kt == 0:
                                P.op("dve", lambda e, pi=pi, m=m: e.tensor_copy(acc[m][:], pT[pi][:]),
                                     reads=[r_pT[pi]], writes=[r_acc[m]])
                            else:
                                P.op("dve", lambda e, pi=pi, m=m: e.tensor_tensor(acc[m][:], acc[m][:], pT[pi][:], ALU.add),
                                     reads=[r_pT[pi], r_acc[m]], writes=[r_acc[m]])
                    for m in range(2):
                        P.op("pe", lambda e, m=m: e.matmul(kb.ps[bL[m]][:], onesf[:], acc[m][:], start=True, stop=True),
                             reads=[r_onesf, r_acc[m]], writes=[kb.psr[bL[m]]])
                    for m in range(2):
                        P.op("dve", lambda e, m=m: e.reciprocal(ev[m][:], kb.ps[bL[m]][:]),
                             reads=[kb.psr[bL[m]]], writes=[r_ev[m]])
                        P.op("dve", lambda e, m=m: e.tensor_tensor(ev[m][:], kb.ps[bO[m]][:], ev[m][:], ALU.mult),
                             reads=[kb.psr[bO[m]], r_ev[m]], writes=[r_ev[m]])
                    P.op("dve", lambda e: e.scalar_tensor_tensor(ev[2][:], ev[1][:], neglam, ev[0][:], ALU.mult, ALU.add),
                         reads=[r_ev[0], r_ev[1], r_lsc], writes=[r_ev[2]])
                    P.op("act", lambda e: e.activation(sqb[:], ev[2][:], AF.Square), reads=[r_ev[2]], writes=[r_sqb])
                    b = sb_i % 4
                    sb_i += 1
                    P.op("pe", lambda e, b=b: e.matmul(kb.ps[b][:], kb.ones, sqb[:], start=True, stop=True),
                         reads=[r_sqb, kb.r_cst], writes=[kb.psr[b]])
                    P.op("act", lambda e, b=b: e.activation(ev[3][:], kb.ps[b][:], AF.Sqrt, bias=epst[:, 0:1], scale=1.0 / 128),
                         reads=[kb.psr[b], r_epst], writes=[r_ev[3]])
                    P.op("dve", lambda e: e.reciprocal(ev[3][:], ev[3][:]), reads=[r_ev[3]], writes=[r_ev[3]])
                    P.op("dve", lambda e: e.tensor_tensor(ev[2][:], ev[2][:], ev[3][:], ALU.mult),
                         reads=[r_ev[2], r_ev[3]], writes=[r_ev[2]])
                    P.op("act", lambda e, h=h, qt=qt: e.activation(
                        an[:, 4 + h, qt * 512:(qt + 1) * 512], ev[2][:], AF.Identity, scale=sg8[:, 0:1]),
                        reads=[r_ev[2], r_sg8], writes=[r_an[4 + h][qt]])
            P.barrier()

        with ExitStack() as es3:
            x_t = kb.sb(es3, "x", [128, NKC, TPC], F32)
            lat = make_tiles(x_t, P, TPC)
            for kc in range(NKC):
                P.dma("sp", f"xin{kc}", x_t[:, kc, :], x1T[kc], writes=[t.xres[kc] for t in lat])
            G5 = mvA[:, 0:8, 0]
            with ExitStack() as es4:
                wo = kb.sb(es4, "wo", [128, NKC, D], BF16)
                r_wo = P.res()
                P.dma("pool", "wo", wo[:], w_out.rearrange("(kc p) n -> p kc n", p=128), writes=[r_wo])
                bc = 0
                for ti, t in enumerate(lat):
                    for oc in range(NKC):
                        b = 1 + bc % 4
                        bc += 1
                        for kc in range(NKC):
                            P.op("pe", lambda e, b=b, oc=oc, kc=kc, ti=ti: e.matmul(
                                kb.ps[b][:], wo[:, kc, oc * 128:(oc + 1) * 128], an[:, kc, ti * 512:(ti + 1) * 512],
                                start=(kc == 0), stop=(kc == NKC - 1)),
                                reads=[r_wo, r_an[kc][ti]], writes=[kb.psr[b]])
                        P.op("dve", lambda e, b=b, oc=oc, t=t: e.scalar_tensor_tensor(
                            t.x[oc], kb.ps[b][:], G5[:, oc:oc + 1], t.x[oc], ALU.mult, ALU.add),
                            reads=[kb.psr[b], t.xres[oc], r_mvA], writes=[t.xres[oc]])
                P.barrier()
            A1, B1, G1, r1 = emit_ab(kb, es3, mvA, r_mvA, 0, 1, 2, ng[:, 0, :], r_ng, "f2", i_gate=3, gate_mul=0.5)
            for t in lat:
                t.A, t.B, t.HG, t.modres = A1, B1, G1, r1
            xn_ext = lambda kc, a, b: an[:, kc, a:b]
            emit_ffn(kb, [lat[0:2], lat[2:4]], wg1, wu1, wd1, xn_ext=xn_ext)
            A2, B2, G2, r2 = emit_ab(kb, es3, mvB, r_mvB, 0, 0, 1, ng[:, 1, :], r_ng, "f3", i_gate=2, gate_mul=0.5)
            for t in lat:
                t.A, t.B, t.HG, t.modres = A2, B2, G2, r2
            emit_ffn(kb, [lat[0:2], lat[2:4]], wg2, wu2, wd2, xn_ext=xn_ext)
            for kc in range(NKC):
                kb.store(x4T[kc], x_t[:, kc, :], [t.xres[kc] for t in lat], f"x{kc}")
            kb.finish()
    return kb.nc


def build_C():
    kb = KB()
    P = kb.P
    x4T = kb.inp("x4T", [NKC, 128, TPC])
    xhT = kb.inp("xhT", [128, NKC, 2])
    hmask = kb.inp("hmask", [128, 2])
    cvec = kb.inp("cvec", [128, 8, 1])
    modw = kb.inp("modw", [D, 6 * D])
    modb = kb.inp("modb", [128, 48])
    normg = kb.inp("normg", [128, 3, 8])
    cw = kb.inp("cw", [128, 3, 8])
    cw_in = kb.inp("cw_in", [D, 3 * D])
    cw_out = kb.inp("cw_out", [D, D])
    wg = kb.inp("wg", [D, DFF])
    wu = kb.inp("wu", [D, DFF])
    wd = kb.inp("wd", [DFF, D])
    cst = kb.inp("cst", [128, 3, 128])
    outT = kb.outp("outT", [NKC, 128, TPC])

    with ExitStack() as es:
        kb.load_consts(es, cst)
        x_t = kb.sb(es, "x", [128, NKC, TPC], F32)
        xh_t = kb.sb(es, "xh", [128, NKC, 2], F32)
        hm = kb.sb(es, "hm", [128, 2], F32)
        ng = kb.sb(es, "ng", [128, 3, 8], F32)
        cwt = kb.sb(es, "cwt", [128, 3, 8], F32)
        r_ng, r_hm, r_cw = P.res(), P.res(), P.res()
        P.dma("sp", "ng", ng[:], normg, writes=[r_ng])
        P.dma("sp", "hm", hm[:], hmask, writes=[r_hm])
        P.dma("sp", "cw", cwt[:], cw, writes=[r_cw])
        lat = make_tiles(x_t, P, TPC)
        halo = make_tiles(xh_t, P, 2)
        for kc in range(NKC):
            P.dma("sp", f"xin{kc}", x_t[:, kc, :], x4T[kc], writes=[t.xres[kc] for t in lat])
        P.dma("sp", "xh", xh_t[:], xhT, writes=halo[0].xres)
        mv, r_mv = emit_mod(kb, es, modw, 48, cvec, 1, modb, "mC")
        Am, Bm, _, rm = emit_ab(kb, es, mv, r_mv, 0, 0, 1, ng[:, 0, :], r_ng, "cm")
        G5 = mv[:, 16:24, 0]
        with ExitStack() as es2:
            U = kb.sb(es2, "U", [128, NKC, TPC + 2], F32)
            r_U = [[P.res() for _ in range(6)] for _ in range(NKC)]
            scr = make_scratch(kb, es2)
            xn = [kb.sb(es2, "xnc", [128, NKC, 512], BF16)] * 2
            r_xn = [P.res()] * 2
            tmpc = [kb.sb(es2, "tmpc", [128, 512], F32) for _ in range(2)]
            r_tmpc = [P.res(), P.res()]
            wi_src = cw_in.rearrange("(kc p) n -> p kc n", p=128)
            with ExitStack() as es3:
                wch = kb.sb(es3, "wch", [128, NKC, 2 * D], BF16)
                r_wch = [P.res() for _ in range(4)]
                for g4 in range(4):
                    P.dma("pool", f"wch{g4}", wch[:, :, g4 * 512:(g4 + 1) * 512],
                          wi_src[:, :, D + g4 * 512:D + (g4 + 1) * 512], writes=[r_wch[g4]])
                bc = 0
                for ti, t in enumerate(lat + halo):
                    is_h = ti >= 4
                    n = t.n
                    xb = ti % 2
                    emit_normmod(kb, scr, t, Am, Bm, rm, lambda kc, xb=xb, n=n: xn[xb][:, kc, :n], r_xn[xb])
                    for c in range(NKC):
                        b1 = 1 + bc % 4
                        b2 = 1 + (bc + 1) % 4
                        bc += 2
                        for (bb, wc) in ((b1, c), (b2, 8 + c)):
                            for kc in range(NKC):
                                P.op("pe", lambda e, bb=bb, wc=wc, kc=kc, xb=xb, n=n: e.matmul(
                                    kb.ps[bb][:, :n], wch[:, kc, wc * 128:(wc + 1) * 128], xn[xb][:, kc, :n],
                                    start=(kc == 0), stop=(kc == NKC - 1)),
                                    reads=[r_wch[wc // 4], r_xn[xb]], writes=[kb.psr[bb]])
                        q = c % 2
                        P.op("act", lambda e, q=q, b1=b1, n=n: e.activation(tmpc[q][:, :n], kb.ps[b1][:, :n], AF.Identity),
                             reads=[kb.psr[b1]], writes=[r_tmpc[q]])
                        if not is_h:
                            P.op("dve", lambda e, q=q, b2=b2, c=c, ti=ti: e.tensor_tensor(
                                U[:, c, 1 + ti * 512:1 + (ti + 1) * 512], tmpc[q][:], kb.ps[b2][:], ALU.mult),
                                reads=[r_tmpc[q], kb.psr[b2]], writes=[r_U[c][1 + ti]])
                        else:
                            P.op("dve", lambda e, q=q, b2=b2: e.tensor_tensor(
                                tmpc[q][:, 0:2], tmpc[q][:, 0:2], kb.ps[b2][:, 0:2], ALU.mult),
                                reads=[r_tmpc[q], kb.psr[b2]], writes=[r_tmpc[q]])
                            P.op("dve", lambda e, q=q, c=c: e.tensor_tensor(
                                U[:, c, 0:1], tmpc[q][:, 0:1], hm[:, 0:1], ALU.mult),
                                reads=[r_tmpc[q], r_hm], writes=[r_U[c][0]])
                            P.op("dve", lambda e, q=q, c=c: e.tensor_tensor(
                                U[:, c, TPC + 1:TPC + 2], tmpc[q][:, 1:2], hm[:, 1:2], ALU.mult),
                                reads=[r_tmpc[q], r_hm], writes=[r_U[c][5]])
                P.barrier()
            with ExitStack() as es3:
                wbg = kb.sb(es3, "wbg", [128, NKC, D], BF16)
                wo = kb.sb(es3, "wo", [128, NKC, D], BF16)
                r_wbg, r_wo = P.res(), P.res()
                P.dma("pool", "wbg", wbg[:], wi_src[:, :, 0:D], writes=[r_wbg])
                P.dma("pool", "wo", wo[:], cw_out.rearrange("(kc p) n -> p kc n", p=128), writes=[r_wo])
                Z = [kb.sb(es3, "Z", [128, NKC, 512], BF16)] * 2
                r_Z = [[P.res() for _ in range(NKC)]] * 2
                bc = 0
                for ti, t in enumerate(lat):
                    xb = ti % 2
                    zb = ti % 2
                    emit_normmod(kb, scr, t, Am, Bm, rm, lambda kc, xb=xb: xn[xb][:, kc, :], r_xn[xb])
                    for c in range(NKC):
                        b = 1 + bc % 4
                        bc += 1
                        for kc in range(NKC):
                            P.op("pe", lambda e, b=b, c=c, kc=kc, xb=xb: e.matmul(
                                kb.ps[b][:], wbg[:, kc, c * 128:(c + 1) * 128], xn[xb][:, kc, :],
                                start=(kc == 0), stop=(kc == NKC - 1)),
                                reads=[r_wbg, r_xn[xb]], writes=[kb.psr[b]])
                        q = c % 2
                        base = 1 + ti * 512
                        ru = [r_U[c][k] for k in range(6)]
                        P.op("dve", lambda e, q=q, c=c, base=base: e.tensor_scalar_mul(
                            tmpc[q][:], U[:, c, base:base + 512], cwt[:, 1, c:c + 1]),
                            reads=ru + [r_cw], writes=[r_tmpc[q]])
                        P.op("dve", lambda e, q=q, c=c, base=base: e.scalar_tensor_tensor(
                            tmpc[q][:], U[:, c, base - 1:base + 511], cwt[:, 0, c:c + 1], tmpc[q][:], ALU.mult, ALU.add),
                            reads=ru + [r_cw, r_tmpc[q]], writes=[r_tmpc[q]])
                        P.op("dve", lambda e, q=q, c=c, base=base: e.scalar_tensor_tensor(
                            tmpc[q][:], U[:, c, base + 1:base + 513], cwt[:, 2, c:c + 1], tmpc[q][:], ALU.mult, ALU.add),
                            reads=ru + [r_cw, r_tmpc[q]], writes=[r_tmpc[q]])
                        P.op("dve", lambda e, q=q, c=c, b=b, zb=zb: e.tensor_tensor(
                            Z[zb][:, c, :], tmpc[q][:], kb.ps[b][:], ALU.mult),
                            reads=[r_tmpc[q], kb.psr[b]], writes=[r_Z[zb][c]])
                    for oc in range(NKC):
                        b = 1 + bc % 4
                        bc += 1
                        for kc in range(NKC):
                            P.op("pe", lambda e, b=b, oc=oc, kc=kc, zb=zb: e.matmul(
                                kb.ps[b][:], wo[:, kc, oc * 128:(oc + 1) * 128], Z[zb][:, kc, :],
                                start=(kc == 0), stop=(kc == NKC - 1)),
                                reads=[r_wo, r_Z[zb][kc]], writes=[kb.psr[b]])
                        P.op("dve", lambda e, b=b, oc=oc, t=t: e.scalar_tensor_tensor(
                            t.x[oc], kb.ps[b][:], G5[:, oc:oc + 1], t.x[oc], ALU.mult, ALU.add),
                            reads=[kb.psr[b], t.xres[oc], r_mv], writes=[t.xres[oc]])
                P.barrier()
        A1, B1, G1, r1 = emit_ab(kb, es, mv, r_mv, 0, 3, 4, ng[:, 1, :], r_ng, "f4", i_gate=5, gate_mul=0.5)
        for t in lat:
            t.A, t.B, t.HG, t.modres = A1, B1, G1, r1
        emit_ffn(kb, [lat[0:2], lat[2:4]], wg, wu, wd)
        with ExitStack() as es2:
            scr = make_scratch(kb, es2)
            ob = [kb.sb(es2, "ob", [128, 512], F32) for _ in range(4)]
            r_ob = [P.res() for _ in range(4)]
            oc_ = 0
            for ti, t in enumerate(lat):
                n = t.n
                ps_stat, r_ps = kb.ps[0], kb.psr[0]
                for kc in range(NKC):
                    s = kc % 2
                    P.op("act", lambda e, s=s, kc=kc, t=t: e.activation(scr["sq"][s][:], t.x[kc], AF.Square),
                         reads=[t.xres[kc]], writes=[scr["r_sq"][s]])
                    P.op("pe", lambda e, s=s, kc=kc: e.matmul(ps_stat[:], kb.ones, scr["sq"][s][:],
                                                              start=(kc == 0), stop=(kc == NKC - 1)),
                         reads=[scr["r_sq"][s], kb.r_cst], writes=[r_ps])
                rstd = scr["rstd"]
                P.op("act", lambda e: e.activation(rstd[:], ps_stat[:], AF.Sqrt, bias=scr["eps"][:, 0:1], scale=1.0 / D),
                     reads=[r_ps, scr["r_eps"]], writes=[scr["r_rstd"]])
                P.op("dve", lambda e: e.reciprocal(rstd[:], rstd[:]), reads=[scr["r_rstd"]], writes=[scr["r_rstd"]])
                for kc in range(NKC):
                    o = oc_ % 4
                    oc_ += 1
                    P.op("dve", lambda e, o=o, kc=kc, t=t: e.scalar_tensor_tensor(
                        ob[o][:], t.x[kc], ng[:, 2, kc:kc + 1], rstd[:], ALU.mult, ALU.mult),
                        reads=[t.xres[kc], scr["r_rstd"], r_ng], writes=[r_ob[o]])
                    kb.store(outT[kc][:, ti * 512:(ti + 1) * 512], ob[o][:], [r_ob[o]], f"ob{o}")
            kb.finish()
    return kb.nc


_CACHE = {}


def _get(name, fn):
    if name not in _CACHE:
        _CACHE[name] = fn()
    return _CACHE[name]


def _pk(v, nch):
    return np.ascontiguousarray(np.asarray(v, np.float32).reshape(nch, 128).T)


def _consts():
    c = np.zeros((128, 3, 128), np.float32)
    c[:, 0, :] = 1.0
    c[:, 1, :] = np.eye(128, dtype=np.float32)
    idx = np.arange(128)
    c[idx, 2, idx ^ 1] = 1.0
    return c


def _rope_tables(t0):
    t = np.arange(t0, t0 + TPC)
    row = (t // 64).astype(np.float64)
    col = (t % 64).astype(np.float64)
    inv = 10000.0 ** (-np.arange(16, dtype=np.float64) / 16)
    ang = np.concatenate([row[:, None] * inv, col[:, None] * inv], -1)
    cos = np.cos(ang.astype(np.float32).astype(np.float64))
    sin = np.sin(ang.astype(np.float32).astype(np.float64))
    ang32 = np.concatenate([row[:, None].astype(np.float32) * inv.astype(np.float32),
                            col[:, None].astype(np.float32) * inv.astype(np.float32)], -1)
    cos = np.cos(ang32)
    sin = np.sin(ang32)
    tab = np.zeros((128, 2, TPC), np.float32)
    for p in range(128):
        d = p % 64
        i = d // 2
        tab[p, 0] = cos[:, i]
        tab[p, 1] = (-sin[:, i]) if d % 2 == 0 else sin[:, i]
    return tab


def _bias_table(rpb):
    rpb = np.asarray(rpb, np.float32)
    qc = np.arange(64)
    kc = np.arange(64)
    col_start = np.clip(qc - 8, 0, 48)
    cmask = (kc[None, :] >= col_start[:, None]) & (kc[None, :] < col_start[:, None] + 16)
    dc = np.clip(kc[None, :] - qc[:, None] + 15, 0, 30)
    TT = np.zeros((128, 8, 22, 64), np.float32)
    for half in range(2):
        for jj in range(22):
            dr = 10 - jj + half
            if -7 <= dr <= 7:
                val = rpb[:, dr + 7, :][:, dc]
            else:
                val = np.zeros((8, 64, 64), np.float32)
            val = np.where(cmask[None], val, NEG)
            TT[half * 64:(half + 1) * 64, :, jj, :] = np.transpose(val, (2, 0, 1))
    return TT.reshape(128, 8, 22 * 64).astype(NPBF)


def _row_valid(r0):
    Rv = np.zeros((4, 2, 8, 8, 64), np.float32)
    for qt in range(4):
        R0 = r0 + 8 * qt
        for j in range(8):
            for a in range(2):
                kr = R0 - 4 + 2 * j + a
                for i in range(8):
                    qr = R0 + i
                    st = min(max(qr - 4, 0), 120)
                    ok = (st <= kr < st + 8)
                    Rv[qt, a, j, i, :] = 0.0 if ok else NEG
    return Rv.reshape(4, 2, 8 * 512).astype(NPBF)


def _run(nc, in_maps):
    res = run_bass_kernel_spmd(nc, in_maps, core_ids=list(range(8)))
    return res.results


def kernel(x, c, ctx, c_ctx, mod_w, mod_b, norm_g, ffn_w_gate, ffn_w_up, ffn_w_down,
           attn_w_in, attn_w_out, na_rpb, diff_lambda, diff_subln_g,
           conv_w_in, conv_w_out, conv_w, final_g):
    f32 = lambda a: np.ascontiguousarray(np.asarray(a, np.float32))
    x, c, ctx, c_ctx = f32(x), f32(c), f32(ctx), f32(c_ctx)
    mod_w, mod_b, norm_g = f32(mod_w), f32(mod_b), f32(norm_g)
    cst = _consts()
    NCORE = 8

    ncA = build_A()
    mapsA = []
    wA = dict(modw=f32(mod_w[0][:, :5 * D]), modb=_pk(mod_b[0][:5 * D], 40),
              normg=f32(np.stack([_pk(norm_g[0, 0], 8), _pk(norm_g[0, 1], 8)], 1)),
              wg=f32(ffn_w_gate[0, 0]), wu=f32(ffn_w_up[0, 0]), wd=f32(ffn_w_down[0, 0]),
              w_in=f32(attn_w_in[0]), cst=cst)
    for core in range(NCORE):
        b, t0 = core // 4, (core % 4) * TPC
        xs = x[b, t0:t0 + TPC]
        xT = f32(xs.T.reshape(NKC, 128, TPC))
        cxT = f32(ctx[b].T.reshape(NKC, 128, 256))
        cv = np.stack([c[b].reshape(8, 128).T, c_ctx.reshape(8, 128).T], -1)
        mapsA.append(dict(xT=xT, cxT=cxT, cvec=f32(cv), rope=_rope_tables(t0), **wA))
    resA = _run(ncA, mapsA)

    ncB = build_B()
    mapsB = []
    E2 = np.zeros((2, 128), np.float32)
    E2[0, :64] = 1.0
    E2[1, 64:] = 1.0
    onesP = np.zeros((128, 2, 128), np.float32)
    onesP[:, 0, :64] = 1.0
    onesP[:, 1, 64:] = 1.0
    TTh = _bias_table(na_rpb[0])
    lamv = f32(np.broadcast_to(np.asarray(diff_lambda[0], np.float32)[None], (128, 4, 64)))
    modwB = np.zeros((D, 4 * D), np.float32)
    modwB[:, :3 * D] = mod_w[1][:, :3 * D]
    modbB = np.zeros((128, 32), np.float32)
    modbB[:, :24] = _pk(mod_b[1][:3 * D], 24)
    wB = dict(TT=TTh, E2=E2.astype(NPBF), onesP=onesP.astype(NPBF), cst=cst, lamv=lamv,
              sublng=f32(np.asarray(diff_subln_g[0], np.float32).reshape(128, 1)),
              modwA=f32(mod_w[0][:, 5 * D:]), modbA=_pk(mod_b[0][5 * D:], 32),
              modwB=modwB, modbB=modbB,
              normg=f32(np.stack([_pk(norm_g[0, 2], 8), _pk(norm_g[1, 0], 8)], 1)),
              w_out=f32(attn_w_out[0]),
              wg1=f32(ffn_w_gate[0, 1]), wu1=f32(ffn_w_up[0, 1]), wd1=f32(ffn_w_down[0, 1]),
              wg2=f32(ffn_w_gate[1, 0]), wu2=f32(ffn_w_up[1, 0]), wd2=f32(ffn_w_down[1, 0]))
    per_batch = {}
    for b in range(2):
        cores = [4 * b + i for i in range(4)]
        qk = [np.asarray(resA[cc]["qkT"]) for cc in cores]
        vt = [np.asarray(resA[cc]["vtm"]).reshape(TPC, 1024) for cc in cores]
        kc_ = np.asarray(resA[cores[0]]["kcT"])
        vc_ = np.asarray(resA[cores[0]]["vc"]).reshape(256, 1024)
        ka_full = np.concatenate([q[4:8] for q in qk], axis=2)
        z = np.zeros((4, 128, 256), ka_full.dtype)
        ka_pad = np.concatenate([z, ka_full, z], axis=2)
        va_full = np.concatenate([v[:, 0:512] for v in vt], axis=0)
        zv = np.zeros((256, 512), va_full.dtype)
        va_pad = np.concatenate([zv, va_full, zv], axis=0)
        kb_full = np.concatenate([q[12:16] for q in qk] , axis=2)
        kbT = np.ascontiguousarray(np.concatenate([kb_full, kc_[4:8]], axis=2))
        vb_full = np.concatenate([v[:, 512:1024] for v in vt] + [vc_[:, 512:1024]], axis=0)
        vbT = np.ascontiguousarray(vb_full.reshape(66, 128, 4, 128).transpose(2, 1, 0, 3).reshape(4, 128, 66 * 128))
        per_batch[b] = (ka_pad, va_pad, kc_, vc_, kbT, vbT)
    for core in range(NCORE):
        b, t0 = core // 4, (core % 4) * TPC
        ka_pad, va_pad, kc_, vc_, kbT, vbT = per_batch[b]
        qk = np.asarray(resA[core]["qkT"])
        kaB = np.ascontiguousarray(np.concatenate([ka_pad[:, :, t0:t0 + 2560], kc_[0:4]], axis=2))
        vband = np.concatenate([va_pad[t0:t0 + 2560], vc_[:, 0:512]], axis=0).reshape(2816, 8, 64)
        vP = np.zeros((2816, 8, 128), vband.dtype)
        vP[:, 0::2, 0:64] = vband[:, 0::2]
        vP[:, 1::2, 64:128] = vband[:, 1::2]
        vaP = np.ascontiguousarray(vP.reshape(22, 128, 4, 256).transpose(2, 1, 0, 3).reshape(4, 128, 22 * 256))
        cv = c[b].reshape(8, 128).T[:, :, None]
        mapsB.append(dict(qaT=np.ascontiguousarray(qk[0:4]), qbT=np.ascontiguousarray(qk[8:12]),
                          kaB=kaB, vaP=vaP, kbT=kbT, vbT=vbT, Rv=_row_valid((core % 4) * 32),
                          x1T=np.asarray(resA[core]["x1T"]), cvec=f32(cv), **wB))
    resB = _run(ncB, mapsB)

    ncC = build_C()
    mapsC = []
    wC = dict(modw=f32(mod_w[1][:, 3 * D:]), modb=_pk(mod_b[1][3 * D:], 48),
              normg=f32(np.stack([_pk(norm_g[1, 1], 8), _pk(norm_g[1, 2], 8), _pk(final_g, 8)], 1)),
              cw=f32(np.stack([_pk(np.asarray(conv_w[0], np.float32)[k], 8) for k in range(3)], 1)),
              cw_in=f32(conv_w_in[0]), cw_out=f32(conv_w_out[0]),
              wg=f32(ffn_w_gate[1, 1]), wu=f32(ffn_w_up[1, 1]), wd=f32(ffn_w_down[1, 1]), cst=cst)
    x4 = [np.asarray(resB[cc]["x4T"]) for cc in range(NCORE)]
    for core in range(NCORE):
        b = core // 4
        xh = np.zeros((128, NKC, 2), np.float32)
        hmask = np.zeros((128, 2), np.float32)
        if core % 4 != 0:
            xh[:, :, 0] = x4[core - 1][:, :, TPC - 1].T
            hmask[:, 0] = 1.0
        if core % 4 != 3:
            xh[:, :, 1] = x4[core + 1][:, :, 0].T
            hmask[:, 1] = 1.0
        cv = c[b].reshape(8, 128).T[:, :, None]
        mapsC.append(dict(x4T=x4[core], xhT=xh, hmask=hmask, cvec=f32(cv), **wC))
    resC = _run(ncC, mapsC)

    out = np.zeros((2, 8192, D), np.float32)
    for core in range(NCORE):
        b, t0 = core // 4, (core % 4) * TPC
        oT = np.asarray(resC[core]["outT"]).reshape(D, TPC)
        out[b, t0:t0 + TPC] = oT.T
    return out
```

```python
import math
from contextlib import ExitStack

import numpy as np
import ml_dtypes
import concourse.bass as bass
import concourse.mybir as mybir
from concourse.bass_utils import run_bass_kernel_spmd

F32 = mybir.dt.float32
BF16 = mybir.dt.bfloat16
ALU = mybir.AluOpType
AF = mybir.ActivationFunctionType
NPBF = ml_dtypes.bfloat16

D = 1024
DFF = 2816
NKC = 8
NFC = 22
TPC = 2048
EPS = 1e-6
NEG = -30000.0
DEBUG_STOP = 0

ENGS = ["pe", "act", "dve", "pool", "sp"]
SEM_CHUNK = 30000


class Res:
    __slots__ = ("name", "w", "r")

    def __init__(self, name):
        self.name = name
        self.w = None
        self.r = []


class Op:
    __slots__ = ("eng", "fn", "deps", "signal", "sig", "dma_key", "dma_val", "dma_inc")

    def __init__(self, eng, fn):
        self.eng = eng
        self.fn = fn
        self.deps = []
        self.signal = False
        self.sig = None
        self.dma_key = None
        self.dma_val = 0
        self.dma_inc = 16


class Prog:
    def __init__(self, nc):
        self.nc = nc
        self.q = {e: [] for e in ENGS}
        self.dma_cnt = {}
        self.nres = 0
        self.pending = {e: [] for e in ENGS}
        self.dmas = []

    def res(self, name=None):
        self.nres += 1
        return Res(name or f"r{self.nres}")

    def _add(self, eng, fn, reads, writes, dma_key=None, inc=16):
        op = Op(eng, fn)
        deps = []
        for r in reads:
            if r.w is not None:
                deps.append(r.w)
        for w in writes:
            if w.w is not None:
                deps.append(w.w)
            deps.extend(w.r)
        if self.pending[eng]:
            deps.extend(self.pending[eng])
            self.pending[eng] = []
        seen = set()
        for d in deps:
            if id(d) in seen:
                continue
            seen.add(id(d))
            if d.dma_key is None:
                if eng == "pe" and d.eng == "pe":
                    continue
                d.signal = True
            op.deps.append(d)
        if dma_key is not None:
            op.dma_key = dma_key
            self.dma_cnt[dma_key] = self.dma_cnt.get(dma_key, 0) + inc
            op.dma_val = self.dma_cnt[dma_key]
            op.dma_inc = inc
            self.dmas.append(op)
        for r in reads:
            r.r.append(op)
        for w in writes:
            w.w = op
            w.r = []
        self.q[eng].append(op)
        return op

    def op(self, eng, fn, reads=(), writes=()):
        return self._add(eng, fn, reads, writes)

    def dma(self, eng, key, out, in_, reads=(), writes=()):
        return self._add(eng, lambda e: e.dma_start(out=out, in_=in_), reads, writes, dma_key=key)

    def barrier(self):
        lasts = []
        for e in ENGS:
            for op in reversed(self.q[e]):
                if op.dma_key is None:
                    lasts.append(op)
                    break
        lasts.extend(self.dmas)
        self.dmas = []
        for e in ENGS:
            self.pending[e] = list(lasts)

    def emit(self):
        nc = self.nc
        sems = {}

        def sem(name):
            if name not in sems:
                sems[name] = nc.alloc_semaphore(name)
            return sems[name]

        for e in ENGS:
            k = 0
            for op in self.q[e]:
                if op.dma_key is None and op.signal:
                    op.sig = (f"s_{e}_{k // SEM_CHUNK}", k % SEM_CHUNK + 1)
                    k += 1
                elif op.dma_key is not None:
                    op.sig = (f"d_{op.dma_key}", op.dma_val)

        def run(e, engine):
            waited = {}
            for op in self.q[e]:
                need = {}
                for d in op.deps:
                    s, v = d.sig
                    if waited.get(s, 0) >= v:
                        continue
                    if need.get(s, 0) < v:
                        need[s] = v
                for s, v in need.items():
                    engine.wait_ge(sem(s), v)
                    waited[s] = v
                ins = op.fn(engine)
                if op.dma_key is not None:
                    ins.then_inc(sem(op.sig[0]), op.dma_inc)
                elif op.signal:
                    ins.then_inc(sem(op.sig[0]), 1)

        with nc.Block() as block:
            @block.tensor
            def _(eng):
                run("pe", eng)

            @block.scalar
            def _(eng):
                run("act", eng)

            @block.vector
            def _(eng):
                run("dve", eng)

            @block.gpsimd
            def _(eng):
                run("pool", eng)

            @block.sync
            def _(eng):
                run("sp", eng)


class KB:
    def __init__(self):
        self.nc = bass.Bass("TRN2", target_bir_lowering=False)
        self.P = Prog(self.nc)
        self.ps = [self.nc.alloc_psum_tensor(f"psb{i}", [128, 512], F32) for i in range(8)]
        self.psr = [self.P.res(f"psb{i}") for i in range(8)]
        self.out_res = []
        self.uid = 0

    def inp(self, name, shape, dt=F32):
        return self.nc.dram_tensor(name, list(shape), dt, kind="ExternalInput").ap()

    def outp(self, name, shape, dt=F32):
        return self.nc.dram_tensor(name, list(shape), dt, kind="ExternalOutput").ap()

    def sb(self, es, name, shape, dt):
        self.uid += 1
        return es.enter_context(self.nc.sbuf_tensor(f"{name}_{self.uid}", list(shape), dt))

    def store(self, dram_ap, sb_ap, reads, key):
        r = self.P.res()
        self.P.dma("sp", "st_" + key, dram_ap, sb_ap, reads=reads, writes=[r])
        self.out_res.append(r)

    def finish(self):
        self.P.op("sp", lambda e: e.nop(), reads=self.out_res)
        self.P.emit()
        return self.nc

    def load_consts(self, es, cst_ap):
        P = self.P
        self.cst = self.sb(es, "cst", [128, 3, 128], BF16)
        self.r_cst = P.res("cst")
        P.dma("pool", "cst", self.cst[:], cst_ap, writes=[self.r_cst])
        self.ones = self.cst[:, 0, :]
        self.ident = self.cst[:, 1, :]
        self.swap = self.cst[:, 2, :]


class Tile:
    def __init__(self, x, xres, n, A=None, B=None, HG=None, modres=None):
        self.x = x
        self.xres = xres
        self.n = n
        self.A = A
        self.B = B
        self.HG = HG
        self.modres = modres
        self.off = 0


def emit_mod(kb, es, modw_ap, nch, cvec_ap, nv, modb_ap, name):
    P = kb.P
    nc = kb.nc
    assert nch % 4 == 0
    cv = kb.sb(es, name + "cv", [128, 8, nv], F32)
    sc = kb.sb(es, name + "sc", [128, 8, nv], BF16)
    mb = kb.sb(es, name + "mb", [128, nch], F32)
    mvec = kb.sb(es, name + "mvec", [128, nch, nv], F32)
    r_cv, r_sc, r_mb, r_mvec = P.res(), P.res(), P.res(), P.res()
    P.dma("sp", name + "cv", cv[:], cvec_ap, writes=[r_cv])
    P.dma("sp", name + "mb", mb[:], modb_ap, writes=[r_mb])
    P.op("act", lambda e: e.activation(sc[:], cv[:], AF.Silu), reads=[r_cv], writes=[r_sc])
    psm = kb.ps[7]
    r_psm = kb.psr[7]
    with ExitStack() as es2:
        mw = [kb.sb(es2, name + f"mw{s}", [128, 8, 512], BF16) for s in range(2)]
        r_mw = [P.res(), P.res()]
        src = modw_ap.rearrange("(kc p) n -> p kc n", p=128)
        for g in range(nch // 4):
            s = g % 2
            P.dma("pool", name + f"mw{s}", mw[s][:], src[:, :, g * 512:(g + 1) * 512], writes=[r_mw[s]])
            for c4 in range(4):
                oc = g * 4 + c4
                for kc in range(8):
                    P.op("pe", lambda e, s=s, c4=c4, oc=oc, kc=kc: e.matmul(
                        psm[:, oc * nv:(oc + 1) * nv], mw[s][:, kc, c4 * 128:(c4 + 1) * 128], sc[:, kc, :],
                        start=(kc == 0), stop=(kc == 7)), reads=[r_mw[s], r_sc], writes=[r_psm])
        for v in range(nv):
            P.op("dve", lambda e, v=v: e.tensor_tensor(
                mvec[:, :, v], psm[:, 0:nch * nv].rearrange("p (c v) -> p c v", v=nv)[:, :, v], mb[:], ALU.add),
                reads=[r_psm, r_mb], writes=[r_mvec])
        P.barrier()
    return mvec, r_mvec


def emit_ab(kb, es, mvec, r_mvec, v, i_shift, i_scale, g_ap, r_g, name, i_gate=None, gate_mul=1.0):
    P = kb.P
    A = kb.sb(es, name + "A", [128, 8], F32)
    r = P.res()
    P.op("dve", lambda e: e.scalar_tensor_tensor(
        A[:], mvec[:, i_scale * 8:(i_scale + 1) * 8, v], 1.0, g_ap, ALU.add, ALU.mult),
        reads=[r_mvec, r_g], writes=[r])
    Bm = mvec[:, i_shift * 8:(i_shift + 1) * 8, v]
    G = None
    if i_gate is not None:
        G = kb.sb(es, name + "G", [128, 8], F32)
        P.op("dve", lambda e: e.tensor_scalar_mul(G[:], mvec[:, i_gate * 8:(i_gate + 1) * 8, v], float(gate_mul)),
             reads=[r_mvec], writes=[r])
    return A, Bm, G, r


def emit_normmod(kb, scr, t, A, Bv, r_ab, out_fn, r_out):
    P = kb.P
    n = t.n
    ps_stat, r_ps = kb.ps[0], kb.psr[0]
    for kc in range(NKC):
        s = kc % 2
        P.op("act", lambda e, s=s, kc=kc: e.activation(scr["sq"][s][:, :n], t.x[kc], AF.Square),
             reads=[t.xres[kc]], writes=[scr["r_sq"][s]])
        P.op("pe", lambda e, s=s, kc=kc: e.matmul(ps_stat[:, :n], kb.ones, scr["sq"][s][:, :n],
                                                  start=(kc == 0), stop=(kc == NKC - 1)),
             reads=[scr["r_sq"][s], kb.r_cst], writes=[r_ps])
    rstd = scr["rstd"]
    P.op("act", lambda e: e.activation(rstd[:, :n], ps_stat[:, :n], AF.Sqrt, bias=scr["eps"][:, 0:1], scale=1.0 / D),
         reads=[r_ps, scr["r_eps"]], writes=[scr["r_rstd"]])
    P.op("dve", lambda e: e.reciprocal(rstd[:, :n], rstd[:, :n]), reads=[scr["r_rstd"]], writes=[scr["r_rstd"]])
    for kc in range(NKC):
        s = kc % 2
        P.op("dve", lambda e, s=s, kc=kc: e.tensor_tensor(scr["tmp"][s][:, :n], t.x[kc], rstd[:, :n], ALU.mult),
             reads=[t.xres[kc], scr["r_rstd"]], writes=[scr["r_tmp"][s]])
        P.op("act", lambda e, s=s, kc=kc: e.activation(out_fn(kc), scr["tmp"][s][:, :n], AF.Identity,
                                                       bias=Bv[:, kc:kc + 1], scale=A[:, kc:kc + 1]),
             reads=[scr["r_tmp"][s], r_ab], writes=[r_out])


def make_scratch(kb, es):
    P = kb.P
    scr = {
        "sq": [kb.sb(es, "sq", [128, 512], BF16) for _ in range(2)],
        "r_sq": [P.res(), P.res()],
        "rstd": kb.sb(es, "rstd", [128, 512], F32),
        "r_rstd": P.res(),
        "tmp": [kb.sb(es, "tmp", [128, 512], F32) for _ in range(2)],
        "r_tmp": [P.res(), P.res()],
        "eps": kb.sb(es, "eps", [128, 1], F32),
        "r_eps": P.res(),
    }
    P.op("dve", lambda e: e.memset(scr["eps"][:], EPS), writes=[scr["r_eps"]])
    return scr


def emit_ffn(kb, groups, wg_ap, wu_ap, wd_ap, xn_ext=None):
    P = kb.P
    with ExitStack() as es:
        maxw = max(sum(t.n for t in g) for g in groups)
        if xn_ext is None:
            xn_t = kb.sb(es, "ffn_xn", [128, NKC, maxw], BF16)
            xn = lambda kc, a, b: xn_t[:, kc, a:b]
        else:
            xn = xn_ext
        H = kb.sb(es, "ffn_H", [128, NFC, maxw], BF16)
        wgb = [kb.sb(es, "wg", [128, NKC, 256], BF16) for _ in range(2)]
        wub = [kb.sb(es, "wu", [128, NKC, 256], BF16) for _ in range(2)]
        wdb = [kb.sb(es, "wd", [128, NFC, 256], BF16) for _ in range(2)]
        sg = [kb.sb(es, "sg", [128, 512], F32) for _ in range(2)]
        r_wg, r_wu, r_wd, r_sg = ([P.res(), P.res()] for _ in range(4))
        scr = make_scratch(kb, es)
        wg_src = wg_ap.rearrange("(kc p) n -> p kc n", p=128)
        wu_src = wu_ap.rearrange("(kc p) n -> p kc n", p=128)
        wd_src = wd_ap.rearrange("(kc p) n -> p kc n", p=128)
        gu_banks = [(1, 2), (3, 4)]
        y_banks = [5, 6]
        cnt = 0
        ycnt = 0
        wcnt = 0
        dcnt = 0
        for g in groups:
            off = 0
            for t in g:
                t.off = off
                off += t.n
            r_xn = {id(t): P.res() for t in g}
            r_H = {(id(t), j): P.res() for t in g for j in range(NFC)}
            for t in g:
                emit_normmod(kb, scr, t, t.A, t.B, t.modres,
                             lambda kc, o=t.off, n=t.n: xn(kc, o, o + n), r_xn[id(t)])
            for j2 in range(NFC // 2):
                s = wcnt % 2
                wcnt += 1
                P.dma("pool", f"wg{s}", wgb[s][:], wg_src[:, :, j2 * 256:(j2 + 1) * 256], writes=[r_wg[s]])
                P.dma("pool", f"wu{s}", wub[s][:], wu_src[:, :, j2 * 256:(j2 + 1) * 256], writes=[r_wu[s]])
                for jj in range(2):
                    j = j2 * 2 + jj
                    for t in g:
                        n = t.n
                        bg_, bu_ = gu_banks[cnt % 2]
                        q = cnt % 2
                        cnt += 1
                        for kc in range(NKC):
                            P.op("pe", lambda e, s=s, jj=jj, kc=kc, o=t.off, n=n, bg_=bg_: e.matmul(
                                kb.ps[bg_][:, :n], wgb[s][:, kc, jj * 128:(jj + 1) * 128], xn(kc, o, o + n),
                                start=(kc == 0), stop=(kc == NKC - 1)),
                                reads=[r_wg[s], r_xn[id(t)]], writes=[kb.psr[bg_]])
                        for kc in range(NKC):
                            P.op("pe", lambda e, s=s, jj=jj, kc=kc, o=t.off, n=n, bu_=bu_: e.matmul(
                                kb.ps[bu_][:, :n], wub[s][:, kc, jj * 128:(jj + 1) * 128], xn(kc, o, o + n),
                                start=(kc == 0), stop=(kc == NKC - 1)),
                                reads=[r_wu[s], r_xn[id(t)]], writes=[kb.psr[bu_]])
                        P.op("act", lambda e, q=q, n=n, bg_=bg_: e.activation(sg[q][:, :n], kb.ps[bg_][:, :n], AF.Silu),
                             reads=[kb.psr[bg_]], writes=[r_sg[q]])
                        P.op("dve", lambda e, q=q, n=n, bu_=bu_, j=j, o=t.off: e.tensor_tensor(
                            H[:, j, o:o + n], sg[q][:, :n], kb.ps[bu_][:, :n], ALU.mult),
                            reads=[r_sg[q], kb.psr[bu_]], writes=[r_H[(id(t), j)]])
            for i4 in range(4):
                s = dcnt % 2
                dcnt += 1
                P.dma("pool", f"wd{s}", wdb[s][:], wd_src[:, :, i4 * 256:(i4 + 1) * 256], writes=[r_wd[s]])
                for ii in range(2):
                    i = i4 * 2 + ii
                    for t in g:
                        n = t.n
                        by = y_banks[ycnt % 2]
                        ycnt += 1
                        for kc in range(NFC):
                            P.op("pe", lambda e, s=s, ii=ii, kc=kc, o=t.off, n=n, by=by: e.matmul(
                                kb.ps[by][:, :n], wdb[s][:, kc, ii * 128:(ii + 1) * 128], H[:, kc, o:o + n],
                                start=(kc == 0), stop=(kc == NFC - 1)),
                                reads=[r_wd[s], r_H[(id(t), kc)]], writes=[kb.psr[by]])
                        P.op("dve", lambda e, i=i, t=t, n=n, by=by, hg=t.HG: e.scalar_tensor_tensor(
                            t.x[i], kb.ps[by][:, :n], hg[:, i:i + 1], t.x[i], ALU.mult, ALU.add),
                            reads=[kb.psr[by], t.xres[i], t.modres], writes=[t.xres[i]])
        P.barrier()


def make_tiles(x_t, P, ntok, width=512):
    tiles = []
    for a in range(0, ntok, width):
        n = min(width, ntok - a)
        tiles.append(Tile([x_t[:, kc, a:a + n] for kc in range(NKC)], [P.res() for _ in range(NKC)], n))
    return tiles


def build_A():
    kb = KB()
    P = kb.P
    xT = kb.inp("xT", [NKC, 128, TPC])
    cxT = kb.inp("cxT", [NKC, 128, 256])
    cvec = kb.inp("cvec", [128, 8, 2])
    modw = kb.inp("modw", [D, 5 * D])
    modb = kb.inp("modb", [128, 40])
    normg = kb.inp("normg", [128, 2, 8])
    wg = kb.inp("wg", [D, DFF])
    wu = kb.inp("wu", [D, DFF])
    wd = kb.inp("wd", [DFF, D])
    w_in = kb.inp("w_in", [D, 3 * D])
    cst = kb.inp("cst", [128, 3, 128])
    rope = kb.inp("rope", [128, 2, TPC])
    x1T = kb.outp("x1T", [NKC, 128, TPC])
    if DEBUG_STOP in (0, 3, 4, 5, 6, 7):
        qkT = kb.outp("qkT", [16, 128, TPC], BF16)
        vtm = kb.outp("vtm", [16, 128, 1024], BF16)
        kcT = kb.outp("kcT", [8, 128, 256], BF16)
        vc = kb.outp("vc", [2, 128, 1024], BF16)

    with ExitStack() as es:
        kb.load_consts(es, cst)
        x_t = kb.sb(es, "x", [128, NKC, TPC], F32)
        xc_t = kb.sb(es, "xc", [128, NKC, 256], F32)
        ng = kb.sb(es, "ng", [128, 2, 8], F32)
        r_ng = P.res()
        P.dma("sp", "ng", ng[:], normg, writes=[r_ng])
        lat = make_tiles(x_t, P, TPC)
        ctx = make_tiles(xc_t, P, 256)
        for kc in range(NKC):
            for t in lat:
                pass
            P.dma("sp", f"xin{kc}", x_t[:, kc, :], xT[kc], writes=[t.xres[kc] for t in lat])
            P.dma("sp", f"xcin{kc}", xc_t[:, kc, :], cxT[kc], writes=[ctx[0].xres[kc]])
        if DEBUG_STOP == -1:
            for kc in range(NKC):
                kb.store(x1T[kc], x_t[:, kc, :], [t.xres[kc] for t in lat], f"x{kc}")
            kb.finish()
            return kb.nc
        mvec, r_mvec = emit_mod(kb, es, modw, 40, cvec, 2, modb, "m0")
        if DEBUG_STOP == -2:
            for kc in range(NKC):
                kb.store(x1T[kc], x_t[:, kc, :], [t.xres[kc] for t in lat], f"x{kc}")
            kb.finish()
            return kb.nc
        ab = {}
        for v in range(2):
            A1, B1, G1, r1 = emit_ab(kb, es, mvec, r_mvec, v, 0, 1, ng[:, 0, :], r_ng, f"f1v{v}", i_gate=2, gate_mul=0.5)
            A2, B2, _, r2 = emit_ab(kb, es, mvec, r_mvec, v, 3, 4, ng[:, 1, :], r_ng, f"qkv{v}")
            ab[v] = (A1, B1, G1, r1, A2, B2, r2)
        for t in lat:
            t.A, t.B, t.HG, t.modres = ab[0][0], ab[0][1], ab[0][2], ab[0][3]
        for t in ctx:
            t.A, t.B, t.HG, t.modres = ab[1][0], ab[1][1], ab[1][2], ab[1][3]
        if DEBUG_STOP != 1:
            emit_ffn(kb, [lat[0:2] + ctx, lat[2:4]], wg, wu, wd)
        for kc in range(NKC):
            kb.store(x1T[kc], x_t[:, kc, :], [t.xres[kc] for t in lat], f"x{kc}")

        if DEBUG_STOP in (1, 2):
            kb.finish()
            return kb.nc
        with ExitStack() as es2:
            wi = kb.sb(es2, "wi", [128, NKC, 3 * D], BF16)
            r_wi = [P.res() for _ in range(6)]
            wi_src = w_in.rearrange("(kc p) n -> p kc n", p=128)
            for g6 in range(6):
                P.dma("pool", f"wi{g6}", wi[:, :, g6 * 512:(g6 + 1) * 512], wi_src[:, :, g6 * 512:(g6 + 1) * 512],
                      writes=[r_wi[g6]])
            rp = kb.sb(es2, "rope", [128, 2, TPC], F32)
            r_rp = P.res()
            P.dma("sp", "rope", rp[:], rope, writes=[r_rp])
            scr = make_scratch(kb, es2)
            xn2 = [kb.sb(es2, "xn2", [128, NKC, 512], BF16) for _ in range(2)]
            r_xn2 = [P.res(), P.res()]
            stg = [kb.sb(es2, "stg", [128, 512], BF16) for _ in range(4)]
            r_stg = [P.res() for _ in range(4)]
            qs = [kb.sb(es2, "qs", [128, 512], BF16) for _ in range(2)]
            r_qs = [P.res(), P.res()]
            t1 = [kb.sb(es2, "t1", [128, 512], F32) for _ in range(2)]
            r_t1 = [P.res(), P.res()]
            t2 = [kb.sb(es2, "t2", [128, 512], F32) for _ in range(2)]
            r_t2 = [P.res(), P.res()]
            banks = [1, 2, 3, 4]
            bc = 0
            sc_ = 0
            rc = 0
            for ti, t in enumerate(lat + ctx):
                is_ctx = ti >= len(lat)
                v = 1 if is_ctx else 0
                n = t.n
                tok0 = 0 if is_ctx else ti * 512
                xb = ti % 2
                emit_normmod(kb, scr, t, ab[v][4], ab[v][5], ab[v][6],
                             lambda kc, xb=xb, n=n: xn2[xb][:, kc, :n], r_xn2[xb])
                fm = [(c, c) for c in range(0, 8)] + [(12 + c, 8 + c) for c in range(8)]
                for (wc, oc) in fm:
                    kind = oc // 4
                    if is_ctx and kind in (0, 2):
                        continue
                    b = banks[bc % 4]
                    bc += 1
                    for kc in range(NKC):
                        P.op("pe", lambda e, b=b, wc=wc, kc=kc, xb=xb, n=n: e.matmul(
                            kb.ps[b][:, :n], wi[:, kc, wc * 128:(wc + 1) * 128], xn2[xb][:, kc, :n],
                            start=(kc == 0), stop=(kc == NKC - 1)),
                            reads=[r_wi[wc // 4], r_xn2[xb]], writes=[kb.psr[b]])
                    s = sc_ % 4
                    sc_ += 1
                    if kind >= 2 and not is_ctx and DEBUG_STOP not in (4, 5):
                        r = rc % 2
                        rc += 1
                        if DEBUG_STOP != 7:
                            b2 = banks[bc % 4]
                            bc += 1
                            P.op("act", lambda e, r=r, b=b, n=n: e.activation(qs[r][:, :n], kb.ps[b][:, :n], AF.Identity),
                                 reads=[kb.psr[b]], writes=[r_qs[r]])
                            P.op("pe", lambda e, r=r, b2=b2, n=n: e.matmul(kb.ps[b2][:, :n], kb.swap, qs[r][:, :n],
                                                                           start=True, stop=True),
                                 reads=[r_qs[r], kb.r_cst], writes=[kb.psr[b2]])
                        else:
                            b2 = b
                        if DEBUG_STOP != 6:
                            P.op("dve", lambda e, r=r, b=b, n=n, tok0=tok0: e.tensor_tensor(
                                t1[r][:, :n], kb.ps[b][:, :n], rp[:, 0, tok0:tok0 + n], ALU.mult),
                                reads=[kb.psr[b], r_rp] + ([r_qs[r]] if DEBUG_STOP != 7 else []), writes=[r_t1[r]])
                            P.op("dve", lambda e, r=r, b2=b2, n=n, tok0=tok0: e.tensor_tensor(
                                t2[r][:, :n], kb.ps[b2][:, :n], rp[:, 1, tok0:tok0 + n], ALU.mult),
                                reads=[kb.psr[b2], r_rp], writes=[r_t2[r]])
                            P.op("dve", lambda e, r=r, s=s, n=n: e.tensor_tensor(
                                stg[s][:, :n], t1[r][:, :n], t2[r][:, :n], ALU.add),
                                reads=[r_t1[r], r_t2[r]], writes=[r_stg[s]])
                        else:
                            P.op("act", lambda e, s=s, b2=b2, n=n: e.activation(
                                stg[s][:, :n], kb.ps[b2][:, :n], AF.Identity),
                                reads=[kb.psr[b2]], writes=[r_stg[s]])
                    else:
                        scale = 0.125 if kind == 0 else 1.0
                        P.op("act", lambda e, s=s, b=b, n=n, scale=scale: e.activation(
                            stg[s][:, :n], kb.ps[b][:, :n], AF.Identity, scale=scale),
                            reads=[kb.psr[b]], writes=[r_stg[s]])
                    if is_ctx:
                        dst = kcT[(oc - 4) if kind == 1 else (oc - 8)]
                        kb.store(dst, stg[s][:, :n], [r_stg[s]], f"stg{s}")
                    else:
                        kb.store(qkT[oc][:, tok0:tok0 + n], stg[s][:, :n], [r_stg[s]], f"stg{s}")
                for tb in range(0 if DEBUG_STOP in (3, 5, 6, 7) else n // 128):
                    for vi, wc0 in enumerate((8, 20)):
                        b = banks[bc % 4]
                        bc += 1
                        for kc in range(NKC):
                            P.op("pe", lambda e, b=b, wc0=wc0, kc=kc, xb=xb, tb=tb: e.matmul(
                                kb.ps[b][:, :], xn2[xb][:, kc, tb * 128:(tb + 1) * 128],
                                wi[:, kc, wc0 * 128:(wc0 + 4) * 128],
                                start=(kc == 0), stop=(kc == NKC - 1)),
                                reads=[r_wi[wc0 // 4], r_xn2[xb]], writes=[kb.psr[b]])
                        s = sc_ % 4
                        sc_ += 1
                        if vi == 0:
                            P.op("act", lambda e, s=s, b=b: e.activation(stg[s][:], kb.ps[b][:], AF.Identity),
                                 reads=[kb.psr[b]], writes=[r_stg[s]])
                        else:
                            P.op("dve", lambda e, s=s, b=b: e.tensor_copy(stg[s][:], kb.ps[b][:]),
                                 reads=[kb.psr[b]], writes=[r_stg[s]])
                        if is_ctx:
                            kb.store(vc[tb][:, vi * 512:(vi + 1) * 512], stg[s][:], [r_stg[s]], f"stg{s}")
                        else:
                            kb.store(vtm[ti * 4 + tb][:, vi * 512:(vi + 1) * 512], stg[s][:], [r_stg[s]], f"stg{s}")
            P.barrier()
        kb.finish()
    return kb.nc


def build_B():
    kb = KB()
    P = kb.P
    qaT = kb.inp("qaT", [4, 128, TPC], BF16)
    qbT = kb.inp("qbT", [4, 128, TPC], BF16)
    kaB = kb.inp("kaB", [4, 128, 2816], BF16)
    vaP = kb.inp("vaP", [4, 128, 22 * 256], BF16)
    kbT = kb.inp("kbT", [4, 128, 8448], BF16)
    vbT = kb.inp("vbT", [4, 128, 66 * 128], BF16)
    TT = kb.inp("TT", [128, 8, 22 * 64], BF16)
    Rv = kb.inp("Rv", [4, 2, 8 * 512], BF16)
    E2 = kb.inp("E2", [2, 128], BF16)
    onesP = kb.inp("onesP", [128, 2, 128], BF16)
    x1T = kb.inp("x1T", [NKC, 128, TPC])
    cst = kb.inp("cst", [128, 3, 128])
    lamv = kb.inp("lamv", [128, 4, 64])
    sublng = kb.inp("sublng", [128, 1])
    cvec = kb.inp("cvec", [128, 8, 1])
    modwA = kb.inp("modwA", [D, 4 * D])
    modbA = kb.inp("modbA", [128, 32])
    modwB = kb.inp("modwB", [D, 4 * D])
    modbB = kb.inp("modbB", [128, 32])
    normg = kb.inp("normg", [128, 2, 8])
    w_out = kb.inp("w_out", [D, D])
    wg1 = kb.inp("wg1", [D, DFF])
    wu1 = kb.inp("wu1", [D, DFF])
    wd1 = kb.inp("wd1", [DFF, D])
    wg2 = kb.inp("wg2", [D, DFF])
    wu2 = kb.inp("wu2", [D, DFF])
    wd2 = kb.inp("wd2", [DFF, D])
    x4T = kb.outp("x4T", [NKC, 128, TPC])

    with ExitStack() as es:
        kb.load_consts(es, cst)
        an = kb.sb(es, "an", [128, NKC, TPC], BF16)
        r_an = [[P.res() for _ in range(4)] for _ in range(NKC)]
        ng = kb.sb(es, "ng", [128, 2, 8], F32)
        r_ng = P.res()
        P.dma("sp", "ng", ng[:], normg, writes=[r_ng])
        mvA, r_mvA = emit_mod(kb, es, modwA, 32, cvec, 1, modbA, "mA")
        mvB, r_mvB = emit_mod(kb, es, modwB, 32, cvec, 1, modbB, "mB")
        lam_t = kb.sb(es, "lam", [128, 4, 64], F32)
        lsc = kb.sb(es, "lsc", [128, 8], F32)
        sg8 = kb.sb(es, "sg8", [128, 1], F32)
        r_lam, r_lsc, r_sg8 = P.res(), P.res(), P.res()
        P.dma("sp", "lam", lam_t[:], lamv, writes=[r_lam])
        P.dma("sp", "sg8", sg8[:], sublng, writes=[r_sg8])
        P.op("dve", lambda e: e.tensor_tensor(lam_t[:, 0, :], lam_t[:, 0, :], lam_t[:, 1, :], ALU.mult),
             reads=[r_lam], writes=[r_lam])
        P.op("dve", lambda e: e.tensor_tensor(lam_t[:, 2, :], lam_t[:, 2, :], lam_t[:, 3, :], ALU.mult),
             reads=[r_lam], writes=[r_lam])
        P.op("dve", lambda e: e.reduce_sum(lsc[:, 0:1], lam_t[:, 0, :], mybir.AxisListType.X), reads=[r_lam], writes=[r_lsc])
        P.op("dve", lambda e: e.reduce_sum(lsc[:, 1:2], lam_t[:, 2, :], mybir.AxisListType.X), reads=[r_lam], writes=[r_lsc])
        P.op("act", lambda e: e.activation(lsc[:, 2:4], lsc[:, 0:2], AF.Exp), reads=[r_lsc], writes=[r_lsc])
        LAM_INIT = 0.8 - 0.6 * math.exp(-0.3 * 0)
        P.op("dve", lambda e: e.tensor_tensor(lsc[:, 4:5], lsc[:, 3:4], lsc[:, 2:3], ALU.subtract), reads=[r_lsc], writes=[r_lsc])
        P.op("dve", lambda e: e.tensor_scalar_add(lsc[:, 5:6], lsc[:, 4:5], -LAM_INIT), reads=[r_lsc], writes=[r_lsc])
        P.op("dve", lambda e: e.tensor_scalar_mul(sg8[:], sg8[:], 1.0 - LAM_INIT), reads=[r_sg8], writes=[r_sg8])
        neglam = lsc[:, 5:6]

        with ExitStack() as es2:
            tt = kb.sb(es2, "tt", [128, 8, 22 * 64], BF16)
            r_tt = P.res()
            P.dma("sp", "tt", tt[:], TT, writes=[r_tt])
            e2 = kb.sb(es2, "e2", [2, 128], BF16)
            r_e2 = P.res()
            P.dma("sp", "e2", e2[:], E2, writes=[r_e2])
            op_ = kb.sb(es2, "onesP", [128, 2, 128], BF16)
            r_op = P.res()
            P.dma("sp", "onesP", op_[:], onesP, writes=[r_op])
            rv = [kb.sb(es2, "rv", [2, 8 * 512], BF16) for _ in range(2)]
            r_rv = [P.res(), P.res()]
            qsb = [kb.sb(es2, "qsb", [128, 512], BF16) for _ in range(2)]
            r_q = [P.res(), P.res()]
            kA = [kb.sb(es2, "kA", [128, 1280], BF16) for _ in range(2)]
            r_kA = [P.res(), P.res()]
            vP = [kb.sb(es2, "vP", [128, 10 * 256], BF16) for _ in range(2)]
            r_vP = [P.res(), P.res()]
            kB_ = [kb.sb(es2, "kB", [128, 8448], BF16) for _ in range(2)]
            r_kB = [P.res(), P.res()]
            vB_ = [kb.sb(es2, "vB", [128, 66 * 128], BF16) for _ in range(2)]
            r_vB = [P.res(), P.res()]
            pT = [kb.sb(es2, "pT", [128, 512], BF16) for _ in range(6)]
            r_pT = [P.res() for _ in range(6)]
            SB = [0, 1, 2, 3, 6, 7]
            ev = [kb.sb(es2, "ev", [128, 512], F32) for _ in range(4)]
            r_ev = [P.res() for _ in range(4)]
            sqb = kb.sb(es2, "sqb", [128, 512], BF16)
            r_sqb = P.res()
            epst = kb.sb(es2, "epst", [128, 1], F32)
            r_epst = P.res()
            P.op("dve", lambda e: e.memset(epst[:], EPS), writes=[r_epst])
            acc = [kb.sb(es2, "acc", [128, 512], F32) for _ in range(2)]
            r_acc = [P.res(), P.res()]
            onesf = kb.sb(es2, "onesf", [128, 128], F32)
            r_onesf = P.res()
            P.op("dve", lambda e: e.memset(onesf[:], 1.0), writes=[r_onesf])
            sb_i = 0
            p_i = 0
            qcnt = 0
            acnt = 0
            for qt in range(4):
                rs = qt % 2
                P.dma("sp", f"rv{rs}", rv[rs][:], Rv[qt], writes=[r_rv[rs]])
                for g in range(4):
                    a = acnt % 2
                    acnt += 1
                    P.dma("sp", f"kA{a}", kA[a][:, 0:1024], kaB[g][:, qt * 512:qt * 512 + 1024], writes=[r_kA[a]])
                    P.dma("sp", f"kA{a}", kA[a][:, 1024:1280], kaB[g][:, 2560:2816], writes=[r_kA[a]])
                    P.dma("sp", f"vP{a}", vP[a][:, 0:8 * 256], vaP[g][:, qt * 4 * 256:(qt * 4 + 8) * 256], writes=[r_vP[a]])
                    P.dma("sp", f"vP{a}", vP[a][:, 8 * 256:10 * 256], vaP[g][:, 20 * 256:22 * 256], writes=[r_vP[a]])
                    qq = qcnt % 2
                    qcnt += 1
                    P.dma("sp", f"q{qq}", qsb[qq][:], qaT[g][:, qt * 512:(qt + 1) * 512], writes=[r_q[qq]])
                    nO, nL = 4, 5
                    its = [(jt, hh) for jt in range(10) for hh in range(2)]
                    NIT = len(its)
                    LA = 5
                    stash = {}
                    for step in range(NIT + LA):
                        if step < NIT:
                            jt, hh = its[step]
                            head = 2 * g + hh
                            rows = slice(hh * 64, (hh + 1) * 64)
                            b = SB[sb_i % 6]
                            sb_i += 1
                            band = jt < 8
                            P.op("pe", lambda e, b=b, a=a, rows=rows, jt=jt, qq=qq, band=band: e.matmul(
                                kb.ps[b][:], kA[a][rows, jt * 128:(jt + 1) * 128], qsb[qq][rows, :],
                                start=True, stop=(not band)),
                                reads=[r_kA[a], r_q[qq]], writes=[kb.psr[b]])
                            if band:
                                s_j = 14 - 2 * jt
                                P.op("pe", lambda e, b=b, head=head, s_j=s_j: e.matmul(
                                    kb.ps[b][:], kb.ident, tt[:, head, s_j * 64:(s_j + 8) * 64], start=False, stop=False),
                                    reads=[r_tt, kb.r_cst], writes=[kb.psr[b]])
                                P.op("pe", lambda e, b=b, rs=rs, jt=jt: e.matmul(
                                    kb.ps[b][:], e2[:, :], rv[rs][:, jt * 512:(jt + 1) * 512], start=False, stop=True),
                                    reads=[r_e2, r_rv[rs]], writes=[kb.psr[b]])
                            pi = p_i % 6
                            p_i += 1
                            P.op("act", lambda e, pi=pi, b=b: e.activation(pT[pi][:], kb.ps[b][:], AF.Exp),
                                 reads=[kb.psr[b]], writes=[r_pT[pi]])
                            stash[step] = pi
                        if step >= LA:
                            jt, hh = its[step - LA]
                            pi = stash.pop(step - LA)
                            first = (step - LA == 0)
                            last = (step - LA == NIT - 1)
                            P.op("pe", lambda e, pi=pi, a=a, jt=jt, hh=hh, first=first, last=last: e.matmul(
                                kb.ps[nO][:], vP[a][:, (jt * 2 + hh) * 128:(jt * 2 + hh + 1) * 128], pT[pi][:], start=first, stop=last),
                                reads=[r_vP[a], r_pT[pi]], writes=[kb.psr[nO]])
                            P.op("pe", lambda e, pi=pi, hh=hh, first=first, last=last: e.matmul(
                                kb.ps[nL][:], op_[:, hh, :], pT[pi][:], start=first, stop=last),
                                reads=[r_op, r_pT[pi]], writes=[kb.psr[nL]])
                    P.op("dve", lambda e: e.reciprocal(ev[0][:], kb.ps[nL][:]), reads=[kb.psr[nL]], writes=[r_ev[0]])
                    P.op("dve", lambda e, g=g, qt=qt: e.tensor_tensor(
                        an[:, g, qt * 512:(qt + 1) * 512], kb.ps[nO][:], ev[0][:], ALU.mult),
                        reads=[kb.psr[nO], r_ev[0]], writes=[r_an[g][qt]])
            for h in range(4):
                a = h % 2
                P.dma("sp", f"kB{a}", kB_[a][:], kbT[h], writes=[r_kB[a]])
                P.dma("sp", f"vB{a}", vB_[a][:], vbT[h], writes=[r_vB[a]])
                for qt in range(4):
                    qq = qcnt % 2
                    qcnt += 1
                    P.dma("sp", f"q{qq}", qsb[qq][:], qbT[h][:, qt * 512:(qt + 1) * 512], writes=[r_q[qq]])
                    bO = [4, 5]
                    bL = [SB[sb_i % 6], SB[(sb_i + 1) % 6]]
                    its = [(kt, m) for kt in range(66) for m in range(2)]
                    NIT = len(its)
                    LA = 5
                    stash = {}
                    for step in range(NIT + LA):
                        if step < NIT:
                            kt, m = its[step]
                            rows = slice(m * 64, (m + 1) * 64)
                            b = SB[sb_i % 6]
                            sb_i += 1
                            P.op("pe", lambda e, b=b, a=a, rows=rows, kt=kt, qq=qq: e.matmul(
                                kb.ps[b][:], kB_[a][rows, kt * 128:(kt + 1) * 128], qsb[qq][rows, :],
                                start=True, stop=True),
                                reads=[r_kB[a], r_q[qq]], writes=[kb.psr[b]])
                            pi = p_i % 6
                            p_i += 1
                            P.op("act", lambda e, pi=pi, b=b: e.activation(pT[pi][:], kb.ps[b][:], AF.Exp, scale=0.125),
                                 reads=[kb.psr[b]], writes=[r_pT[pi]])
                            stash[step] = pi
                        if step >= LA:
                            kt, m = its[step - LA]
                            pi = stash.pop(step - LA)
                            P.op("pe", lambda e, pi=pi, a=a, kt=kt, m=m: e.matmul(
                                kb.ps[bO[m]][:], vB_[a][:, kt * 128:(kt + 1) * 128], pT[pi][:], start=(kt == 0), stop=(kt == 65)),
                                reads=[r_vB[a], r_pT[pi]], writes=[kb.psr[bO[m]]])
                            if kt == 0:
                                P.op("dve", lambda e, pi=pi, m=m: e.tensor_copy(acc[m][:], pT[pi][:]),
                                     reads=[r_pT[pi]], writes=[r_acc[m]])
                            else:
                                P.op("dve", lambda e, pi=pi, m=m: e.tensor_tensor(acc[m][:], acc[m][:], pT[pi][:], ALU.add),
                                     reads=[r_pT[pi], r_acc[m]], writes=[r_acc[m]])
                    bL = [SB[sb_i % 6], SB[(sb_i + 1) % 6]]
                    sb_i += 2
                    for m in range(2):
                        P.op("pe", lambda e, m=m, bL=bL: e.matmul(kb.ps[bL[m]][:], onesf[:], acc[m][:], start=True, stop=True),
                             reads=[r_onesf, r_acc[m]], writes=[kb.psr[bL[m]]])
                    for m in range(2):
                        P.op("dve", lambda e, m=m, bL=bL: e.reciprocal(ev[m][:], kb.ps[bL[m]][:]),
                             reads=[kb.psr[bL[m]]], writes=[r_ev[m]])
                        P.op("dve", lambda e, m=m: e.tensor_tensor(ev[m][:], kb.ps[bO[m]][:], ev[m][:], ALU.mult),
                             reads=[kb.psr[bO[m]], r_ev[m]], writes=[r_ev[m]])
                    P.op("dve", lambda e: e.scalar_tensor_tensor(ev[2][:], ev[1][:], neglam, ev[0][:], ALU.mult, ALU.add),
                         reads=[r_ev[0], r_ev[1], r_lsc], writes=[r_ev[2]])
                    P.op("act", lambda e: e.activation(sqb[:], ev[2][:], AF.Square), reads=[r_ev[2]], writes=[r_sqb])
                    b = SB[sb_i % 6]
                    sb_i += 1
                    P.op("pe", lambda e, b=b: e.matmul(kb.ps[b][:], kb.ones, sqb[:], start=True, stop=True),
                         reads=[r_sqb, kb.r_cst], writes=[kb.psr[b]])
                    P.op("act", lambda e, b=b: e.activation(ev[3][:], kb.ps[b][:], AF.Sqrt, bias=epst[:, 0:1], scale=1.0 / 128),
                         reads=[kb.psr[b], r_epst], writes=[r_ev[3]])
                    P.op("dve", lambda e: e.reciprocal(ev[3][:], ev[3][:]), reads=[r_ev[3]], writes=[r_ev[3]])
                    P.op("dve", lambda e: e.tensor_tensor(ev[2][:], ev[2][:], ev[3][:], ALU.mult),
                         reads=[r_ev[2], r_ev[3]], writes=[r_ev[2]])
                    P.op("act", lambda e, h=h, qt=qt: e.activation(
                        an[:, 4 + h, qt * 512:(qt + 1) * 512], ev[2][:], AF.Identity, scale=sg8[:, 0:1]),
                        reads=[r_ev[2], r_sg8], writes=[r_an[4 + h][qt]])
            P.barrier()

        with ExitStack() as es3:
            x_t = kb.sb(es3, "x", [128, NKC, TPC], F32)
            lat = make_tiles(x_t, P, TPC)
            for kc in range(NKC):
                P.dma("sp", f"xin{kc}", x_t[:, kc, :], x1T[kc], writes=[t.xres[kc] for t in lat])
            G5 = mvA[:, 0:8, 0]
            with ExitStack() as es4:
                wo = kb.sb(es4, "wo", [128, NKC, D], BF16)
                r_wo = P.res()
                P.dma("pool", "wo", wo[:], w_out.rearrange("(kc p) n -> p kc n", p=128), writes=[r_wo])
                bc = 0
                for ti, t in enumerate(lat):
                    for oc in range(NKC):
                        b = 1 + bc % 4
                        bc += 1
                        for kc in range(NKC):
                            P.op("pe", lambda e, b=b, oc=oc, kc=kc, ti=ti: e.matmul(
                                kb.ps[b][:], wo[:, kc, oc * 128:(oc + 1) * 128], an[:, kc, ti * 512:(ti + 1) * 512],
                                start=(kc == 0), stop=(kc == NKC - 1)),
                                reads=[r_wo, r_an[kc][ti]], writes=[kb.psr[b]])
                        P.op("dve", lambda e, b=b, oc=oc, t=t: e.scalar_tensor_tensor(
                            t.x[oc], kb.ps[b][:], G5[:, oc:oc + 1], t.x[oc], ALU.mult, ALU.add),
                            reads=[kb.psr[b], t.xres[oc], r_mvA], writes=[t.xres[oc]])
                P.barrier()
            A1, B1, G1, r1 = emit_ab(kb, es3, mvA, r_mvA, 0, 1, 2, ng[:, 0, :], r_ng, "f2", i_gate=3, gate_mul=0.5)
            for t in lat:
                t.A, t.B, t.HG, t.modres = A1, B1, G1, r1
            xn_ext = lambda kc, a, b: an[:, kc, a:b]
            emit_ffn(kb, [lat[0:2], lat[2:4]], wg1, wu1, wd1, xn_ext=xn_ext)
            A2, B2, G2, r2 = emit_ab(kb, es3, mvB, r_mvB, 0, 0, 1, ng[:, 1, :], r_ng, "f3", i_gate=2, gate_mul=0.5)
            for t in lat:
                t.A, t.B, t.HG, t.modres = A2, B2, G2, r2
            emit_ffn(kb, [lat[0:2], lat[2:4]], wg2, wu2, wd2, xn_ext=xn_ext)
            for kc in range(NKC):
                kb.store(x4T[kc], x_t[:, kc, :], [t.xres[kc] for t in lat], f"x{kc}")
            kb.finish()
    return kb.nc


def build_C():
    kb = KB()
    P = kb.P
    x4T = kb.inp("x4T", [NKC, 128, TPC])
    xhT = kb.inp("xhT", [128, NKC, 2])
    hmask = kb.inp("hmask", [128, 2])
    cvec = kb.inp("cvec", [128, 8, 1])
    modw = kb.inp("modw", [D, 6 * D])
    modb = kb.inp("modb", [128, 48])
    normg = kb.inp("normg", [128, 3, 8])
    cw = kb.inp("cw", [128, 3, 8])
    cw_in = kb.inp("cw_in", [D, 3 * D])
    cw_out = kb.inp("cw_out", [D, D])
    wg = kb.inp("wg", [D, DFF])
    wu = kb.inp("wu", [D, DFF])
    wd = kb.inp("wd", [DFF, D])
    cst = kb.inp("cst", [128, 3, 128])
    outT = kb.outp("outT", [NKC, 128, TPC])

    with ExitStack() as es:
        kb.load_consts(es, cst)
        x_t = kb.sb(es, "x", [128, NKC, TPC], F32)
        xh_t = kb.sb(es, "xh", [128, NKC, 2], F32)
        hm = kb.sb(es, "hm", [128, 2], F32)
        ng = kb.sb(es, "ng", [128, 3, 8], F32)
        cwt = kb.sb(es, "cwt", [128, 3, 8], F32)
        r_ng, r_hm, r_cw = P.res(), P.res(), P.res()
        P.dma("sp", "ng", ng[:], normg, writes=[r_ng])
        P.dma("sp", "hm", hm[:], hmask, writes=[r_hm])
        P.dma("sp", "cw", cwt[:], cw, writes=[r_cw])
        lat = make_tiles(x_t, P, TPC)
        halo = make_tiles(xh_t, P, 2)
        for kc in range(NKC):
            P.dma("sp", f"xin{kc}", x_t[:, kc, :], x4T[kc], writes=[t.xres[kc] for t in lat])
        P.dma("sp", "xh", xh_t[:], xhT, writes=halo[0].xres)
        mv, r_mv = emit_mod(kb, es, modw, 48, cvec, 1, modb, "mC")
        Am, Bm, _, rm = emit_ab(kb, es, mv, r_mv, 0, 0, 1, ng[:, 0, :], r_ng, "cm")
        G5 = mv[:, 16:24, 0]
        with ExitStack() as es2:
            U = kb.sb(es2, "U", [128, NKC, TPC + 2], F32)
            r_U = [[P.res() for _ in range(6)] for _ in range(NKC)]
            scr = make_scratch(kb, es2)
            xn = [kb.sb(es2, "xnc", [128, NKC, 512], BF16)] * 2
            r_xn = [P.res()] * 2
            tmpc = [kb.sb(es2, "tmpc", [128, 512], F32) for _ in range(2)]
            r_tmpc = [P.res(), P.res()]
            wi_src = cw_in.rearrange("(kc p) n -> p kc n", p=128)
            with ExitStack() as es3:
                wch = kb.sb(es3, "wch", [128, NKC, 2 * D], BF16)
                r_wch = [P.res() for _ in range(4)]
                for g4 in range(4):
                    P.dma("pool", f"wch{g4}", wch[:, :, g4 * 512:(g4 + 1) * 512],
                          wi_src[:, :, D + g4 * 512:D + (g4 + 1) * 512], writes=[r_wch[g4]])
                bc = 0
                for ti, t in enumerate(lat + halo):
                    is_h = ti >= 4
                    n = t.n
                    xb = ti % 2
                    emit_normmod(kb, scr, t, Am, Bm, rm, lambda kc, xb=xb, n=n: xn[xb][:, kc, :n], r_xn[xb])
                    for c in range(NKC):
                        b1 = 1 + bc % 4
                        b2 = 1 + (bc + 1) % 4
                        bc += 2
                        for (bb, wc) in ((b1, c), (b2, 8 + c)):
                            for kc in range(NKC):
                                P.op("pe", lambda e, bb=bb, wc=wc, kc=kc, xb=xb, n=n: e.matmul(
                                    kb.ps[bb][:, :n], wch[:, kc, wc * 128:(wc + 1) * 128], xn[xb][:, kc, :n],
                                    start=(kc == 0), stop=(kc == NKC - 1)),
                                    reads=[r_wch[wc // 4], r_xn[xb]], writes=[kb.psr[bb]])
                        q = c % 2
                        P.op("act", lambda e, q=q, b1=b1, n=n: e.activation(tmpc[q][:, :n], kb.ps[b1][:, :n], AF.Identity),
                             reads=[kb.psr[b1]], writes=[r_tmpc[q]])
                        if not is_h:
                            P.op("dve", lambda e, q=q, b2=b2, c=c, ti=ti: e.tensor_tensor(
                                U[:, c, 1 + ti * 512:1 + (ti + 1) * 512], tmpc[q][:], kb.ps[b2][:], ALU.mult),
                                reads=[r_tmpc[q], kb.psr[b2]], writes=[r_U[c][1 + ti]])
                        else:
                            P.op("dve", lambda e, q=q, b2=b2: e.tensor_tensor(
                                tmpc[q][:, 0:2], tmpc[q][:, 0:2], kb.ps[b2][:, 0:2], ALU.mult),
                                reads=[r_tmpc[q], kb.psr[b2]], writes=[r_tmpc[q]])
                            P.op("dve", lambda e, q=q, c=c: e.tensor_tensor(
                                U[:, c, 0:1], tmpc[q][:, 0:1], hm[:, 0:1], ALU.mult),
                                reads=[r_tmpc[q], r_hm], writes=[r_U[c][0]])
                            P.op("dve", lambda e, q=q, c=c: e.tensor_tensor(
                                U[:, c, TPC + 1:TPC + 2], tmpc[q][:, 1:2], hm[:, 1:2], ALU.mult),
                                reads=[r_tmpc[q], r_hm], writes=[r_U[c][5]])
                P.barrier()
            with ExitStack() as es3:
                wbg = kb.sb(es3, "wbg", [128, NKC, D], BF16)
                wo = kb.sb(es3, "wo", [128, NKC, D], BF16)
                r_wbg, r_wo = P.res(), P.res()
                P.dma("pool", "wbg", wbg[:], wi_src[:, :, 0:D], writes=[r_wbg])
                P.dma("pool", "wo", wo[:], cw_out.rearrange("(kc p) n -> p kc n", p=128), writes=[r_wo])
                Z = [kb.sb(es3, "Z", [128, NKC, 512], BF16)] * 2
                r_Z = [[P.res() for _ in range(NKC)]] * 2
                bc = 0
                for ti, t in enumerate(lat):
                    xb = ti % 2
                    zb = ti % 2
                    emit_normmod(kb, scr, t, Am, Bm, rm, lambda kc, xb=xb: xn[xb][:, kc, :], r_xn[xb])
                    for c in range(NKC):
                        b = 1 + bc % 4
                        bc += 1
                        for kc in range(NKC):
                            P.op("pe", lambda e, b=b, c=c, kc=kc, xb=xb: e.matmul(
                                kb.ps[b][:], wbg[:, kc, c * 128:(c + 1) * 128], xn[xb][:, kc, :],
                                start=(kc == 0), stop=(kc == NKC - 1)),
                                reads=[r_wbg, r_xn[xb]], writes=[kb.psr[b]])
                        q = c % 2
                        base = 1 + ti * 512
                        ru = [r_U[c][k] for k in range(6)]
                        P.op("dve", lambda e, q=q, c=c, base=base: e.tensor_scalar_mul(
                            tmpc[q][:], U[:, c, base:base + 512], cwt[:, 1, c:c + 1]),
                            reads=ru + [r_cw], writes=[r_tmpc[q]])
                        P.op("dve", lambda e, q=q, c=c, base=base: e.scalar_tensor_tensor(
                            tmpc[q][:], U[:, c, base - 1:base + 511], cwt[:, 0, c:c + 1], tmpc[q][:], ALU.mult, ALU.add),
                            reads=ru + [r_cw, r_tmpc[q]], writes=[r_tmpc[q]])
                        P.op("dve", lambda e, q=q, c=c, base=base: e.scalar_tensor_tensor(
                            tmpc[q][:], U[:, c, base + 1:base + 513], cwt[:, 2, c:c + 1], tmpc[q][:], ALU.mult, ALU.add),
                            reads=ru + [r_cw, r_tmpc[q]], writes=[r_tmpc[q]])
                        P.op("dve", lambda e, q=q, c=c, b=b, zb=zb: e.tensor_tensor(
                            Z[zb][:, c, :], tmpc[q][:], kb.ps[b][:], ALU.mult),
                            reads=[r_tmpc[q], kb.psr[b]], writes=[r_Z[zb][c]])
                    for oc in range(NKC):
                        b = 1 + bc % 4
                        bc += 1
                        for kc in range(NKC):
                            P.op("pe", lambda e, b=b, oc=oc, kc=kc, zb=zb: e.matmul(
                                kb.ps[b][:], wo[:, kc, oc * 128:(oc + 1) * 128], Z[zb][:, kc, :],
                                start=(kc == 0), stop=(kc == NKC - 1)),
                                reads=[r_wo, r_Z[zb][kc]], writes=[kb.psr[b]])
                        P.op("dve", lambda e, b=b, oc=oc, t=t: e.scalar_tensor_tensor(
                            t.x[oc], kb.ps[b][:], G5[:, oc:oc + 1], t.x[oc], ALU.mult, ALU.add),
                            reads=[kb.psr[b], t.xres[oc], r_mv], writes=[t.xres[oc]])
                P.barrier()
        A1, B1, G1, r1 = emit_ab(kb, es, mv, r_mv, 0, 3, 4, ng[:, 1, :], r_ng, "f4", i_gate=5, gate_mul=0.5)
        for t in lat:
            t.A, t.B, t.HG, t.modres = A1, B1, G1, r1
        emit_ffn(kb, [lat[0:2], lat[2:4]], wg, wu, wd)
        with ExitStack() as es2:
            scr = make_scratch(kb, es2)
            ob = [kb.sb(es2, "ob", [128, 512], F32) for _ in range(4)]
            r_ob = [P.res() for _ in range(4)]
            oc_ = 0
            for ti, t in enumerate(lat):
                n = t.n
                ps_stat, r_ps = kb.ps[0], kb.psr[0]
                for kc in range(NKC):
                    s = kc % 2
                    P.op("act", lambda e, s=s, kc=kc, t=t: e.activation(scr["sq"][s][:], t.x[kc], AF.Square),
                         reads=[t.xres[kc]], writes=[scr["r_sq"][s]])
                    P.op("pe", lambda e, s=s, kc=kc: e.matmul(ps_stat[:], kb.ones, scr["sq"][s][:],
                                                              start=(kc == 0), stop=(kc == NKC - 1)),
                         reads=[scr["r_sq"][s], kb.r_cst], writes=[r_ps])
                rstd = scr["rstd"]
                P.op("act", lambda e: e.activation(rstd[:], ps_stat[:], AF.Sqrt, bias=scr["eps"][:, 0:1], scale=1.0 / D),
                     reads=[r_ps, scr["r_eps"]], writes=[scr["r_rstd"]])
                P.op("dve", lambda e: e.reciprocal(rstd[:], rstd[:]), reads=[scr["r_rstd"]], writes=[scr["r_rstd"]])
                for kc in range(NKC):
                    o = oc_ % 4
                    oc_ += 1
                    P.op("dve", lambda e, o=o, kc=kc, t=t: e.scalar_tensor_tensor(
                        ob[o][:], t.x[kc], ng[:, 2, kc:kc + 1], rstd[:], ALU.mult, ALU.mult),
                        reads=[t.xres[kc], scr["r_rstd"], r_ng], writes=[r_ob[o]])
                    kb.store(outT[kc][:, ti * 512:(ti + 1) * 512], ob[o][:], [r_ob[o]], f"ob{o}")
            kb.finish()
    return kb.nc


_CACHE = {}


def _get(name, fn):
    if name not in _CACHE:
        _CACHE[name] = fn()
    return _CACHE[name]


def _pk(v, nch):
    return np.ascontiguousarray(np.asarray(v, np.float32).reshape(nch, 128).T)


def _consts():
    c = np.zeros((128, 3, 128), np.float32)
    c[:, 0, :] = 1.0
    c[:, 1, :] = np.eye(128, dtype=np.float32)
    idx = np.arange(128)
    c[idx, 2, idx ^ 1] = 1.0
    return c


def _rope_tables(t0):
    t = np.arange(t0, t0 + TPC)
    row = (t // 64).astype(np.float64)
    col = (t % 64).astype(np.float64)
    inv = 10000.0 ** (-np.arange(16, dtype=np.float64) / 16)
    ang = np.concatenate([row[:, None] * inv, col[:, None] * inv], -1)
    cos = np.cos(ang.astype(np.float32).astype(np.float64))
    sin = np.sin(ang.astype(np.float32).astype(np.float64))
    ang32 = np.concatenate([row[:, None].astype(np.float32) * inv.astype(np.float32),
                            col[:, None].astype(np.float32) * inv.astype(np.float32)], -1)
    cos = np.cos(ang32)
    sin = np.sin(ang32)
    tab = np.zeros((128, 2, TPC), np.float32)
    for p in range(128):
        d = p % 64
        i = d // 2
        tab[p, 0] = cos[:, i]
        tab[p, 1] = (-sin[:, i]) if d % 2 == 0 else sin[:, i]
    return tab


def _bias_table(rpb):
    rpb = np.asarray(rpb, np.float32)
    qc = np.arange(64)
    kc = np.arange(64)
    col_start = np.clip(qc - 8, 0, 48)
    cmask = (kc[None, :] >= col_start[:, None]) & (kc[None, :] < col_start[:, None] + 16)
    dc = np.clip(kc[None, :] - qc[:, None] + 15, 0, 30)
    TT = np.zeros((128, 8, 22, 64), np.float32)
    for half in range(2):
        for jj in range(22):
            dr = 10 - jj + half
            if -7 <= dr <= 7:
                val = rpb[:, dr + 7, :][:, dc]
            else:
                val = np.zeros((8, 64, 64), np.float32)
            val = np.where(cmask[None], val, NEG)
            TT[half * 64:(half + 1) * 64, :, jj, :] = np.transpose(val, (2, 0, 1))
    return TT.reshape(128, 8, 22 * 64).astype(NPBF)


def _row_valid(r0):
    Rv = np.zeros((4, 2, 8, 8, 64), np.float32)
    for qt in range(4):
        R0 = r0 + 8 * qt
        for j in range(8):
            for a in range(2):
                kr = R0 - 4 + 2 * j + a
                for i in range(8):
                    qr = R0 + i
                    st = min(max(qr - 4, 0), 120)
                    ok = (st <= kr < st + 8)
                    Rv[qt, a, j, i, :] = 0.0 if ok else NEG
    return Rv.reshape(4, 2, 8 * 512).astype(NPBF)


def _run(nc, in_maps):
    res = run_bass_kernel_spmd(nc, in_maps, core_ids=list(range(8)))
    return res.results


def kernel(x, c, ctx, c_ctx, mod_w, mod_b, norm_g, ffn_w_gate, ffn_w_up, ffn_w_down,
           attn_w_in, attn_w_out, na_rpb, diff_lambda, diff_subln_g,
           conv_w_in, conv_w_out, conv_w, final_g):
    f32 = lambda a: np.ascontiguousarray(np.asarray(a, np.float32))
    x, c, ctx, c_ctx = f32(x), f32(c), f32(ctx), f32(c_ctx)
    mod_w, mod_b, norm_g = f32(mod_w), f32(mod_b), f32(norm_g)
    cst = _consts()
    NCORE = 8

    ncA = build_A()
    mapsA = []
    wA = dict(modw=f32(mod_w[0][:, :5 * D]), modb=_pk(mod_b[0][:5 * D], 40),
              normg=f32(np.stack([_pk(norm_g[0, 0], 8), _pk(norm_g[0, 1], 8)], 1)),
              wg=f32(ffn_w_gate[0, 0]), wu=f32(ffn_w_up[0, 0]), wd=f32(ffn_w_down[0, 0]),
              w_in=f32(attn_w_in[0]), cst=cst)
    for core in range(NCORE):
        b, t0 = core // 4, (core % 4) * TPC
        xs = x[b, t0:t0 + TPC]
        xT = f32(xs.T.reshape(NKC, 128, TPC))
        cxT = f32(ctx[b].T.reshape(NKC, 128, 256))
        cv = np.stack([c[b].reshape(8, 128).T, c_ctx.reshape(8, 128).T], -1)
        mapsA.append(dict(xT=xT, cxT=cxT, cvec=f32(cv), rope=_rope_tables(t0), **wA))
    resA = _run(ncA, mapsA)

    ncB = build_B()
    mapsB = []
    E2 = np.zeros((2, 128), np.float32)
    E2[0, :64] = 1.0
    E2[1, 64:] = 1.0
    onesP = np.zeros((128, 2, 128), np.float32)
    onesP[:, 0, :64] = 1.0
    onesP[:, 1, 64:] = 1.0
    TTh = _bias_table(na_rpb[0])
    lamv = f32(np.broadcast_to(np.asarray(diff_lambda[0], np.float32)[None], (128, 4, 64)))
    modwB = np.zeros((D, 4 * D), np.float32)
    modwB[:, :3 * D] = mod_w[1][:, :3 * D]
    modbB = np.zeros((128, 32), np.float32)
    modbB[:, :24] = _pk(mod_b[1][:3 * D], 24)
    wB = dict(TT=TTh, E2=E2.astype(NPBF), onesP=onesP.astype(NPBF), cst=cst, lamv=lamv,
              sublng=f32(np.asarray(diff_subln_g[0], np.float32).reshape(128, 1)),
              modwA=f32(mod_w[0][:, 5 * D:]), modbA=_pk(mod_b[0][5 * D:], 32),
              modwB=modwB, modbB=modbB,
              normg=f32(np.stack([_pk(norm_g[0, 2], 8), _pk(norm_g[1, 0], 8)], 1)),
              w_out=f32(attn_w_out[0]),
              wg1=f32(ffn_w_gate[0, 1]), wu1=f32(ffn_w_up[0, 1]), wd1=f32(ffn_w_down[0, 1]),
              wg2=f32(ffn_w_gate[1, 0]), wu2=f32(ffn_w_up[1, 0]), wd2=f32(ffn_w_down[1, 0]))
    per_batch = {}
    for b in range(2):
        cores = [4 * b + i for i in range(4)]
        qk = [np.asarray(resA[cc]["qkT"]) for cc in cores]
        vt = [np.asarray(resA[cc]["vtm"]).reshape(TPC, 1024) for cc in cores]
        kc_ = np.asarray(resA[cores[0]]["kcT"])
        vc_ = np.asarray(resA[cores[0]]["vc"]).reshape(256, 1024)
        ka_full = np.concatenate([q[4:8] for q in qk], axis=2)
        z = np.zeros((4, 128, 256), ka_full.dtype)
        ka_pad = np.concatenate([z, ka_full, z], axis=2)
        va_full = np.concatenate([v[:, 0:512] for v in vt], axis=0)
        zv = np.zeros((256, 512), va_full.dtype)
        va_pad = np.concatenate([zv, va_full, zv], axis=0)
        kb_full = np.concatenate([q[12:16] for q in qk] , axis=2)
        kbT = np.ascontiguousarray(np.concatenate([kb_full, kc_[4:8]], axis=2))
        vb_full = np.concatenate([v[:, 512:1024] for v in vt] + [vc_[:, 512:1024]], axis=0)
        vbT = np.ascontiguousarray(vb_full.reshape(66, 128, 4, 128).transpose(2, 1, 0, 3).reshape(4, 128, 66 * 128))
        per_batch[b] = (ka_pad, va_pad, kc_, vc_, kbT, vbT)
    for core in range(NCORE):
        b, t0 = core // 4, (core % 4) * TPC
        ka_pad, va_pad, kc_, vc_, kbT, vbT = per_batch[b]
        qk = np.asarray(resA[core]["qkT"])
        kaB = np.ascontiguousarray(np.concatenate([ka_pad[:, :, t0:t0 + 2560], kc_[0:4]], axis=2))
        vband = np.concatenate([va_pad[t0:t0 + 2560], vc_[:, 0:512]], axis=0).reshape(2816, 8, 64)
        vP = np.zeros((2816, 8, 128), vband.dtype)
        vP[:, 0::2, 0:64] = vband[:, 0::2]
        vP[:, 1::2, 64:128] = vband[:, 1::2]
        vaP = np.ascontiguousarray(vP.reshape(22, 128, 4, 256).transpose(2, 1, 0, 3).reshape(4, 128, 22 * 256))
        cv = c[b].reshape(8, 128).T[:, :, None]
        mapsB.append(dict(qaT=np.ascontiguousarray(qk[0:4]), qbT=np.ascontiguousarray(qk[8:12]),
                          kaB=kaB, vaP=vaP, kbT=kbT, vbT=vbT, Rv=_row_valid((core % 4) * 32),
                          x1T=np.asarray(resA[core]["x1T"]), cvec=f32(cv), **wB))
    resB = _run(ncB, mapsB)

    ncC = build_C()
    mapsC = []
    wC = dict(modw=f32(mod_w[1][:, 3 * D:]), modb=_pk(mod_b[1][3 * D:], 48),
              normg=f32(np.stack([_pk(norm_g[1, 1], 8), _pk(norm_g[1, 2], 8), _pk(final_g, 8)], 1)),
              cw=f32(np.stack([_pk(np.asarray(conv_w[0], np.float32)[k], 8) for k in range(3)], 1)),
              cw_in=f32(conv_w_in[0]), cw_out=f32(conv_w_out[0]),
              wg=f32(ffn_w_gate[1, 1]), wu=f32(ffn_w_up[1, 1]), wd=f32(ffn_w_down[1, 1]), cst=cst)
    x4 = [np.asarray(resB[cc]["x4T"]) for cc in range(NCORE)]
    for core in range(NCORE):
        b = core // 4
        xh = np.zeros((128, NKC, 2), np.float32)
        hmask = np.zeros((128, 2), np.float32)
        if core % 4 != 0:
            xh[:, :, 0] = x4[core - 1][:, :, TPC - 1].T
            hmask[:, 0] = 1.0
        if core % 4 != 3:
            xh[:, :, 1] = x4[core + 1][:, :, 0].T
            hmask[:, 1] = 1.0
        cv = c[b].reshape(8, 128).T[:, :, None]
        mapsC.append(dict(x4T=x4[core], xhT=xh, hmask=hmask, cvec=f32(cv), **wC))
    resC = _run(ncC, mapsC)

    out = np.zeros((2, 8192, D), np.float32)
    for core in range(NCORE):
        b, t0 = core // 4, (core % 4) * TPC
        oT = np.asarray(resC[core]["outT"]).reshape(D, TPC)
        out[b, t0:t0 + TPC] = oT.T
    return out
```

```python
import math
from contextlib import ExitStack

import numpy as np
import ml_dtypes
import concourse.bass as bass
import concourse.mybir as mybir
from concourse.bass_utils import run_bass_kernel_spmd

F32 = mybir.dt.float32
BF16 = mybir.dt.bfloat16
ALU = mybir.AluOpType
AF = mybir.ActivationFunctionType
NPBF = ml_dtypes.bfloat16

D = 1024
DFF = 2816
NKC = 8
NFC = 22
TPC = 2048
EPS = 1e-6
NEG = -30000.0
DEBUG_STOP = 0

ENGS = ["pe", "act", "dve", "pool", "sp"]
SEM_CHUNK = 30000


class Res:
    __slots__ = ("name", "w", "r")

    def __init__(self, name):
        self.name = name
        self.w = None
        self.r = []


class Op:
    __slots__ = ("eng", "fn", "deps", "signal", "sig", "dma_key", "dma_val", "dma_inc")

    def __init__(self, eng, fn):
        self.eng = eng
        self.fn = fn
        self.deps = []
        self.signal = False
        self.sig = None
        self.dma_key = None
        self.dma_val = 0
        self.dma_inc = 16


class Prog:
    def __init__(self, nc):
        self.nc = nc
        self.q = {e: [] for e in ENGS}
        self.dma_cnt = {}
        self.nres = 0
        self.pending = {e: [] for e in ENGS}
        self.dmas = []

    def res(self, name=None):
        self.nres += 1
        return Res(name or f"r{self.nres}")

    def _add(self, eng, fn, reads, writes, dma_key=None, inc=16):
        op = Op(eng, fn)
        deps = []
        for r in reads:
            if r.w is not None:
                deps.append(r.w)
        for w in writes:
            if w.w is not None:
                deps.append(w.w)
            deps.extend(w.r)
        if self.pending[eng]:
            deps.extend(self.pending[eng])
            self.pending[eng] = []
        seen = set()
        for d in deps:
            if id(d) in seen:
                continue
            seen.add(id(d))
            if d.dma_key is None:
                if eng == "pe" and d.eng == "pe":
                    continue
                d.signal = True
            op.deps.append(d)
        if dma_key is not None:
            op.dma_key = dma_key
            self.dma_cnt[dma_key] = self.dma_cnt.get(dma_key, 0) + inc
            op.dma_val = self.dma_cnt[dma_key]
            op.dma_inc = inc
            self.dmas.append(op)
        for r in reads:
            r.r.append(op)
        for w in writes:
            w.w = op
            w.r = []
        self.q[eng].append(op)
        return op

    def op(self, eng, fn, reads=(), writes=()):
        return self._add(eng, fn, reads, writes)

    def dma(self, eng, key, out, in_, reads=(), writes=()):
        return self._add(eng, lambda e: e.dma_start(out=out, in_=in_), reads, writes, dma_key=key)

    def barrier(self):
        lasts = []
        for e in ENGS:
            for op in reversed(self.q[e]):
                if op.dma_key is None:
                    lasts.append(op)
                    break
        lasts.extend(self.dmas)
        self.dmas = []
        for e in ENGS:
            self.pending[e] = list(lasts)

    def emit(self):
        nc = self.nc
        sems = {}

        def sem(name):
            if name not in sems:
                sems[name] = nc.alloc_semaphore(name)
            return sems[name]

        for e in ENGS:
            k = 0
            for op in self.q[e]:
                if op.dma_key is None and op.signal:
                    op.sig = (f"s_{e}_{k // SEM_CHUNK}", k % SEM_CHUNK + 1)
                    k += 1
                elif op.dma_key is not None:
                    op.sig = (f"d_{op.dma_key}", op.dma_val)

        def run(e, engine):
            waited = {}
            for op in self.q[e]:
                need = {}
                for d in op.deps:
                    s, v = d.sig
                    if waited.get(s, 0) >= v:
                        continue
                    if need.get(s, 0) < v:
                        need[s] = v
                for s, v in need.items():
                    engine.wait_ge(sem(s), v)
                    waited[s] = v
                ins = op.fn(engine)
                if op.dma_key is not None:
                    ins.then_inc(sem(op.sig[0]), op.dma_inc)
                elif op.signal:
                    ins.then_inc(sem(op.sig[0]), 1)

        with nc.Block() as block:
            @block.tensor
            def _(eng):
                run("pe", eng)

            @block.scalar
            def _(eng):
                run("act", eng)

            @block.vector
            def _(eng):
                run("dve", eng)

            @block.gpsimd
            def _(eng):
                run("pool", eng)

            @block.sync
            def _(eng):
                run("sp", eng)


class KB:
    def __init__(self):
        self.nc = bass.Bass("TRN2", target_bir_lowering=False)
        self.P = Prog(self.nc)
        self.ps = [self.nc.alloc_psum_tensor(f"psb{i}", [128, 512], F32) for i in range(8)]
        self.psr = [self.P.res(f"psb{i}") for i in range(8)]
        self.out_res = []
        self.uid = 0

    def inp(self, name, shape, dt=F32):
        return self.nc.dram_tensor(name, list(shape), dt, kind="ExternalInput").ap()

    def outp(self, name, shape, dt=F32):
        return self.nc.dram_tensor(name, list(shape), dt, kind="ExternalOutput").ap()

    def sb(self, es, name, shape, dt):
        self.uid += 1
        return es.enter_context(self.nc.sbuf_tensor(f"{name}_{self.uid}", list(shape), dt))

    def store(self, dram_ap, sb_ap, reads, key):
        r = self.P.res()
        self.P.dma("sp", "st_" + key, dram_ap, sb_ap, reads=reads, writes=[r])
        self.out_res.append(r)

    def finish(self):
        self.P.op("sp", lambda e: e.nop(), reads=self.out_res)
        self.P.emit()
        return self.nc

    def load_consts(self, es, cst_ap):
        P = self.P
        self.cst = self.sb(es, "cst", [128, 3, 128], BF16)
        self.r_cst = P.res("cst")
        P.dma("pool", "cst", self.cst[:], cst_ap, writes=[self.r_cst])
        self.ones = self.cst[:, 0, :]
        self.ident = self.cst[:, 1, :]
        self.swap = self.cst[:, 2, :]


class Tile:
    def __init__(self, x, xres, n, A=None, B=None, HG=None, modres=None):
        self.x = x
        self.xres = xres
        self.n = n
        self.A = A
        self.B = B
        self.HG = HG
        self.modres = modres
        self.off = 0


def emit_mod(kb, es, modw_ap, nch, cvec_ap, nv, modb_ap, name):
    P = kb.P
    nc = kb.nc
    assert nch % 4 == 0
    cv = kb.sb(es, name + "cv", [128, 8, nv], F32)
    sc = kb.sb(es, name + "sc", [128, 8, nv], BF16)
    mb = kb.sb(es, name + "mb", [128, nch], F32)
    mvec = kb.sb(es, name + "mvec", [128, nch, nv], F32)
    r_cv, r_sc, r_mb, r_mvec = P.res(), P.res(), P.res(), P.res()
    P.dma("sp", name + "cv", cv[:], cvec_ap, writes=[r_cv])
    P.dma("sp", name + "mb", mb[:], modb_ap, writes=[r_mb])
    P.op("act", lambda e: e.activation(sc[:], cv[:], AF.Silu), reads=[r_cv], writes=[r_sc])
    psm = kb.ps[7]
    r_psm = kb.psr[7]
    with ExitStack() as es2:
        mw = [kb.sb(es2, name + f"mw{s}", [128, 8, 512], BF16) for s in range(2)]
        r_mw = [P.res(), P.res()]
        src = modw_ap.rearrange("(kc p) n -> p kc n", p=128)
        for g in range(nch // 4):
            s = g % 2
            P.dma("pool", name + f"mw{s}", mw[s][:], src[:, :, g * 512:(g + 1) * 512], writes=[r_mw[s]])
            for c4 in range(4):
                oc = g * 4 + c4
                for kc in range(8):
                    P.op("pe", lambda e, s=s, c4=c4, oc=oc, kc=kc: e.matmul(
                        psm[:, oc * nv:(oc + 1) * nv], mw[s][:, kc, c4 * 128:(c4 + 1) * 128], sc[:, kc, :],
                        start=(kc == 0), stop=(kc == 7)), reads=[r_mw[s], r_sc], writes=[r_psm])
        for v in range(nv):
            P.op("dve", lambda e, v=v: e.tensor_tensor(
                mvec[:, :, v], psm[:, 0:nch * nv].rearrange("p (c v) -> p c v", v=nv)[:, :, v], mb[:], ALU.add),
                reads=[r_psm, r_mb], writes=[r_mvec])
        P.barrier()
    return mvec, r_mvec


class ModJob:
    def __init__(self, kb, es, modw_ap, nch, cvec_ap, nv, modb_ap, name, mw, r_mw, bank):
        P = kb.P
        self.kb, self.nv, self.name, self.mw, self.r_mw, self.bank = kb, nv, name, mw, r_mw, bank
        self.cv = kb.sb(es, name + "cv", [128, 8, nv], F32)
        self.sc = kb.sb(es, name + "sc", [128, 8, nv], BF16)
        self.mb = kb.sb(es, name + "mb", [128, nch], F32)
        self.mvec = kb.sb(es, name + "mvec", [128, nch, nv], F32)
        self.r_cv, self.r_sc, self.r_mb, self.r_mvec = P.res(), P.res(), P.res(), P.res()
        P.dma("sp", name + "cv", self.cv[:], cvec_ap, writes=[self.r_cv])
        P.dma("sp", name + "mb", self.mb[:], modb_ap, writes=[self.r_mb])
        P.op("act", lambda e: e.activation(self.sc[:], self.cv[:], AF.Silu), reads=[self.r_cv], writes=[self.r_sc])
        self.src = modw_ap.rearrange("(kc p) n -> p kc n", p=128)
        self.ng = nch // 4
        self.g = 0

    slot = 0

    def step(self):
        kb, P, nv = self.kb, self.kb.P, self.nv
        g = self.g
        self.g += 1
        s = ModJob.slot
        ModJob.slot ^= 1
        mw, r_mw = self.mw[s], self.r_mw[s]
        psm, r_psm = kb.ps[self.bank], kb.psr[self.bank]
        P.dma("pool", f"mwj{s}", mw[:], self.src[:, :, g * 512:(g + 1) * 512], writes=[r_mw])
        for c4 in range(4):
            for kc in range(8):
                P.op("pe", lambda e, c4=c4, kc=kc, mw=mw: e.matmul(
                    psm[:, c4 * nv:(c4 + 1) * nv], mw[:, kc, c4 * 128:(c4 + 1) * 128], self.sc[:, kc, :],
                    start=(kc == 0), stop=(kc == 7)), reads=[r_mw, self.r_sc], writes=[r_psm])
        for v in range(nv):
            P.op("dve", lambda e, v=v, g=g: e.tensor_tensor(
                self.mvec[:, g * 4:(g + 1) * 4, v], psm[:, 0:4 * nv].rearrange("p (c v) -> p c v", v=nv)[:, :, v],
                self.mb[:, g * 4:(g + 1) * 4], ALU.add),
                reads=[r_psm, self.r_mb], writes=[self.r_mvec])


def emit_ab(kb, es, mvec, r_mvec, v, i_shift, i_scale, g_ap, r_g, name, i_gate=None, gate_mul=1.0):
    P = kb.P
    A = kb.sb(es, name + "A", [128, 8], F32)
    r = P.res()
    P.op("dve", lambda e: e.scalar_tensor_tensor(
        A[:], mvec[:, i_scale * 8:(i_scale + 1) * 8, v], 1.0, g_ap, ALU.add, ALU.mult),
        reads=[r_mvec, r_g], writes=[r])
    Bm = mvec[:, i_shift * 8:(i_shift + 1) * 8, v]
    G = None
    if i_gate is not None:
        G = kb.sb(es, name + "G", [128, 8], F32)
        P.op("dve", lambda e: e.tensor_scalar_mul(G[:], mvec[:, i_gate * 8:(i_gate + 1) * 8, v], float(gate_mul)),
             reads=[r_mvec], writes=[r])
    return A, Bm, G, r


def emit_normmod(kb, scr, t, A, Bv, r_ab, out_fn, r_out):
    P = kb.P
    n = t.n
    ps_stat, r_ps = kb.ps[0], kb.psr[0]
    for kc in range(NKC):
        s = kc % 2
        P.op("act", lambda e, s=s, kc=kc: e.activation(scr["sq"][s][:, :n], t.x[kc], AF.Square),
             reads=[t.xres[kc]], writes=[scr["r_sq"][s]])
        P.op("pe", lambda e, s=s, kc=kc: e.matmul(ps_stat[:, :n], kb.ones, scr["sq"][s][:, :n],
                                                  start=(kc == 0), stop=(kc == NKC - 1)),
             reads=[scr["r_sq"][s], kb.r_cst], writes=[r_ps])
    rstd = scr["rstd"]
    P.op("act", lambda e: e.activation(rstd[:, :n], ps_stat[:, :n], AF.Sqrt, bias=scr["eps"][:, 0:1], scale=1.0 / D),
         reads=[r_ps, scr["r_eps"]], writes=[scr["r_rstd"]])
    P.op("dve", lambda e: e.reciprocal(rstd[:, :n], rstd[:, :n]), reads=[scr["r_rstd"]], writes=[scr["r_rstd"]])
    for kc in range(NKC):
        s = kc % 2
        P.op("dve", lambda e, s=s, kc=kc: e.tensor_tensor(scr["tmp"][s][:, :n], t.x[kc], rstd[:, :n], ALU.mult),
             reads=[t.xres[kc], scr["r_rstd"]], writes=[scr["r_tmp"][s]])
        P.op("act", lambda e, s=s, kc=kc: e.activation(out_fn(kc), scr["tmp"][s][:, :n], AF.Identity,
                                                       bias=Bv[:, kc:kc + 1], scale=A[:, kc:kc + 1]),
             reads=[scr["r_tmp"][s], r_ab], writes=[r_out])


def make_scratch(kb, es):
    P = kb.P
    scr = {
        "sq": [kb.sb(es, "sq", [128, 512], BF16) for _ in range(2)],
        "r_sq": [P.res(), P.res()],
        "rstd": kb.sb(es, "rstd", [128, 512], F32),
        "r_rstd": P.res(),
        "tmp": [kb.sb(es, "tmp", [128, 512], F32) for _ in range(2)],
        "r_tmp": [P.res(), P.res()],
        "eps": kb.sb(es, "eps", [128, 1], F32),
        "r_eps": P.res(),
    }
    P.op("dve", lambda e: e.memset(scr["eps"][:], EPS), writes=[scr["r_eps"]])
    return scr


def emit_ffn(kb, groups, wg_ap, wu_ap, wd_ap, xn_ext=None):
    P = kb.P
    with ExitStack() as es:
        maxw = max(sum(t.n for t in g) for g in groups)
        if xn_ext is None:
            xn_t = kb.sb(es, "ffn_xn", [128, NKC, maxw], BF16)
            xn = lambda kc, a, b: xn_t[:, kc, a:b]
        else:
            xn = xn_ext
        H = kb.sb(es, "ffn_H", [128, NFC, maxw], BF16)
        wgb = [kb.sb(es, "wg", [128, NKC, 256], BF16) for _ in range(2)]
        wub = [kb.sb(es, "wu", [128, NKC, 256], BF16) for _ in range(2)]
        wdb = [kb.sb(es, "wd", [128, NFC, 256], BF16) for _ in range(2)]
        sg = [kb.sb(es, "sg", [128, 512], F32) for _ in range(2)]
        r_wg, r_wu, r_wd, r_sg = ([P.res(), P.res()] for _ in range(4))
        scr = make_scratch(kb, es)
        wg_src = wg_ap.rearrange("(kc p) n -> p kc n", p=128)
        wu_src = wu_ap.rearrange("(kc p) n -> p kc n", p=128)
        wd_src = wd_ap.rearrange("(kc p) n -> p kc n", p=128)
        gu_banks = [(1, 2), (3, 4)]
        y_banks = [5, 6]
        cnt = 0
        ycnt = 0
        wcnt = 0
        dcnt = 0
        for g in groups:
            off = 0
            for t in g:
                t.off = off
                off += t.n
            r_xn = {id(t): P.res() for t in g}
            r_H = {(id(t), j): P.res() for t in g for j in range(NFC)}
            for t in g:
                emit_normmod(kb, scr, t, t.A, t.B, t.modres,
                             lambda kc, o=t.off, n=t.n: xn(kc, o, o + n), r_xn[id(t)])
            for j2 in range(NFC // 2):
                s = wcnt % 2
                wcnt += 1
                P.dma("pool", f"wg{s}", wgb[s][:], wg_src[:, :, j2 * 256:(j2 + 1) * 256], writes=[r_wg[s]])
                P.dma("pool", f"wu{s}", wub[s][:], wu_src[:, :, j2 * 256:(j2 + 1) * 256], writes=[r_wu[s]])
                for jj in range(2):
                    j = j2 * 2 + jj
                    for t in g:
                        n = t.n
                        bg_, bu_ = gu_banks[cnt % 2]
                        q = cnt % 2
                        cnt += 1
                        for kc in range(NKC):
                            P.op("pe", lambda e, s=s, jj=jj, kc=kc, o=t.off, n=n, bg_=bg_: e.matmul(
                                kb.ps[bg_][:, :n], wgb[s][:, kc, jj * 128:(jj + 1) * 128], xn(kc, o, o + n),
                                start=(kc == 0), stop=(kc == NKC - 1)),
                                reads=[r_wg[s], r_xn[id(t)]], writes=[kb.psr[bg_]])
                        for kc in range(NKC):
                            P.op("pe", lambda e, s=s, jj=jj, kc=kc, o=t.off, n=n, bu_=bu_: e.matmul(
                                kb.ps[bu_][:, :n], wub[s][:, kc, jj * 128:(jj + 1) * 128], xn(kc, o, o + n),
                                start=(kc == 0), stop=(kc == NKC - 1)),
                                reads=[r_wu[s], r_xn[id(t)]], writes=[kb.psr[bu_]])
                        P.op("act", lambda e, q=q, n=n, bg_=bg_: e.activation(sg[q][:, :n], kb.ps[bg_][:, :n], AF.Silu),
                             reads=[kb.psr[bg_]], writes=[r_sg[q]])
                        P.op("dve", lambda e, q=q, n=n, bu_=bu_, j=j, o=t.off: e.tensor_tensor(
                            H[:, j, o:o + n], sg[q][:, :n], kb.ps[bu_][:, :n], ALU.mult),
                            reads=[r_sg[q], kb.psr[bu_]], writes=[r_H[(id(t), j)]])
            for i4 in range(4):
                s = dcnt % 2
                dcnt += 1
                P.dma("pool", f"wd{s}", wdb[s][:], wd_src[:, :, i4 * 256:(i4 + 1) * 256], writes=[r_wd[s]])
                for ii in range(2):
                    i = i4 * 2 + ii
                    for t in g:
                        n = t.n
                        by = y_banks[ycnt % 2]
                        ycnt += 1
                        for kc in range(NFC):
                            P.op("pe", lambda e, s=s, ii=ii, kc=kc, o=t.off, n=n, by=by: e.matmul(
                                kb.ps[by][:, :n], wdb[s][:, kc, ii * 128:(ii + 1) * 128], H[:, kc, o:o + n],
                                start=(kc == 0), stop=(kc == NFC - 1)),
                                reads=[r_wd[s], r_H[(id(t), kc)]], writes=[kb.psr[by]])
                        P.op("dve", lambda e, i=i, t=t, n=n, by=by, hg=t.HG: e.scalar_tensor_tensor(
                            t.x[i], kb.ps[by][:, :n], hg[:, i:i + 1], t.x[i], ALU.mult, ALU.add),
                            reads=[kb.psr[by], t.xres[i], t.modres], writes=[t.xres[i]])
        P.barrier()


def make_tiles(x_t, P, ntok, width=512):
    tiles = []
    for a in range(0, ntok, width):
        n = min(width, ntok - a)
        tiles.append(Tile([x_t[:, kc, a:a + n] for kc in range(NKC)], [P.res() for _ in range(NKC)], n))
    return tiles


def build_A():
    kb = KB()
    P = kb.P
    xT = kb.inp("xT", [NKC, 128, TPC])
    cxT = kb.inp("cxT", [NKC, 128, 256])
    cvec = kb.inp("cvec", [128, 8, 2])
    modw = kb.inp("modw", [D, 5 * D])
    modb = kb.inp("modb", [128, 40])
    normg = kb.inp("normg", [128, 2, 8])
    wg = kb.inp("wg", [D, DFF])
    wu = kb.inp("wu", [D, DFF])
    wd = kb.inp("wd", [DFF, D])
    w_in = kb.inp("w_in", [D, 3 * D])
    cst = kb.inp("cst", [128, 3, 128])
    rope = kb.inp("rope", [128, 2, TPC])
    x1T = kb.outp("x1T", [NKC, 128, TPC])
    if DEBUG_STOP in (0, 3, 4, 5, 6, 7):
        qkT = kb.outp("qkT", [16, 128, TPC], BF16)
        vtm = kb.outp("vtm", [16, 128, 1024], BF16)
        kcT = kb.outp("kcT", [8, 128, 256], BF16)
        vc = kb.outp("vc", [2, 128, 1024], BF16)

    with ExitStack() as es:
        kb.load_consts(es, cst)
        x_t = kb.sb(es, "x", [128, NKC, TPC], F32)
        xc_t = kb.sb(es, "xc", [128, NKC, 256], F32)
        ng = kb.sb(es, "ng", [128, 2, 8], F32)
        r_ng = P.res()
        P.dma("sp", "ng", ng[:], normg, writes=[r_ng])
        lat = make_tiles(x_t, P, TPC)
        ctx = make_tiles(xc_t, P, 256)
        for kc in range(NKC):
            for t in lat:
                pass
            P.dma("sp", f"xin{kc}", x_t[:, kc, :], xT[kc], writes=[t.xres[kc] for t in lat])
            P.dma("sp", f"xcin{kc}", xc_t[:, kc, :], cxT[kc], writes=[ctx[0].xres[kc]])
        if DEBUG_STOP == -1:
            for kc in range(NKC):
                kb.store(x1T[kc], x_t[:, kc, :], [t.xres[kc] for t in lat], f"x{kc}")
            kb.finish()
            return kb.nc
        mvec, r_mvec = emit_mod(kb, es, modw, 40, cvec, 2, modb, "m0")
        if DEBUG_STOP == -2:
            for kc in range(NKC):
                kb.store(x1T[kc], x_t[:, kc, :], [t.xres[kc] for t in lat], f"x{kc}")
            kb.finish()
            return kb.nc
        ab = {}
        for v in range(2):
            A1, B1, G1, r1 = emit_ab(kb, es, mvec, r_mvec, v, 0, 1, ng[:, 0, :], r_ng, f"f1v{v}", i_gate=2, gate_mul=0.5)
            A2, B2, _, r2 = emit_ab(kb, es, mvec, r_mvec, v, 3, 4, ng[:, 1, :], r_ng, f"qkv{v}")
            ab[v] = (A1, B1, G1, r1, A2, B2, r2)
        for t in lat:
            t.A, t.B, t.HG, t.modres = ab[0][0], ab[0][1], ab[0][2], ab[0][3]
        for t in ctx:
            t.A, t.B, t.HG, t.modres = ab[1][0], ab[1][1], ab[1][2], ab[1][3]
        if DEBUG_STOP != 1:
            emit_ffn(kb, [lat[0:2] + ctx, lat[2:4]], wg, wu, wd)
        for kc in range(NKC):
            kb.store(x1T[kc], x_t[:, kc, :], [t.xres[kc] for t in lat], f"x{kc}")

        if DEBUG_STOP in (1, 2):
            kb.finish()
            return kb.nc
        with ExitStack() as es2:
            wi = kb.sb(es2, "wi", [128, NKC, 3 * D], BF16)
            r_wi = [P.res() for _ in range(6)]
            wi_src = w_in.rearrange("(kc p) n -> p kc n", p=128)
            for g6 in range(6):
                P.dma("pool", f"wi{g6}", wi[:, :, g6 * 512:(g6 + 1) * 512], wi_src[:, :, g6 * 512:(g6 + 1) * 512],
                      writes=[r_wi[g6]])
            rp = kb.sb(es2, "rope", [128, 2, TPC], F32)
            r_rp = P.res()
            P.dma("sp", "rope", rp[:], rope, writes=[r_rp])
            scr = make_scratch(kb, es2)
            xn2 = [kb.sb(es2, "xn2", [128, NKC, 512], BF16) for _ in range(2)]
            r_xn2 = [P.res(), P.res()]
            stg = [kb.sb(es2, "stg", [128, 512], BF16) for _ in range(4)]
            r_stg = [P.res() for _ in range(4)]
            qs = [kb.sb(es2, "qs", [128, 512], BF16) for _ in range(2)]
            r_qs = [P.res(), P.res()]
            t1 = [kb.sb(es2, "t1", [128, 512], F32) for _ in range(2)]
            r_t1 = [P.res(), P.res()]
            t2 = [kb.sb(es2, "t2", [128, 512], F32) for _ in range(2)]
            r_t2 = [P.res(), P.res()]
            banks = [1, 2, 3, 4]
            bc = 0
            sc_ = 0
            rc = 0
            for ti, t in enumerate(lat + ctx):
                is_ctx = ti >= len(lat)
                v = 1 if is_ctx else 0
                n = t.n
                tok0 = 0 if is_ctx else ti * 512
                xb = ti % 2
                emit_normmod(kb, scr, t, ab[v][4], ab[v][5], ab[v][6],
                             lambda kc, xb=xb, n=n: xn2[xb][:, kc, :n], r_xn2[xb])
                fm = [(c, c) for c in range(0, 8)] + [(12 + c, 8 + c) for c in range(8)]
                for (wc, oc) in fm:
                    kind = oc // 4
                    if is_ctx and kind in (0, 2):
                        continue
                    b = banks[bc % 4]
                    bc += 1
                    for kc in range(NKC):
                        P.op("pe", lambda e, b=b, wc=wc, kc=kc, xb=xb, n=n: e.matmul(
                            kb.ps[b][:, :n], wi[:, kc, wc * 128:(wc + 1) * 128], xn2[xb][:, kc, :n],
                            start=(kc == 0), stop=(kc == NKC - 1)),
                            reads=[r_wi[wc // 4], r_xn2[xb]], writes=[kb.psr[b]])
                    s = sc_ % 4
                    sc_ += 1
                    if kind >= 2 and not is_ctx and DEBUG_STOP not in (4, 5):
                        r = rc % 2
                        rc += 1
                        if DEBUG_STOP != 7:
                            b2 = banks[bc % 4]
                            bc += 1
                            P.op("act", lambda e, r=r, b=b, n=n: e.activation(qs[r][:, :n], kb.ps[b][:, :n], AF.Identity),
                                 reads=[kb.psr[b]], writes=[r_qs[r]])
                            P.op("pe", lambda e, r=r, b2=b2, n=n: e.matmul(kb.ps[b2][:, :n], kb.swap, qs[r][:, :n],
                                                                           start=True, stop=True),
                                 reads=[r_qs[r], kb.r_cst], writes=[kb.psr[b2]])
                        else:
                            b2 = b
                        if DEBUG_STOP != 6:
                            P.op("dve", lambda e, r=r, b=b, n=n, tok0=tok0: e.tensor_tensor(
                                t1[r][:, :n], kb.ps[b][:, :n], rp[:, 0, tok0:tok0 + n], ALU.mult),
                                reads=[kb.psr[b], r_rp] + ([r_qs[r]] if DEBUG_STOP != 7 else []), writes=[r_t1[r]])
                            P.op("dve", lambda e, r=r, b2=b2, n=n, tok0=tok0: e.tensor_tensor(
                                t2[r][:, :n], kb.ps[b2][:, :n], rp[:, 1, tok0:tok0 + n], ALU.mult),
                                reads=[kb.psr[b2], r_rp], writes=[r_t2[r]])
                            P.op("dve", lambda e, r=r, s=s, n=n: e.tensor_tensor(
                                stg[s][:, :n], t1[r][:, :n], t2[r][:, :n], ALU.add),
                                reads=[r_t1[r], r_t2[r]], writes=[r_stg[s]])
                        else:
                            P.op("act", lambda e, s=s, b2=b2, n=n: e.activation(
                                stg[s][:, :n], kb.ps[b2][:, :n], AF.Identity),
                                reads=[kb.psr[b2]], writes=[r_stg[s]])
                    else:
                        scale = 0.125 if kind == 0 else 1.0
                        P.op("act", lambda e, s=s, b=b, n=n, scale=scale: e.activation(
                            stg[s][:, :n], kb.ps[b][:, :n], AF.Identity, scale=scale),
                            reads=[kb.psr[b]], writes=[r_stg[s]])
                    if is_ctx:
                        dst = kcT[(oc - 4) if kind == 1 else (oc - 8)]
                        kb.store(dst, stg[s][:, :n], [r_stg[s]], f"stg{s}")
                    else:
                        kb.store(qkT[oc][:, tok0:tok0 + n], stg[s][:, :n], [r_stg[s]], f"stg{s}")
                for tb in range(0 if DEBUG_STOP in (3, 5, 6, 7) else n // 128):
                    for vi, wc0 in enumerate((8, 20)):
                        b = banks[bc % 4]
                        bc += 1
                        for kc in range(NKC):
                            P.op("pe", lambda e, b=b, wc0=wc0, kc=kc, xb=xb, tb=tb: e.matmul(
                                kb.ps[b][:, :], xn2[xb][:, kc, tb * 128:(tb + 1) * 128],
                                wi[:, kc, wc0 * 128:(wc0 + 4) * 128],
                                start=(kc == 0), stop=(kc == NKC - 1)),
                                reads=[r_wi[wc0 // 4], r_xn2[xb]], writes=[kb.psr[b]])
                        s = sc_ % 4
                        sc_ += 1
                        if vi == 0:
                            P.op("act", lambda e, s=s, b=b: e.activation(stg[s][:], kb.ps[b][:], AF.Identity),
                                 reads=[kb.psr[b]], writes=[r_stg[s]])
                        else:
                            P.op("dve", lambda e, s=s, b=b: e.tensor_copy(stg[s][:], kb.ps[b][:]),
                                 reads=[kb.psr[b]], writes=[r_stg[s]])
                        if is_ctx:
                            kb.store(vc[tb][:, vi * 512:(vi + 1) * 512], stg[s][:], [r_stg[s]], f"stg{s}")
                        else:
                            kb.store(vtm[ti * 4 + tb][:, vi * 512:(vi + 1) * 512], stg[s][:], [r_stg[s]], f"stg{s}")
            P.barrier()
        kb.finish()
    return kb.nc


def build_B():
    kb = KB()
    P = kb.P
    qaT = kb.inp("qaT", [4, 128, TPC], BF16)
    qbT = kb.inp("qbT", [4, 128, TPC], BF16)
    kaB = kb.inp("kaB", [4, 128, 2816], BF16)
    vaP = kb.inp("vaP", [4, 128, 22 * 256], BF16)
    kbT = kb.inp("kbT", [4, 128, 8448], BF16)
    vbT = kb.inp("vbT", [4, 128, 66 * 128], BF16)
    TT = kb.inp("TT", [128, 8, 22 * 64], BF16)
    Rv = kb.inp("Rv", [4, 2, 8 * 512], BF16)
    E2 = kb.inp("E2", [2, 128], BF16)
    onesP = kb.inp("onesP", [128, 2, 128], BF16)
    x1T = kb.inp("x1T", [NKC, 128, TPC])
    cst = kb.inp("cst", [128, 3, 128])
    lamv = kb.inp("lamv", [128, 4, 64])
    sublng = kb.inp("sublng", [128, 1])
    cvec = kb.inp("cvec", [128, 8, 1])
    modwA = kb.inp("modwA", [D, 4 * D])
    modbA = kb.inp("modbA", [128, 32])
    modwB = kb.inp("modwB", [D, 4 * D])
    modbB = kb.inp("modbB", [128, 32])
    normg = kb.inp("normg", [128, 2, 8])
    w_out = kb.inp("w_out", [D, D])
    wg1 = kb.inp("wg1", [D, DFF])
    wu1 = kb.inp("wu1", [D, DFF])
    wd1 = kb.inp("wd1", [DFF, D])
    wg2 = kb.inp("wg2", [D, DFF])
    wu2 = kb.inp("wu2", [D, DFF])
    wd2 = kb.inp("wd2", [DFF, D])
    x4T = kb.outp("x4T", [NKC, 128, TPC])

    with ExitStack() as es:
        kb.load_consts(es, cst)
        an = kb.sb(es, "an", [128, NKC, TPC], BF16)
        r_an = [[P.res() for _ in range(4)] for _ in range(NKC)]
        ng = kb.sb(es, "ng", [128, 2, 8], F32)
        r_ng = P.res()
        P.dma("sp", "ng", ng[:], normg, writes=[r_ng])
        jobA = ModJob(kb, es, modwA, 32, cvec, 1, modbA, "mA", None, None, 7)
        jobB = ModJob(kb, es, modwB, 32, cvec, 1, modbB, "mB", None, None, 7)
        mvA, r_mvA, mvB, r_mvB = jobA.mvec, jobA.r_mvec, jobB.mvec, jobB.r_mvec
        mod_steps = [jobA] * 8 + [jobB] * 8
        lam_t = kb.sb(es, "lam", [128, 4, 64], F32)
        lsc = kb.sb(es, "lsc", [128, 8], F32)
        sg8 = kb.sb(es, "sg8", [128, 1], F32)
        r_lam, r_lsc, r_sg8 = P.res(), P.res(), P.res()
        P.dma("sp", "lam", lam_t[:], lamv, writes=[r_lam])
        P.dma("sp", "sg8", sg8[:], sublng, writes=[r_sg8])
        P.op("dve", lambda e: e.tensor_tensor(lam_t[:, 0, :], lam_t[:, 0, :], lam_t[:, 1, :], ALU.mult),
             reads=[r_lam], writes=[r_lam])
        P.op("dve", lambda e: e.tensor_tensor(lam_t[:, 2, :], lam_t[:, 2, :], lam_t[:, 3, :], ALU.mult),
             reads=[r_lam], writes=[r_lam])
        P.op("dve", lambda e: e.reduce_sum(lsc[:, 0:1], lam_t[:, 0, :], mybir.AxisListType.X), reads=[r_lam], writes=[r_lsc])
        P.op("dve", lambda e: e.reduce_sum(lsc[:, 1:2], lam_t[:, 2, :], mybir.AxisListType.X), reads=[r_lam], writes=[r_lsc])
        P.op("act", lambda e: e.activation(lsc[:, 2:4], lsc[:, 0:2], AF.Exp), reads=[r_lsc], writes=[r_lsc])
        LAM_INIT = 0.8 - 0.6 * math.exp(-0.3 * 0)
        P.op("dve", lambda e: e.tensor_tensor(lsc[:, 4:5], lsc[:, 3:4], lsc[:, 2:3], ALU.subtract), reads=[r_lsc], writes=[r_lsc])
        P.op("dve", lambda e: e.tensor_scalar_add(lsc[:, 5:6], lsc[:, 4:5], -LAM_INIT), reads=[r_lsc], writes=[r_lsc])
        P.op("dve", lambda e: e.tensor_scalar_mul(sg8[:], sg8[:], 1.0 - LAM_INIT), reads=[r_sg8], writes=[r_sg8])
        neglam = lsc[:, 5:6]

        with ExitStack() as es2:
            tt = kb.sb(es2, "tt", [128, 8, 22 * 64], BF16)
            r_tt = P.res()
            P.dma("sp", "tt", tt[:], TT, writes=[r_tt])
            e2 = kb.sb(es2, "e2", [2, 128], BF16)
            r_e2 = P.res()
            P.dma("sp", "e2", e2[:], E2, writes=[r_e2])
            op_ = kb.sb(es2, "onesP", [128, 2, 128], BF16)
            r_op = P.res()
            P.dma("sp", "onesP", op_[:], onesP, writes=[r_op])
            rv = [kb.sb(es2, "rv", [2, 8 * 512], BF16) for _ in range(2)]
            r_rv = [P.res(), P.res()]
            qsb = [kb.sb(es2, "qsb", [128, 512], BF16) for _ in range(2)]
            r_q = [P.res(), P.res()]
            kA = [kb.sb(es2, "kA", [128, 1280], BF16) for _ in range(2)]
            r_kA = [P.res(), P.res()]
            vP = [kb.sb(es2, "vP", [128, 10 * 256], BF16) for _ in range(2)]
            r_vP = [P.res(), P.res()]
            kB_ = [kb.sb(es2, "kB", [128, 8448], BF16) for _ in range(2)]
            r_kB = [P.res(), P.res()]
            vB_ = [kb.sb(es2, "vB", [128, 66 * 128], BF16) for _ in range(2)]
            r_vB = [P.res(), P.res()]
            pT = [kb.sb(es2, "pT", [128, 512], BF16) for _ in range(6)]
            r_pT = [P.res() for _ in range(6)]
            SB = [0, 1, 2, 3, 6]
            ev = [kb.sb(es2, "ev", [128, 512], F32) for _ in range(4)]
            r_ev = [P.res() for _ in range(4)]
            sqb = kb.sb(es2, "sqb", [128, 512], BF16)
            r_sqb = P.res()
            epst = kb.sb(es2, "epst", [128, 1], F32)
            r_epst = P.res()
            P.op("dve", lambda e: e.memset(epst[:], EPS), writes=[r_epst])
            mwj = [kb.sb(es2, "mwj", [128, 8, 512], BF16) for _ in range(2)]
            r_mwj = [P.res(), P.res()]
            jobA.mw = jobB.mw = mwj
            jobA.r_mw = jobB.r_mw = r_mwj
            acc = [kb.sb(es2, "acc", [128, 512], F32) for _ in range(2)]
            r_acc = [P.res(), P.res()]
            onesf = kb.sb(es2, "onesf", [128, 128], F32)
            r_onesf = P.res()
            P.op("dve", lambda e: e.memset(onesf[:], 1.0), writes=[r_onesf])
            onesPf = kb.sb(es2, "onesPf", [128, 2, 128], F32)
            r_onesPf = P.res()
            P.op("dve", lambda e: e.tensor_copy(onesPf[:], op_[:]), reads=[r_op], writes=[r_onesPf])
            sb_i = 0
            p_i = 0
            qcnt = 0
            acnt = 0
            for qt in range(4):
                rs = qt % 2
                P.dma("sp", f"rv{rs}", rv[rs][:], Rv[qt], writes=[r_rv[rs]])
                for g in range(4):
                    a = acnt % 2
                    acnt += 1
                    P.dma("sp", f"kA{a}", kA[a][:, 0:1024], kaB[g][:, qt * 512:qt * 512 + 1024], writes=[r_kA[a]])
                    P.dma("sp", f"kA{a}", kA[a][:, 1024:1280], kaB[g][:, 2560:2816], writes=[r_kA[a]])
                    P.dma("sp", f"vP{a}", vP[a][:, 0:8 * 256], vaP[g][:, qt * 4 * 256:(qt * 4 + 8) * 256], writes=[r_vP[a]])
                    P.dma("sp", f"vP{a}", vP[a][:, 8 * 256:10 * 256], vaP[g][:, 20 * 256:22 * 256], writes=[r_vP[a]])
                    qq = qcnt % 2
                    qcnt += 1
                    P.dma("sp", f"q{qq}", qsb[qq][:], qaT[g][:, qt * 512:(qt + 1) * 512], writes=[r_q[qq]])
                    nO, nL = 4, 5
                    its = [(jt, hh) for jt in range(10) for hh in range(2)]
                    NIT = len(its)
                    LA = 5
                    stash = {}
                    for step in range(NIT + LA):
                        if step < NIT:
                            jt, hh = its[step]
                            head = 2 * g + hh
                            rows = slice(hh * 64, (hh + 1) * 64)
                            b = SB[sb_i % 5]
                            sb_i += 1
                            band = jt < 8
                            P.op("pe", lambda e, b=b, a=a, rows=rows, jt=jt, qq=qq, band=band: e.matmul(
                                kb.ps[b][:], kA[a][rows, jt * 128:(jt + 1) * 128], qsb[qq][rows, :],
                                start=True, stop=(not band)),
                                reads=[r_kA[a], r_q[qq]], writes=[kb.psr[b]])
                            if band:
                                s_j = 14 - 2 * jt
                                P.op("pe", lambda e, b=b, head=head, s_j=s_j: e.matmul(
                                    kb.ps[b][:], kb.ident, tt[:, head, s_j * 64:(s_j + 8) * 64], start=False, stop=False),
                                    reads=[r_tt, kb.r_cst], writes=[kb.psr[b]])
                                P.op("pe", lambda e, b=b, rs=rs, jt=jt: e.matmul(
                                    kb.ps[b][:], e2[:, :], rv[rs][:, jt * 512:(jt + 1) * 512], start=False, stop=True),
                                    reads=[r_e2, r_rv[rs]], writes=[kb.psr[b]])
                            pi = p_i % 6
                            p_i += 1
                            P.op("act", lambda e, pi=pi, b=b: e.activation(pT[pi][:], kb.ps[b][:], AF.Exp),
                                 reads=[kb.psr[b]], writes=[r_pT[pi]])
                            stash[step] = pi
                        if step >= LA:
                            jt, hh = its[step - LA]
                            pi = stash.pop(step - LA)
                            first = (step - LA == 0)
                            last = (step - LA == NIT - 1)
                            P.op("pe", lambda e, pi=pi, a=a, jt=jt, hh=hh, first=first, last=last: e.matmul(
                                kb.ps[nO][:], vP[a][:, (jt * 2 + hh) * 128:(jt * 2 + hh + 1) * 128], pT[pi][:], start=first, stop=last),
                                reads=[r_vP[a], r_pT[pi]], writes=[kb.psr[nO]])
                            if jt == 0:
                                P.op("dve", lambda e, pi=pi, hh=hh: e.tensor_copy(acc[hh][:], pT[pi][:]),
                                     reads=[r_pT[pi]], writes=[r_acc[hh]])
                            else:
                                P.op("dve", lambda e, pi=pi, hh=hh: e.tensor_tensor(acc[hh][:], acc[hh][:], pT[pi][:], ALU.add),
                                     reads=[r_pT[pi], r_acc[hh]], writes=[r_acc[hh]])
                    for hh in range(2):
                        P.op("pe", lambda e, hh=hh: e.matmul(kb.ps[nL][:], onesPf[:, hh, :], acc[hh][:],
                                                             start=(hh == 0), stop=(hh == 1)),
                             reads=[r_onesPf, r_acc[hh]], writes=[kb.psr[nL]])
                    P.op("dve", lambda e: e.reciprocal(ev[0][:], kb.ps[nL][:]), reads=[kb.psr[nL]], writes=[r_ev[0]])
                    P.op("dve", lambda e, g=g, qt=qt: e.tensor_tensor(
                        an[:, g, qt * 512:(qt + 1) * 512], kb.ps[nO][:], ev[0][:], ALU.mult),
                        reads=[kb.psr[nO], r_ev[0]], writes=[r_an[g][qt]])
            for h in range(4):
                a = h % 2
                P.dma("sp", f"kB{a}", kB_[a][:], kbT[h], writes=[r_kB[a]])
                P.dma("sp", f"vB{a}", vB_[a][:], vbT[h], writes=[r_vB[a]])
                for qt in range(4):
                    qq = qcnt % 2
                    qcnt += 1
                    P.dma("sp", f"q{qq}", qsb[qq][:], qbT[h][:, qt * 512:(qt + 1) * 512], writes=[r_q[qq]])
                    bO = [4, 5]
                    bL = [SB[sb_i % 5], SB[(sb_i + 1) % 5]]
                    its = [(kt, m) for kt in range(66) for m in range(2)]
                    NIT = len(its)
                    LA = 5
                    stash = {}
                    for step in range(NIT + LA):
                        if step < NIT:
                            kt, m = its[step]
                            rows = slice(m * 64, (m + 1) * 64)
                            b = SB[sb_i % 5]
                            sb_i += 1
                            P.op("pe", lambda e, b=b, a=a, rows=rows, kt=kt, qq=qq: e.matmul(
                                kb.ps[b][:], kB_[a][rows, kt * 128:(kt + 1) * 128], qsb[qq][rows, :],
                                start=True, stop=True),
                                reads=[r_kB[a], r_q[qq]], writes=[kb.psr[b]])
                            pi = p_i % 6
                            p_i += 1
                            P.op("act", lambda e, pi=pi, b=b: e.activation(pT[pi][:], kb.ps[b][:], AF.Exp, scale=0.125),
                                 reads=[kb.psr[b]], writes=[r_pT[pi]])
                            stash[step] = pi
                        if step >= LA:
                            kt, m = its[step - LA]
                            pi = stash.pop(step - LA)
                            P.op("pe", lambda e, pi=pi, a=a, kt=kt, m=m: e.matmul(
                                kb.ps[bO[m]][:], vB_[a][:, kt * 128:(kt + 1) * 128], pT[pi][:], start=(kt == 0), stop=(kt == 65)),
                                reads=[r_vB[a], r_pT[pi]], writes=[kb.psr[bO[m]]])
                            if kt == 0:
                                P.op("dve", lambda e, pi=pi, m=m: e.tensor_copy(acc[m][:], pT[pi][:]),
                                     reads=[r_pT[pi]], writes=[r_acc[m]])
                            else:
                                P.op("dve", lambda e, pi=pi, m=m: e.tensor_tensor(acc[m][:], acc[m][:], pT[pi][:], ALU.add),
                                     reads=[r_pT[pi], r_acc[m]], writes=[r_acc[m]])
                    bL = [SB[sb_i % 5], SB[(sb_i + 1) % 5]]
                    sb_i += 2
                    for m in range(2):
                        P.op("pe", lambda e, m=m, bL=bL: e.matmul(kb.ps[bL[m]][:], onesf[:], acc[m][:], start=True, stop=True),
                             reads=[r_onesf, r_acc[m]], writes=[kb.psr[bL[m]]])
                    for m in range(2):
                        P.op("dve", lambda e, m=m, bL=bL: e.reciprocal(ev[m][:], kb.ps[bL[m]][:]),
                             reads=[kb.psr[bL[m]]], writes=[r_ev[m]])
                        P.op("dve", lambda e, m=m: e.tensor_tensor(ev[m][:], kb.ps[bO[m]][:], ev[m][:], ALU.mult),
                             reads=[kb.psr[bO[m]], r_ev[m]], writes=[r_ev[m]])
                    P.op("dve", lambda e: e.scalar_tensor_tensor(ev[2][:], ev[1][:], neglam, ev[0][:], ALU.mult, ALU.add),
                         reads=[r_ev[0], r_ev[1], r_lsc], writes=[r_ev[2]])
                    P.op("act", lambda e: e.activation(sqb[:], ev[2][:], AF.Square), reads=[r_ev[2]], writes=[r_sqb])
                    b = SB[sb_i % 5]
                    sb_i += 1
                    P.op("pe", lambda e, b=b: e.matmul(kb.ps[b][:], kb.ones, sqb[:], start=True, stop=True),
                         reads=[r_sqb, kb.r_cst], writes=[kb.psr[b]])
                    P.op("act", lambda e, b=b: e.activation(ev[3][:], kb.ps[b][:], AF.Sqrt, bias=epst[:, 0:1], scale=1.0 / 128),
                         reads=[kb.psr[b], r_epst], writes=[r_ev[3]])
                    P.op("dve", lambda e: e.reciprocal(ev[3][:], ev[3][:]), reads=[r_ev[3]], writes=[r_ev[3]])
                    P.op("dve", lambda e: e.tensor_tensor(ev[2][:], ev[2][:], ev[3][:], ALU.mult),
                         reads=[r_ev[2], r_ev[3]], writes=[r_ev[2]])
                    P.op("act", lambda e, h=h, qt=qt: e.activation(
                        an[:, 4 + h, qt * 512:(qt + 1) * 512], ev[2][:], AF.Identity, scale=sg8[:, 0:1]),
                        reads=[r_ev[2], r_sg8], writes=[r_an[4 + h][qt]])
                    if mod_steps:
                        mod_steps.pop(0).step()
            P.barrier()

        with ExitStack() as es3:
            x_t = kb.sb(es3, "x", [128, NKC, TPC], F32)
            lat = make_tiles(x_t, P, TPC)
            for kc in range(NKC):
                P.dma("sp", f"xin{kc}", x_t[:, kc, :], x1T[kc], writes=[t.xres[kc] for t in lat])
            G5 = mvA[:, 0:8, 0]
            with ExitStack() as es4:
                wo = kb.sb(es4, "wo", [128, NKC, D], BF16)
                r_wo = P.res()
                P.dma("pool", "wo", wo[:], w_out.rearrange("(kc p) n -> p kc n", p=128), writes=[r_wo])
                bc = 0
                for ti, t in enumerate(lat):
                    for oc in range(NKC):
                        b = 1 + bc % 4
                        bc += 1
                        for kc in range(NKC):
                            P.op("pe", lambda e, b=b, oc=oc, kc=kc, ti=ti: e.matmul(
                                kb.ps[b][:], wo[:, kc, oc * 128:(oc + 1) * 128], an[:, kc, ti * 512:(ti + 1) * 512],
                                start=(kc == 0), stop=(kc == NKC - 1)),
                                reads=[r_wo, r_an[kc][ti]], writes=[kb.psr[b]])
                        P.op("dve", lambda e, b=b, oc=oc, t=t: e.scalar_tensor_tensor(
                            t.x[oc], kb.ps[b][:], G5[:, oc:oc + 1], t.x[oc], ALU.mult, ALU.add),
                            reads=[kb.psr[b], t.xres[oc], r_mvA], writes=[t.xres[oc]])
                P.barrier()
            A1, B1, G1, r1 = emit_ab(kb, es3, mvA, r_mvA, 0, 1, 2, ng[:, 0, :], r_ng, "f2", i_gate=3, gate_mul=0.5)
            for t in lat:
                t.A, t.B, t.HG, t.modres = A1, B1, G1, r1
            xn_ext = lambda kc, a, b: an[:, kc, a:b]
            emit_ffn(kb, [lat[0:2], lat[2:4]], wg1, wu1, wd1, xn_ext=xn_ext)
            A2, B2, G2, r2 = emit_ab(kb, es3, mvB, r_mvB, 0, 0, 1, ng[:, 1, :], r_ng, "f3", i_gate=2, gate_mul=0.5)
            for t in lat:
                t.A, t.B, t.HG, t.modres = A2, B2, G2, r2
            emit_ffn(kb, [lat[0:2], lat[2:4]], wg2, wu2, wd2, xn_ext=xn_ext)
            for kc in range(NKC):
                kb.store(x4T[kc], x_t[:, kc, :], [t.xres[kc] for t in lat], f"x{kc}")
            kb.finish()
    return kb.nc


def build_C():
    kb = KB()
    P = kb.P
    x4T = kb.inp("x4T", [NKC, 128, TPC])
    xhT = kb.inp("xhT", [128, NKC, 2])
    hmask = kb.inp("hmask", [128, 2])
    cvec = kb.inp("cvec", [128, 8, 1])
    modw = kb.inp("modw", [D, 6 * D])
    modb = kb.inp("modb", [128, 48])
    normg = kb.inp("normg", [128, 3, 8])
    cw = kb.inp("cw", [128, 3, 8])
    cw_in = kb.inp("cw_in", [D, 3 * D])
    cw_out = kb.inp("cw_out", [D, D])
    wg = kb.inp("wg", [D, DFF])
    wu = kb.inp("wu", [D, DFF])
    wd = kb.inp("wd", [DFF, D])
    cst = kb.inp("cst", [128, 3, 128])
    outT = kb.outp("outT", [NKC, 128, TPC])

    with ExitStack() as es:
        kb.load_consts(es, cst)
        x_t = kb.sb(es, "x", [128, NKC, TPC], F32)
        xh_t = kb.sb(es, "xh", [128, NKC, 2], F32)
        hm = kb.sb(es, "hm", [128, 2], F32)
        ng = kb.sb(es, "ng", [128, 3, 8], F32)
        cwt = kb.sb(es, "cwt", [128, 3, 8], F32)
        r_ng, r_hm, r_cw = P.res(), P.res(), P.res()
        P.dma("sp", "ng", ng[:], normg, writes=[r_ng])
        P.dma("sp", "hm", hm[:], hmask, writes=[r_hm])
        P.dma("sp", "cw", cwt[:], cw, writes=[r_cw])
        lat = make_tiles(x_t, P, TPC)
        halo = make_tiles(xh_t, P, 2)
        for kc in range(NKC):
            P.dma("sp", f"xin{kc}", x_t[:, kc, :], x4T[kc], writes=[t.xres[kc] for t in lat])
        P.dma("sp", "xh", xh_t[:], xhT, writes=halo[0].xres)
        mv, r_mv = emit_mod(kb, es, modw, 48, cvec, 1, modb, "mC")
        Am, Bm, _, rm = emit_ab(kb, es, mv, r_mv, 0, 0, 1, ng[:, 0, :], r_ng, "cm")
        G5 = mv[:, 16:24, 0]
        with ExitStack() as es2:
            U = kb.sb(es2, "U", [128, NKC, TPC + 2], F32)
            r_U = [[P.res() for _ in range(6)] for _ in range(NKC)]
            scr = make_scratch(kb, es2)
            xn = [kb.sb(es2, "xnc", [128, NKC, 512], BF16)] * 2
            r_xn = [P.res()] * 2
            tmpc = [kb.sb(es2, "tmpc", [128, 512], F32) for _ in range(2)]
            r_tmpc = [P.res(), P.res()]
            wi_src = cw_in.rearrange("(kc p) n -> p kc n", p=128)
            with ExitStack() as es3:
                wch = kb.sb(es3, "wch", [128, NKC, 2 * D], BF16)
                r_wch = [P.res() for _ in range(4)]
                for g4 in range(4):
                    P.dma("pool", f"wch{g4}", wch[:, :, g4 * 512:(g4 + 1) * 512],
                          wi_src[:, :, D + g4 * 512:D + (g4 + 1) * 512], writes=[r_wch[g4]])
                bc = 0
                for ti, t in enumerate(lat + halo):
                    is_h = ti >= 4
                    n = t.n
                    xb = ti % 2
                    emit_normmod(kb, scr, t, Am, Bm, rm, lambda kc, xb=xb, n=n: xn[xb][:, kc, :n], r_xn[xb])
                    for c in range(NKC):
                        b1 = 1 + bc % 4
                        b2 = 1 + (bc + 1) % 4
                        bc += 2
                        for (bb, wc) in ((b1, c), (b2, 8 + c)):
                            for kc in range(NKC):
                                P.op("pe", lambda e, bb=bb, wc=wc, kc=kc, xb=xb, n=n: e.matmul(
                                    kb.ps[bb][:, :n], wch[:, kc, wc * 128:(wc + 1) * 128], xn[xb][:, kc, :n],
                                    start=(kc == 0), stop=(kc == NKC - 1)),
                                    reads=[r_wch[wc // 4], r_xn[xb]], writes=[kb.psr[bb]])
                        q = c % 2
                        P.op("act", lambda e, q=q, b1=b1, n=n: e.activation(tmpc[q][:, :n], kb.ps[b1][:, :n], AF.Identity),
                             reads=[kb.psr[b1]], writes=[r_tmpc[q]])
                        if not is_h:
                            P.op("dve", lambda e, q=q, b2=b2, c=c, ti=ti: e.tensor_tensor(
                                U[:, c, 1 + ti * 512:1 + (ti + 1) * 512], tmpc[q][:], kb.ps[b2][:], ALU.mult),
                                reads=[r_tmpc[q], kb.psr[b2]], writes=[r_U[c][1 + ti]])
                        else:
                            P.op("dve", lambda e, q=q, b2=b2: e.tensor_tensor(
                                tmpc[q][:, 0:2], tmpc[q][:, 0:2], kb.ps[b2][:, 0:2], ALU.mult),
                                reads=[r_tmpc[q], kb.psr[b2]], writes=[r_tmpc[q]])
                            P.op("dve", lambda e, q=q, c=c: e.tensor_tensor(
                                U[:, c, 0:1], tmpc[q][:, 0:1], hm[:, 0:1], ALU.mult),
                                reads=[r_tmpc[q], r_hm], writes=[r_U[c][0]])
                            P.op("dve", lambda e, q=q, c=c: e.tensor_tensor(
                                U[:, c, TPC + 1:TPC + 2], tmpc[q][:, 1:2], hm[:, 1:2], ALU.mult),
                                reads=[r_tmpc[q], r_hm], writes=[r_U[c][5]])
                P.barrier()
            with ExitStack() as es3:
                wbg = kb.sb(es3, "wbg", [128, NKC, D], BF16)
                wo = kb.sb(es3, "wo", [128, NKC, D], BF16)
                r_wbg, r_wo = P.res(), P.res()
                P.dma("pool", "wbg", wbg[:], wi_src[:, :, 0:D], writes=[r_wbg])
                P.dma("pool", "wo", wo[:], cw_out.rearrange("(kc p) n -> p kc n", p=128), writes=[r_wo])
                Z = [kb.sb(es3, "Z", [128, NKC, 512], BF16)] * 2
                r_Z = [[P.res() for _ in range(NKC)]] * 2
                bc = 0
                for ti, t in enumerate(lat):
                    xb = ti % 2
                    zb = ti % 2
                    emit_normmod(kb, scr, t, Am, Bm, rm, lambda kc, xb=xb: xn[xb][:, kc, :], r_xn[xb])
                    for c in range(NKC):
                        b = 1 + bc % 4
                        bc += 1
                        for kc in range(NKC):
                            P.op("pe", lambda e, b=b, c=c, kc=kc, xb=xb: e.matmul(
                                kb.ps[b][:], wbg[:, kc, c * 128:(c + 1) * 128], xn[xb][:, kc, :],
                                start=(kc == 0), stop=(kc == NKC - 1)),
                                reads=[r_wbg, r_xn[xb]], writes=[kb.psr[b]])
                        q = c % 2
                        base = 1 + ti * 512
                        ru = [r_U[c][k] for k in range(6)]
                        P.op("dve", lambda e, q=q, c=c, base=base: e.tensor_scalar_mul(
                            tmpc[q][:], U[:, c, base:base + 512], cwt[:, 1, c:c + 1]),
                            reads=ru + [r_cw], writes=[r_tmpc[q]])
                        P.op("dve", lambda e, q=q, c=c, base=base: e.scalar_tensor_tensor(
                            tmpc[q][:], U[:, c, base - 1:base + 511], cwt[:, 0, c:c + 1], tmpc[q][:], ALU.mult, ALU.add),
                            reads=ru + [r_cw, r_tmpc[q]], writes=[r_tmpc[q]])
                        P.op("dve", lambda e, q=q, c=c, base=base: e.scalar_tensor_tensor(
                            tmpc[q][:], U[:, c, base + 1:base + 513], cwt[:, 2, c:c + 1], tmpc[q][:], ALU.mult, ALU.add),
                            reads=ru + [r_cw, r_tmpc[q]], writes=[r_tmpc[q]])
                        P.op("dve", lambda e, q=q, c=c, b=b, zb=zb: e.tensor_tensor(
                            Z[zb][:, c, :], tmpc[q][:], kb.ps[b][:], ALU.mult),
                            reads=[r_tmpc[q], kb.psr[b]], writes=[r_Z[zb][c]])
                    for oc in range(NKC):
                        b = 1 + bc % 4
                        bc += 1
                        for kc in range(NKC):
                            P.op("pe", lambda e, b=b, oc=oc, kc=kc, zb=zb: e.matmul(
                                kb.ps[b][:], wo[:, kc, oc * 128:(oc + 1) * 128], Z[zb][:, kc, :],
                                start=(kc == 0), stop=(kc == NKC - 1)),
                                reads=[r_wo, r_Z[zb][kc]], writes=[kb.psr[b]])
                        P.op("dve", lambda e, b=b, oc=oc, t=t: e.scalar_tensor_tensor(
                            t.x[oc], kb.ps[b][:], G5[:, oc:oc + 1], t.x[oc], ALU.mult, ALU.add),
                            reads=[kb.psr[b], t.xres[oc], r_mv], writes=[t.xres[oc]])
                P.barrier()
        A1, B1, G1, r1 = emit_ab(kb, es, mv, r_mv, 0, 3, 4, ng[:, 1, :], r_ng, "f4", i_gate=5, gate_mul=0.5)
        for t in lat:
            t.A, t.B, t.HG, t.modres = A1, B1, G1, r1
        emit_ffn(kb, [lat[0:2], lat[2:4]], wg, wu, wd)
        with ExitStack() as es2:
            scr = make_scratch(kb, es2)
            ob = [kb.sb(es2, "ob", [128, 512], F32) for _ in range(4)]
            r_ob = [P.res() for _ in range(4)]
            oc_ = 0
            for ti, t in enumerate(lat):
                n = t.n
                ps_stat, r_ps = kb.ps[0], kb.psr[0]
                for kc in range(NKC):
                    s = kc % 2
                    P.op("act", lambda e, s=s, kc=kc, t=t: e.activation(scr["sq"][s][:], t.x[kc], AF.Square),
                         reads=[t.xres[kc]], writes=[scr["r_sq"][s]])
                    P.op("pe", lambda e, s=s, kc=kc: e.matmul(ps_stat[:], kb.ones, scr["sq"][s][:],
                                                              start=(kc == 0), stop=(kc == NKC - 1)),
                         reads=[scr["r_sq"][s], kb.r_cst], writes=[r_ps])
                rstd = scr["rstd"]
                P.op("act", lambda e: e.activation(rstd[:], ps_stat[:], AF.Sqrt, bias=scr["eps"][:, 0:1], scale=1.0 / D),
                     reads=[r_ps, scr["r_eps"]], writes=[scr["r_rstd"]])
                P.op("dve", lambda e: e.reciprocal(rstd[:], rstd[:]), reads=[scr["r_rstd"]], writes=[scr["r_rstd"]])
                for kc in range(NKC):
                    o = oc_ % 4
                    oc_ += 1
                    P.op("dve", lambda e, o=o, kc=kc, t=t: e.scalar_tensor_tensor(
                        ob[o][:], t.x[kc], ng[:, 2, kc:kc + 1], rstd[:], ALU.mult, ALU.mult),
                        reads=[t.xres[kc], scr["r_rstd"], r_ng], writes=[r_ob[o]])
                    kb.store(outT[kc][:, ti * 512:(ti + 1) * 512], ob[o][:], [r_ob[o]], f"ob{o}")
            kb.finish()
    return kb.nc


_CACHE = {}


def _get(name, fn):
    if name not in _CACHE:
        _CACHE[name] = fn()
    return _CACHE[name]


def _pk(v, nch):
    return np.ascontiguousarray(np.asarray(v, np.float32).reshape(nch, 128).T)


def _consts():
    c = np.zeros((128, 3, 128), np.float32)
    c[:, 0, :] = 1.0
    c[:, 1, :] = np.eye(128, dtype=np.float32)
    idx = np.arange(128)
    c[idx, 2, idx ^ 1] = 1.0
    return c


def _rope_tables(t0):
    t = np.arange(t0, t0 + TPC)
    row = (t // 64).astype(np.float64)
    col = (t % 64).astype(np.float64)
    inv = 10000.0 ** (-np.arange(16, dtype=np.float64) / 16)
    ang = np.concatenate([row[:, None] * inv, col[:, None] * inv], -1)
    cos = np.cos(ang.astype(np.float32).astype(np.float64))
    sin = np.sin(ang.astype(np.float32).astype(np.float64))
    ang32 = np.concatenate([row[:, None].astype(np.float32) * inv.astype(np.float32),
                            col[:, None].astype(np.float32) * inv.astype(np.float32)], -1)
    cos = np.cos(ang32)
    sin = np.sin(ang32)
    tab = np.zeros((128, 2, TPC), np.float32)
    for p in range(128):
        d = p % 64
        i = d // 2
        tab[p, 0] = cos[:, i]
        tab[p, 1] = (-sin[:, i]) if d % 2 == 0 else sin[:, i]
    return tab


def _bias_table(rpb):
    rpb = np.asarray(rpb, np.float32)
    qc = np.arange(64)
    kc = np.arange(64)
    col_start = np.clip(qc - 8, 0, 48)
    cmask = (kc[None, :] >= col_start[:, None]) & (kc[None, :] < col_start[:, None] + 16)
    dc = np.clip(kc[None, :] - qc[:, None] + 15, 0, 30)
    TT = np.zeros((128, 8, 22, 64), np.float32)
    for half in range(2):
        for jj in range(22):
            dr = 10 - jj + half
            if -7 <= dr <= 7:
                val = rpb[:, dr + 7, :][:, dc]
            else:
                val = np.zeros((8, 64, 64), np.float32)
            val = np.where(cmask[None], val, NEG)
            TT[half * 64:(half + 1) * 64, :, jj, :] = np.transpose(val, (2, 0, 1))
    return TT.reshape(128, 8, 22 * 64).astype(NPBF)


def _row_valid(r0):
    Rv = np.zeros((4, 2, 8, 8, 64), np.float32)
    for qt in range(4):
        R0 = r0 + 8 * qt
        for j in range(8):
            for a in range(2):
                kr = R0 - 4 + 2 * j + a
                for i in range(8):
                    qr = R0 + i
                    st = min(max(qr - 4, 0), 120)
                    ok = (st <= kr < st + 8)
                    Rv[qt, a, j, i, :] = 0.0 if ok else NEG
    return Rv.reshape(4, 2, 8 * 512).astype(NPBF)


def _run(nc, in_maps):
    res = run_bass_kernel_spmd(nc, in_maps, core_ids=list(range(8)))
    return res.results


def kernel(x, c, ctx, c_ctx, mod_w, mod_b, norm_g, ffn_w_gate, ffn_w_up, ffn_w_down,
           attn_w_in, attn_w_out, na_rpb, diff_lambda, diff_subln_g,
           conv_w_in, conv_w_out, conv_w, final_g):
    f32 = lambda a: np.ascontiguousarray(np.asarray(a, np.float32))
    x, c, ctx, c_ctx = f32(x), f32(c), f32(ctx), f32(c_ctx)
    mod_w, mod_b, norm_g = f32(mod_w), f32(mod_b), f32(norm_g)
    cst = _consts()
    NCORE = 8

    ncA = build_A()
    mapsA = []
    wA = dict(modw=f32(mod_w[0][:, :5 * D]), modb=_pk(mod_b[0][:5 * D], 40),
              normg=f32(np.stack([_pk(norm_g[0, 0], 8), _pk(norm_g[0, 1], 8)], 1)),
              wg=f32(ffn_w_gate[0, 0]), wu=f32(ffn_w_up[0, 0]), wd=f32(ffn_w_down[0, 0]),
              w_in=f32(attn_w_in[0]), cst=cst)
    for core in range(NCORE):
        b, t0 = core // 4, (core % 4) * TPC
        xs = x[b, t0:t0 + TPC]
        xT = f32(xs.T.reshape(NKC, 128, TPC))
        cxT = f32(ctx[b].T.reshape(NKC, 128, 256))
        cv = np.stack([c[b].reshape(8, 128).T, c_ctx.reshape(8, 128).T], -1)
        mapsA.append(dict(xT=xT, cxT=cxT, cvec=f32(cv), rope=_rope_tables(t0), **wA))
    resA = _run(ncA, mapsA)

    ncB = build_B()
    mapsB = []
    E2 = np.zeros((2, 128), np.float32)
    E2[0, :64] = 1.0
    E2[1, 64:] = 1.0
    onesP = np.zeros((128, 2, 128), np.float32)
    onesP[:, 0, :64] = 1.0
    onesP[:, 1, 64:] = 1.0
    TTh = _bias_table(na_rpb[0])
    lamv = f32(np.broadcast_to(np.asarray(diff_lambda[0], np.float32)[None], (128, 4, 64)))
    modwB = np.zeros((D, 4 * D), np.float32)
    modwB[:, :3 * D] = mod_w[1][:, :3 * D]
    modbB = np.zeros((128, 32), np.float32)
    modbB[:, :24] = _pk(mod_b[1][:3 * D], 24)
    wB = dict(TT=TTh, E2=E2.astype(NPBF), onesP=onesP.astype(NPBF), cst=cst, lamv=lamv,
              sublng=f32(np.asarray(diff_subln_g[0], np.float32).reshape(128, 1)),
              modwA=f32(mod_w[0][:, 5 * D:]), modbA=_pk(mod_b[0][5 * D:], 32),
              modwB=modwB, modbB=modbB,
              normg=f32(np.stack([_pk(norm_g[0, 2], 8), _pk(norm_g[1, 0], 8)], 1)),
              w_out=f32(attn_w_out[0]),
              wg1=f32(ffn_w_gate[0, 1]), wu1=f32(ffn_w_up[0, 1]), wd1=f32(ffn_w_down[0, 1]),
              wg2=f32(ffn_w_gate[1, 0]), wu2=f32(ffn_w_up[1, 0]), wd2=f32(ffn_w_down[1, 0]))
    per_batch = {}
    for b in range(2):
        cores = [4 * b + i for i in range(4)]
        qk = [np.asarray(resA[cc]["qkT"]) for cc in cores]
        vt = [np.asarray(resA[cc]["vtm"]).reshape(TPC, 1024) for cc in cores]
        kc_ = np.asarray(resA[cores[0]]["kcT"])
        vc_ = np.asarray(resA[cores[0]]["vc"]).reshape(256, 1024)
        ka_full = np.concatenate([q[4:8] for q in qk], axis=2)
        z = np.zeros((4, 128, 256), ka_full.dtype)
        ka_pad = np.concatenate([z, ka_full, z], axis=2)
        va_full = np.concatenate([v[:, 0:512] for v in vt], axis=0)
        zv = np.zeros((256, 512), va_full.dtype)
        va_pad = np.concatenate([zv, va_full, zv], axis=0)
        kb_full = np.concatenate([q[12:16] for q in qk] , axis=2)
        kbT = np.ascontiguousarray(np.concatenate([kb_full, kc_[4:8]], axis=2))
        vb_full = np.concatenate([v[:, 512:1024] for v in vt] + [vc_[:, 512:1024]], axis=0)
        vbT = np.ascontiguousarray(vb_full.reshape(66, 128, 4, 128).transpose(2, 1, 0, 3).reshape(4, 128, 66 * 128))
        per_batch[b] = (ka_pad, va_pad, kc_, vc_, kbT, vbT)
    for core in range(NCORE):
        b, t0 = core // 4, (core % 4) * TPC
        ka_pad, va_pad, kc_, vc_, kbT, vbT = per_batch[b]
        qk = np.asarray(resA[core]["qkT"])
        kaB = np.ascontiguousarray(np.concatenate([ka_pad[:, :, t0:t0 + 2560], kc_[0:4]], axis=2))
        vband = np.concatenate([va_pad[t0:t0 + 2560], vc_[:, 0:512]], axis=0).reshape(2816, 8, 64)
        vP = np.zeros((2816, 8, 128), vband.dtype)
        vP[:, 0::2, 0:64] = vband[:, 0::2]
        vP[:, 1::2, 64:128] = vband[:, 1::2]
        vaP = np.ascontiguousarray(vP.reshape(22, 128, 4, 256).transpose(2, 1, 0, 3).reshape(4, 128, 22 * 256))
        cv = c[b].reshape(8, 128).T[:, :, None]
        mapsB.append(dict(qaT=np.ascontiguousarray(qk[0:4]), qbT=np.ascontiguousarray(qk[8:12]),
                          kaB=kaB, vaP=vaP, kbT=kbT, vbT=vbT, Rv=_row_valid((core % 4) * 32),
                          x1T=np.asarray(resA[core]["x1T"]), cvec=f32(cv), **wB))
    resB = _run(ncB, mapsB)

    ncC = build_C()
    mapsC = []
    wC = dict(modw=f32(mod_w[1][:, 3 * D:]), modb=_pk(mod_b[1][3 * D:], 48),
              normg=f32(np.stack([_pk(norm_g[1, 1], 8), _pk(norm_g[1, 2], 8), _pk(final_g, 8)], 1)),
              cw=f32(np.stack([_pk(np.asarray(conv_w[0], np.float32)[k], 8) for k in range(3)], 1)),
              cw_in=f32(conv_w_in[0]), cw_out=f32(conv_w_out[0]),
              wg=f32(ffn_w_gate[1, 1]), wu=f32(ffn_w_up[1, 1]), wd=f32(ffn_w_down[1, 1]), cst=cst)
    x4 = [np.asarray(resB[cc]["x4T"]) for cc in range(NCORE)]
    for core in range(NCORE):
        b = core // 4
        xh = np.zeros((128, NKC, 2), np.float32)
        hmask = np.zeros((128, 2), np.float32)
        if core % 4 != 0:
            xh[:, :, 0] = x4[core - 1][:, :, TPC - 1].T
            hmask[:, 0] = 1.0
        if core % 4 != 3:
            xh[:, :, 1] = x4[core + 1][:, :, 0].T
            hmask[:, 1] = 1.0
        cv = c[b].reshape(8, 128).T[:, :, None]
        mapsC.append(dict(x4T=x4[core], xhT=xh, hmask=hmask, cvec=f32(cv), **wC))
    resC = _run(ncC, mapsC)

    out = np.zeros((2, 8192, D), np.float32)
    for core in range(NCORE):
        b, t0 = core // 4, (core % 4) * TPC
        oT = np.asarray(resC[core]["outT"]).reshape(D, TPC)
        out[b, t0:t0 + TPC] = oT.T
    return out
```
